# Optimizing a Trainium2 kernel written in Bass

```python
import jax, jax.numpy as jnp
from jax import lax
import numpy as np

D_MODEL = 1024
BATCH = 16
SEQ = 2048
DEPTH = 2
DEC_BATCH = 32
DEC_SEQ = 8
PAST_LEN = 16384
PAGE_SIZE = 128

GLA_HEADS = 4
GLA_DK = D_MODEL // 2 // GLA_HEADS
GLA_DV = D_MODEL // GLA_HEADS
GLA_QK = GLA_HEADS * GLA_DK
GLA_VW = GLA_HEADS * GLA_DV
GLA_GATE_RANK = 16
GLA_GATE_NORM = 16.0
GLA_CHUNK = 64
POOL_WIDTH = D_MODEL
POOL_WINDOWS = (2, 4, 8, 16)
POOL_GROUP = POOL_WIDTH // 4
POOL_STATE = 15
ATT_GROUPS = ((128, 1), (512, 4), (2048, 16))
ATT_HEADS_PER_GROUP = 4
ATT_HEADS = 12
ATT_HEAD_DIM = 64
ATT_WIDTH = ATT_HEADS * ATT_HEAD_DIM
D_FF = ((8 * D_MODEL // 3 + 127) // 128) * 128
CONV_WIDTH = 3
NORM_EPS = 1e-6
NEG_INF = -1e30
IN_SPLITS = (GLA_QK, GLA_QK, GLA_VW, GLA_VW, GLA_GATE_RANK, POOL_WIDTH, ATT_WIDTH, ATT_WIDTH, ATT_WIDTH, D_MODEL, D_MODEL, D_MODEL)
IN_WIDTH = sum(IN_SPLITS)

kernel_name = 'hybrid_gla_pool_dilated_attn_step'


def _split_points():
    return [int(v) for v in np.cumsum(IN_SPLITS)[:-1]]


def alibi_slopes():
    return (2.0 ** (-8.0 * np.arange(1, ATT_HEADS + 1) / ATT_HEADS)).astype(np.float32)


def rmsnorm(x, g):
    xf = x.astype(jnp.float32)
    y = xf * lax.rsqrt(jnp.mean(xf * xf, axis=-1, keepdims=True) + NORM_EPS)
    return (y * g.astype(jnp.float32)).astype(x.dtype)


def gla_chunked(q, k, v, log_a, s0):
    f32 = jnp.float32
    bsz, t_len = q.shape[:2]
    c = GLA_CHUNK if t_len % GLA_CHUNK == 0 else t_len
    nc = t_len // c
    def chunks(z):
        return z.astype(f32).reshape(bsz, nc, c, *z.shape[2:])
    q, k, v, log_a = chunks(q), chunks(k), chunks(v), chunks(log_a)
    b = jnp.cumsum(log_a, axis=2)
    r = (c - 1) // 2
    b_ref = b[:, :, r:r + 1]
    b_last = b[:, :, -1:]
    causal = jnp.tril(jnp.ones((c, c), dtype=bool))
    att = jnp.einsum('bnthk,bnshk->bnhts', q * jnp.exp(b - b_ref), k * jnp.exp(b_ref - b))
    att = jnp.where(causal, att, 0.0)
    o_intra = jnp.einsum('bnhts,bnshv->bnthv', att, v)
    q_in = q * jnp.exp(b)
    k_st = k * jnp.exp(b_last - b)
    decay = jnp.exp(b_last[:, :, 0])
    def step(s, xs):
        qn, kn, vn, dn = xs
        on = jnp.einsum('bthk,bhkv->bthv', qn, s)
        s = dn[..., None] * s + jnp.einsum('bthk,bthv->bhkv', kn, vn)
        return s, on
    xs = (jnp.moveaxis(q_in, 1, 0), jnp.moveaxis(k_st, 1, 0), jnp.moveaxis(v, 1, 0), jnp.moveaxis(decay, 1, 0))
    s_fin, o_inter = lax.scan(step, s0.astype(f32), xs)
    o = o_intra + jnp.moveaxis(o_inter, 0, 1)
    return o.reshape(bsz, t_len, GLA_HEADS, GLA_DV), s_fin


def pool_mixer(u_ext, n_prev, pool_w, pool_scale):
    f32 = jnp.float32
    bsz, n, _ = u_ext.shape
    uf = u_ext.astype(f32)
    cs = jnp.concatenate([jnp.zeros((bsz, 1, POOL_WIDTH), f32), jnp.cumsum(uf, axis=1)], axis=1)
    rows = np.arange(n_prev, n)
    groups = []
    for gi, w in enumerate(POOL_WINDOWS):
        lo = np.maximum(rows + 1 - w, 0)
        cnt = (rows + 1 - lo).astype(np.float32)[None, :, None]
        sl = slice(gi * POOL_GROUP, (gi + 1) * POOL_GROUP)
        cg = cs[:, :, sl]
        mean = (cg[:, rows + 1] - cg[:, lo]) / cnt
        groups.append(mean - uf[:, n_prev:, sl])
    pooled = jnp.stack(groups, axis=2)
    mixed = jnp.einsum('btgc,gcd->btgd', pooled, pool_w.astype(f32))
    out = mixed.reshape(bsz, n - n_prev, POOL_WIDTH) * pool_scale.astype(f32)
    return out.astype(u_ext.dtype)


def _softmax_parts(s, axis):
    m = jnp.max(s, axis=axis, keepdims=True)
    p = jnp.exp(s - m)
    den = jnp.sum(p, axis=axis, keepdims=True)
    return p, den, m + jnp.log(den)


def dilated_attention_prompt(q, k, v, window, dil, slopes):
    f32 = jnp.float32
    bsz, s_len, hg, e = q.shape
    span = window // dil
    n = s_len // dil
    nb = -(-n // span)
    pad = nb * span - n
    def strided(z):
        z = z.astype(f32).reshape(bsz, n, dil, hg, e).transpose(0, 2, 1, 3, 4)
        return jnp.pad(z, ((0, 0), (0, 0), (0, pad), (0, 0), (0, 0)))
    qs, ks, vs = strided(q), strided(k), strided(v)
    qb = qs.reshape(bsz, dil, nb, span, hg, e)
    def band(z):
        zp = jnp.pad(z, ((0, 0), (0, 0), (span, 0), (0, 0), (0, 0)))
        prev = zp[:, :, :nb * span].reshape(bsz, dil, nb, span, hg, e)
        cur = z.reshape(bsz, dil, nb, span, hg, e)
        return jnp.concatenate([prev, cur], axis=3)
    kb, vb = band(ks), band(vs)
    s = jnp.einsum('bdnqhe,bdnkhe->bdnhqk', qb, kb) * (e ** -0.5)
    a_idx = np.arange(span)[:, None]
    c_idx = np.arange(2 * span)[None, :]
    j = a_idx - c_idx + span
    blk = np.arange(nb)[:, None, None]
    valid = (j >= 0) & (j <= span) & ((blk > 0) | (c_idx >= span))[...]
    bias = (-slopes[:, None, None] * (j * dil).astype(np.float32)[None]).astype(np.float32)
    s = jnp.where(valid[None, None, :, None], s + bias, NEG_INF)
    p, den, lse = _softmax_parts(s, -1)
    o = jnp.einsum('bdnhqk,bdnkhe->bdnqhe', p, vb) / jnp.swapaxes(den[..., 0], -1, -2)[..., None]
    lse = jnp.swapaxes(lse[..., 0], -1, -2)
    o = o.reshape(bsz, dil, nb * span, hg, e)[:, :, :n].transpose(0, 2, 1, 3, 4).reshape(bsz, s_len, hg, e)
    lse = lse.reshape(bsz, dil, nb * span, hg)[:, :, :n].transpose(0, 2, 1, 3).reshape(bsz, s_len, hg)
    return o, lse


def dilated_attention_sample(q, k_ext, v_ext, window, dil, slopes):
    f32 = jnp.float32
    bsz, l_len, hg, e = q.shape
    wb = k_ext.shape[1] - l_len
    span = window // dil
    jj = np.arange(span + 1)
    idx = wb + np.arange(l_len)[:, None] - jj[None, :] * dil
    valid = idx >= 0
    idxc = np.maximum(idx, 0)
    kg = k_ext.astype(f32)[:, idxc]
    vg = v_ext.astype(f32)[:, idxc]
    s = jnp.einsum('blhe,bljhe->blhj', q.astype(f32), kg) * (e ** -0.5)
    bias = (-slopes[:, None] * (jj * dil).astype(np.float32)[None, :]).astype(np.float32)
    s = jnp.where(valid[None, :, None, :], s + bias, NEG_INF)
    p, den, lse = _softmax_parts(s, -1)
    o = jnp.einsum('blhj,bljhe->blhe', p, vg) / den
    return o, lse[..., 0]


def token_mixer(xn, gla_prev, pool_prev, kv_prev, w_in, gla_wa2, gla_ba, gla_norm_g,
                pool_w, pool_scale, w_oa, w_ob, w_oc, w_out):
    f32 = jnp.float32
    dt = xn.dtype
    bsz, t_len, _ = xn.shape
    prompt = gla_prev is None
    (q_g, k_g, v_g, r_g, a_lr, u_pool, q_a, k_a, v_a,
     g_a, g_b, g_c) = jnp.split(xn @ w_in, _split_points(), axis=-1)
    q_g = q_g.reshape(bsz, t_len, GLA_HEADS, GLA_DK) * (GLA_DK ** -0.5)
    k_g = k_g.reshape(bsz, t_len, GLA_HEADS, GLA_DK)
    v_g = v_g.reshape(bsz, t_len, GLA_HEADS, GLA_DV)
    log_a = jax.nn.log_sigmoid((a_lr @ gla_wa2 + gla_ba).astype(f32)) / GLA_GATE_NORM
    log_a = log_a.reshape(bsz, t_len, GLA_HEADS, GLA_DK)
    s0 = jnp.zeros((bsz, GLA_HEADS, GLA_DK, GLA_DV), f32) if prompt else gla_prev
    o_g, gla_new = gla_chunked(q_g, k_g, v_g, log_a, s0)
    o_g = rmsnorm(o_g, gla_norm_g) * jax.nn.silu(r_g.reshape(bsz, t_len, GLA_HEADS, GLA_DV).astype(f32))
    y_a = o_g.reshape(bsz, t_len, GLA_VW).astype(dt) @ w_oa
    u_ext = u_pool if prompt else jnp.concatenate([pool_prev.astype(dt), u_pool], axis=1)
    n_prev = u_ext.shape[1] - t_len
    y_b = pool_mixer(u_ext, n_prev, pool_w, pool_scale) @ w_ob
    pool_new = u_ext[:, -POOL_STATE:]
    q_a = q_a.reshape(bsz, t_len, ATT_HEADS, ATT_HEAD_DIM)
    k_a = k_a.reshape(bsz, t_len, ATT_HEADS, ATT_HEAD_DIM)
    v_a = v_a.reshape(bsz, t_len, ATT_HEADS, ATT_HEAD_DIM)
    slopes = alibi_slopes()
    outs, lses, kv_new = [], [], []
    for gi, (win, dil) in enumerate(ATT_GROUPS):
        hs = slice(gi * ATT_HEADS_PER_GROUP, (gi + 1) * ATT_HEADS_PER_GROUP)
        qh, kh, vh = q_a[:, :, hs], k_a[:, :, hs], v_a[:, :, hs]
        if prompt:
            o_h, l_h = dilated_attention_prompt(qh, kh, vh, win, dil, slopes[hs])
            keep = min(win, t_len)
            kv_new += [kh[:, -keep:], vh[:, -keep:]]
        else:
            k_ext = jnp.concatenate([kv_prev[2 * gi].astype(dt), kh], axis=1)
            v_ext = jnp.concatenate([kv_prev[2 * gi + 1].astype(dt), vh], axis=1)
            o_h, l_h = dilated_attention_sample(qh, k_ext, v_ext, win, dil, slopes[hs])
            keep = kv_prev[2 * gi].shape[1]
            kv_new += [k_ext[:, -keep:], v_ext[:, -keep:]]
        outs.append(o_h)
        lses.append(l_h)
    alpha = jax.nn.softmax(jnp.stack(lses, axis=2), axis=2)
    o_c = jnp.concatenate([outs[gi] * alpha[:, :, gi, :, None] for gi in range(len(ATT_GROUPS))], axis=2)
    y_c = o_c.reshape(bsz, t_len, ATT_WIDTH).astype(dt) @ w_oc
    merged = jax.nn.sigmoid(g_a) * y_a + jax.nn.sigmoid(g_b) * y_b + jax.nn.sigmoid(g_c) * y_c
    return merged @ w_out, gla_new.astype(dt), pool_new, kv_new


def conv_ffn(xn, conv_prev, w_up, conv_w, conv_b, w_down):
    t_len = xn.shape[1]
    a, b = jnp.split(xn @ w_up, 2, axis=-1)
    ext = jnp.concatenate([conv_prev.astype(a.dtype), a], axis=1)
    y = conv_b + conv_w[0] * ext[:, 0:t_len] + conv_w[1] * ext[:, 1:t_len + 1] + conv_w[2] * ext[:, 2:t_len + 2]
    h = jax.nn.silu(y) * b
    return h @ w_down, ext[:, -(CONV_WIDTH - 1):]


def run_group(x, gla_st, pool_st, kv_st, conv_st, norm1_g, norm2_g, w_in, gla_wa2, gla_ba, gla_norm_g,
              pool_w, pool_scale, w_oa, w_ob, w_oc, w_out, ffn_w_up, ffn_conv_w, ffn_conv_b, ffn_w_down,
              final_norm_g):
    prompt = gla_st is None
    bsz = x.shape[0]
    new_gla, new_pool, new_kv, new_conv = [], [], [], []
    for l in range(DEPTH):
        kv_l = None if prompt else [c[l] for c in kv_st]
        mix, g_new, p_new, kv_new = token_mixer(
            rmsnorm(x, norm1_g[l]), None if prompt else gla_st[l], None if prompt else pool_st[l], kv_l,
            w_in[l], gla_wa2[l], gla_ba[l], gla_norm_g[l], pool_w[l], pool_scale[l],
            w_oa[l], w_ob[l], w_oc[l], w_out[l])
        h = x + mix
        conv_prev = jnp.zeros((bsz, CONV_WIDTH - 1, D_FF), x.dtype) if prompt else conv_st[l]
        f, c_new = conv_ffn(rmsnorm(h, norm2_g[l]), conv_prev, ffn_w_up[l], ffn_conv_w[l], ffn_conv_b[l], ffn_w_down[l])
        x = h + f
        new_gla.append(g_new)
        new_pool.append(p_new)
        new_kv.append(kv_new)
        new_conv.append(c_new)
    kv_out = [jnp.stack([kvl[i] for kvl in new_kv], axis=0) for i in range(2 * len(ATT_GROUPS))]
    return (rmsnorm(x, final_norm_g), jnp.stack(new_gla, 0), jnp.stack(new_pool, 0), kv_out,
            jnp.stack(new_conv, 0))


def setup_inputs(seed: int = 0) -> dict:
    key = jax.random.key(seed)
    ks = jax.random.split(key, 40)
    f32 = jnp.float32
    def rnd(i, shape, scale):
        return jax.random.normal(ks[i], shape, f32) * scale
    hg = ATT_HEADS_PER_GROUP
    wb = [min(w, PAST_LEN) for (w, _) in ATT_GROUPS]
    return {
        'x_prompt': rnd(0, (BATCH, SEQ, D_MODEL), 1.0),
        'x_sample': rnd(1, (DEC_BATCH, DEC_SEQ, D_MODEL), 1.0),
        'state_gla': rnd(2, (DEPTH, DEC_BATCH, GLA_HEADS, GLA_DK, GLA_DV), 1.0),
        'state_pool': rnd(3, (DEPTH, DEC_BATCH, POOL_STATE, POOL_WIDTH), 1.0),
        'cache_k_w128': rnd(4, (DEPTH, DEC_BATCH, wb[0], hg, ATT_HEAD_DIM), 1.0),
        'cache_v_w128': rnd(5, (DEPTH, DEC_BATCH, wb[0], hg, ATT_HEAD_DIM), 1.0),
        'cache_k_w512': rnd(6, (DEPTH, DEC_BATCH, wb[1], hg, ATT_HEAD_DIM), 1.0),
        'cache_v_w512': rnd(7, (DEPTH, DEC_BATCH, wb[1], hg, ATT_HEAD_DIM), 1.0),
        'cache_k_w2048': rnd(8, (DEPTH, DEC_BATCH, wb[2], hg, ATT_HEAD_DIM), 1.0),
        'cache_v_w2048': rnd(9, (DEPTH, DEC_BATCH, wb[2], hg, ATT_HEAD_DIM), 1.0),
        'state_ffn_conv': rnd(10, (DEPTH, DEC_BATCH, CONV_WIDTH - 1, D_FF), 1.0),
        'norm1_g': 1.0 + rnd(11, (DEPTH, D_MODEL), 0.05),
        'norm2_g': 1.0 + rnd(12, (DEPTH, D_MODEL), 0.05),
        'w_in': rnd(13, (DEPTH, D_MODEL, IN_WIDTH), D_MODEL ** -0.5),
        'gla_wa2': rnd(14, (DEPTH, GLA_GATE_RANK, GLA_QK), GLA_GATE_RANK ** -0.5),
        'gla_ba': rnd(15, (DEPTH, GLA_QK), 0.01),
        'gla_norm_g': 1.0 + rnd(16, (DEPTH, GLA_DV), 0.05),
        'pool_w': rnd(17, (DEPTH, 4, POOL_GROUP, POOL_GROUP), POOL_GROUP ** -0.5),
        'pool_scale': 1.0 + rnd(18, (DEPTH, POOL_WIDTH), 0.1),
        'w_oa': rnd(19, (DEPTH, GLA_VW, D_MODEL), GLA_VW ** -0.5),
        'w_ob': rnd(20, (DEPTH, POOL_WIDTH, D_MODEL), POOL_WIDTH ** -0.5),
        'w_oc': rnd(21, (DEPTH, ATT_WIDTH, D_MODEL), ATT_WIDTH ** -0.5),
        'w_out': rnd(22, (DEPTH, D_MODEL, D_MODEL), D_MODEL ** -0.5),
        'ffn_w_up': rnd(23, (DEPTH, D_MODEL, 2 * D_FF), D_MODEL ** -0.5),
        'ffn_conv_w': rnd(24, (DEPTH, CONV_WIDTH, D_FF), CONV_WIDTH ** -0.5),
        'ffn_conv_b': rnd(25, (DEPTH, D_FF), 0.01),
        'ffn_w_down': rnd(26, (DEPTH, D_FF, D_MODEL), D_FF ** -0.5),
        'final_norm_g': 1.0 + rnd(27, (D_MODEL,), 0.05),
    }


def reference(x_prompt, x_sample, state_gla, state_pool, cache_k_w128, cache_v_w128, cache_k_w512, cache_v_w512,
              cache_k_w2048, cache_v_w2048, state_ffn_conv, norm1_g, norm2_g, w_in, gla_wa2, gla_ba, gla_norm_g,
              pool_w, pool_scale, w_oa, w_ob, w_oc, w_out, ffn_w_up, ffn_conv_w, ffn_conv_b, ffn_w_down,
              final_norm_g):
    y_prompt, gla_p, pool_p, kv_p, conv_p = run_group(
        x_prompt, None, None, None, None, norm1_g, norm2_g, w_in, gla_wa2, gla_ba, gla_norm_g,
        pool_w, pool_scale, w_oa, w_ob, w_oc, w_out, ffn_w_up, ffn_conv_w, ffn_conv_b, ffn_w_down, final_norm_g)
    kv_in = [cache_k_w128, cache_v_w128, cache_k_w512, cache_v_w512, cache_k_w2048, cache_v_w2048]
    y_sample, gla_s, pool_s, kv_s, conv_s = run_group(
        x_sample, state_gla, state_pool, kv_in, state_ffn_conv, norm1_g, norm2_g, w_in, gla_wa2, gla_ba,
        gla_norm_g, pool_w, pool_scale, w_oa, w_ob, w_oc, w_out, ffn_w_up, ffn_conv_w, ffn_conv_b, ffn_w_down,
        final_norm_g)
    k128_p, v128_p, k512_p, v512_p, k2048_p, v2048_p = kv_p
    k128_s, v128_s, k512_s, v512_s, k2048_s, v2048_s = kv_s
    return (y_prompt, y_sample, gla_p, gla_s, pool_p, pool_s, k128_p, k128_s, v128_p, v128_s,
            k512_p, k512_s, v512_p, v512_s, k2048_p, k2048_s, v2048_p, v2048_s, conv_p, conv_s)
```

```python
import numpy as np
from contextlib import ExitStack
import concourse.bass as bass
import concourse.mybir as mybir
from concourse.bass_utils import run_bass_kernel_spmd

F32 = mybir.dt.float32
BF16 = mybir.dt.bfloat16
AF = mybir.ActivationFunctionType
ALU = mybir.AluOpType

SAME_ENGINE_SYNC = True
PREFETCH_WO = True
SEM_CHUNK = 30000
N_DMA_SLOTS = {'sp': 8, 'pool': 6}


class Buf:
    __slots__ = ('w', 'r', 'name')

    def __init__(self, name=''):
        self.w = None
        self.r = {}
        self.name = name


class Prog:
    ENGS = ('pe', 'act', 'dve', 'pool', 'sp')

    def __init__(self, nc):
        self.nc = nc
        self.ops = {e: [] for e in self.ENGS}
        self.waited_c = {e: {} for e in self.ENGS}
        self.waited_d = {e: {} for e in self.ENGS}
        self.dma_n = {}
        self.dma_rr = {q: 0 for q in N_DMA_SLOTS}
        self.pending = {e: [] for e in self.ENGS}

    def barrier(self):
        evs = []
        for f in self.ENGS:
            if f in ('pe', 'act', 'dve', 'pool') or True:
                for i in range(len(self.ops[f]) - 1, -1, -1):
                    if self.ops[f][i]['dma'] is None:
                        evs.append(('c', f, i))
                        break
        for (q, k), n in self.dma_n.items():
            evs.append(('d', q, k, n))
        for e in self.ENGS:
            self.pending[e].extend(self._add_waits(e, evs, force_same=True))

    def _collect(self, reads, writes):
        deps = []
        for b in reads:
            if b.w is not None:
                deps.append(b.w)
        for b in writes:
            if b.w is not None:
                deps.append(b.w)
            deps.extend(b.r.values())
        return deps

    def _add_waits(self, eng, deps, is_pe_mm=False, force_same=False):
        waits = []
        for ev in deps:
            if ev[0] == 'c':
                _, src, idx = ev
                if src == eng and (force_same or not SAME_ENGINE_SYNC or eng == 'pe'):
                    continue
                if self.waited_c[eng].get(src, -1) >= idx:
                    continue
                self.waited_c[eng][src] = idx
                self.ops[src][idx]['sig'] = True
                waits.append(ev)
            else:
                _, q, k, n = ev
                if self.waited_d[eng].get((q, k), 0) >= n:
                    continue
                self.waited_d[eng][(q, k)] = n
                waits.append(ev)
        return waits

    def _publish(self, ev, key, reads, writes):
        for b in reads:
            b.r[key] = ev
        for b in writes:
            b.w = ev
            b.r = {}

    def op(self, eng, fn, reads=(), writes=(), pe_acc=False):
        deps = self._collect(reads, writes)
        waits = self.pending[eng] + self._add_waits(eng, deps, is_pe_mm=pe_acc)
        self.pending[eng] = []
        idx = len(self.ops[eng])
        self.ops[eng].append({'fn': fn, 'waits': waits, 'sig': False, 'dma': None})
        self._publish(('c', eng, idx), eng, reads, writes)

    def dma(self, q, out, in_, reads=(), writes=(), **kw):
        nslots = N_DMA_SLOTS[q]
        k = self.dma_rr[q]
        self.dma_rr[q] = (k + 1) % nslots
        n = self.dma_n.get((q, k), 0)
        deps = self._collect(reads, writes)
        if n > 0:
            deps.append(('d', q, k, n))
        waits = self.pending[q] + self._add_waits(q, deps)
        self.pending[q] = []
        self.dma_n[(q, k)] = n + 1
        self.ops[q].append({'fn': (lambda e, out=out, in_=in_, kw=kw: e.dma_start(out=out, in_=in_, **kw)),
                            'waits': waits, 'sig': False, 'dma': (q, k)})
        self._publish(('d', q, k, n + 1), ('d', q, k), reads, writes)

    def emit(self, es):
        nc = self.nc
        sig_rank = {}
        eng_sems = {}
        for e in self.ENGS:
            rank = 0
            for i, o in enumerate(self.ops[e]):
                if o['sig']:
                    sig_rank[(e, i)] = rank
                    rank += 1
            nsem = (rank + SEM_CHUNK - 1) // SEM_CHUNK
            eng_sems[e] = [es.enter_context(nc.semaphore('s_%s_%d' % (e, j))) for j in range(nsem)]
        dma_sems = {}
        for (q, k) in self.dma_n:
            dma_sems[(q, k)] = es.enter_context(nc.semaphore('d_%s_%d' % (q, k)))

        def resolve(ev):
            if ev[0] == 'c':
                r = sig_rank[(ev[1], ev[2])]
                return eng_sems[ev[1]][r // SEM_CHUNK], (r % SEM_CHUNK) + 1
            return dma_sems[(ev[1], ev[2])], 16 * ev[3]

        final_waits = [('d', q, k, n) for (q, k), n in self.dma_n.items()]

        def run(ename, e):
            for i, o in enumerate(self.ops[ename]):
                for ev in o['waits']:
                    s, v = resolve(ev)
                    e.wait_ge(s, v)
                ins = o['fn'](e)
                if o['dma'] is not None:
                    ins.then_inc(dma_sems[o['dma']], 16)
                elif o['sig']:
                    r = sig_rank[(ename, i)]
                    ins.then_inc(eng_sems[ename][r // SEM_CHUNK], 1)
            if ename == 'sp':
                for ev in final_waits:
                    s, v = resolve(ev)
                    e.wait_ge(s, v)

        with nc.Block() as block:
            @block.tensor
            def _(e):
                run('pe', e)

            @block.scalar
            def _(e):
                run('act', e)

            @block.vector
            def _(e):
                run('dve', e)

            @block.gpsimd
            def _(e):
                run('pool', e)

            @block.sync
            def _(e):
                run('sp', e)


class T:
    __slots__ = ('ap', 'b')

    def __init__(self, ap, name=''):
        self.ap = ap
        self.b = Buf(name)


class Arena:
    def __init__(self, tensor, words):
        self.t = tensor
        self.words = words
        self.top = 0

    def alloc(self, free, dt, name='', parts=128):
        n = 1
        for f in free:
            n *= f
        w = n if dt == F32 else (n + 1) // 2
        w = (w + 3) // 4 * 4
        off = self.top
        self.top += w
        assert self.top <= self.words, 'arena overflow %s %d > %d' % (name, self.top, self.words)
        nw = n if dt == F32 else (n + 1) // 2
        ap = self.t[0:parts, off:off + nw]
        if dt == BF16:
            ap = ap.bitcast(BF16)
            if n % 2:
                ap = ap[:, 0:n]
        if len(free) == 2:
            ap = ap.rearrange('p (a b) -> p a b', b=free[1])
        elif len(free) == 3:
            ap = ap.rearrange('p (a b c) -> p a b c', b=free[1], c=free[2])
        return T(ap, name)


def _bs(ts):
    return [t.b if isinstance(t, T) else t for t in ts]


def mm(P, out, lhsT, rhs, start, stop, reads, writes):
    P.op('pe', lambda e: e.matmul(out, lhsT=lhsT, rhs=rhs, start=start, stop=stop),
         reads=_bs(reads), writes=_bs(writes))


def tr(P, out, in_, ident, reads, writes):
    P.op('pe', lambda e: e.transpose(out, in_, ident), reads=_bs(reads), writes=_bs(writes))


def act(P, out, in_, func, reads, writes, scale=1.0, bias=None):
    if bias is None:
        P.op('act', lambda e: e.activation(out=out, in_=in_, func=func, scale=scale),
             reads=_bs(reads), writes=_bs(writes))
    else:
        P.op('act', lambda e: e.activation(out=out, in_=in_, func=func, scale=scale, bias=bias),
             reads=_bs(reads), writes=_bs(writes))


def tt(P, eng, out, in0, in1, op, reads, writes):
    P.op(eng, lambda e: e.tensor_tensor(out=out, in0=in0, in1=in1, op=op), reads=_bs(reads), writes=_bs(writes))


def ts(P, eng, out, in0, s1, s2, op0, op1, reads, writes):
    P.op(eng, lambda e: e.tensor_scalar(out=out, in0=in0, scalar1=s1, scalar2=s2, op0=op0, op1=op1),
         reads=_bs(reads), writes=_bs(writes))


def stt(P, out, in0, scalar, in1, op0, op1, reads, writes):
    P.op('dve', lambda e: e.scalar_tensor_tensor(out=out, in0=in0, scalar=scalar, in1=in1, op0=op0, op1=op1),
         reads=_bs(reads), writes=_bs(writes))


def cp(P, eng, out, in_, reads, writes):
    if eng == 'act':
        P.op('act', lambda e: e.copy(out=out, in_=in_), reads=_bs(reads), writes=_bs(writes))
    else:
        P.op(eng, lambda e: e.tensor_copy(out=out, in_=in_), reads=_bs(reads), writes=_bs(writes))


def ms(P, eng, ap, val, writes):
    P.op(eng, lambda e: e.memset(ap, val), writes=_bs(writes))


def dma(P, q, out, in_, reads=(), writes=()):
    P.dma(q, out, in_, reads=_bs(reads), writes=_bs(writes))


DEPTH = 2
DM = 1024
NPS = 2
SEQ = 2048
NSS = 4
DSEQ = 8
NTP = NPS * SEQ
NT = NTP + NSS * DSEQ
DFF = 2816
NFF = 22
INW = 9488
C_QG, C_KG, C_VG, C_RG, C_ALR, C_UP, C_QA, C_KA, C_VA, C_GA, C_GB, C_GC = (
    0, 512, 1024, 2048, 3072, 3088, 4112, 4880, 5648, 6416, 7440, 8464)
F_QG, F_KG, F_RG, F_ALR, F_QA, F_KA, F_GA, F_GB, F_GC, NFB = 0, 4, 8, 16, 17, 23, 29, 37, 45, 53
GROUPS = ((128, 1), (512, 4), (2048, 16))
SUPER = [(i * 512, 512) for i in range(8)] + [(NTP, NSS * DSEQ)]
V_N1, V_N2, V_FN, V_BA, V_GN, V_PS, V_CW, V_CB, NV = 0, 16, 32, 40, 48, 52, 68, 200, 244
ARENA_WORDS = 52736


def alibi_slopes():
    return (2.0 ** (-8.0 * np.arange(1, 13) / 12)).astype(np.float32)


def host_constants():
    sl = alibi_slopes()
    c = {}
    kl = np.arange(128)[:, None]
    col = np.arange(256)[None, :]
    j = col - kl
    bm = np.zeros((128, 12, 256), np.float32)
    for h in range(12):
        dil = GROUPS[h // 4][1]
        bm[:, h, :] = np.where((j >= 0) & (j <= 128), -sl[h] * (j * dil).astype(np.float32), -1e30)
    c['bm'] = bm
    for gi, (win, dil) in enumerate(GROUPS):
        nt_ = win // 128 + 1
        t = np.zeros((128, 4, nt_, 8), np.float32)
        p = np.arange(128)[:, None, None]
        ti = np.arange(nt_)[None, :, None]
        qi = np.arange(8)[None, None, :]
        cidx = np.where(ti < nt_ - 1, ti * 128 + p, win + p)
        okrow = np.where(ti < nt_ - 1, True, p < 8)
        dist = win + qi - cidx
        valid = okrow & (dist >= 0) & (dist % dil == 0) & (dist // dil <= win // dil)
        for hh in range(4):
            t[:, hh] = np.where(valid, -sl[gi * 4 + hh] * dist.astype(np.float32), -1e30)
        c['bms%d' % gi] = t
    mu = (np.arange(128)[:, None] <= np.arange(128)[None, :]).astype(np.float32)
    c['maskU'] = mu
    ic = np.zeros((128, 8, 16), np.float32)
    for cc in range(8):
        w = (2, 4, 8, 16)[cc // 2]
        ic[:, cc, :] = 1.0 / np.minimum(np.arange(16) + 1, w)
    c['invc'] = ic
    return c


class Ring:
    def __init__(self, tiles):
        self.tiles = tiles
        self.i = 0

    def next(self):
        t = self.tiles[self.i]
        self.i = (self.i + 1) % len(self.tiles)
        return t


def build_program(stop_after=None, debug_scratch=False, sub=None):
    nc = bass.Bass('TRN2', target_bir_lowering=False)
    D = {}

    def din(name, shape, dt=F32):
        D[name] = nc.dram_tensor(name, list(shape), dt, kind='ExternalInput').ap()

    def dout(name, shape, dt=F32):
        D[name] = nc.dram_tensor(name, list(shape), dt, kind='ExternalOutput').ap()

    def dscr(name, shape, dt):
        D[name] = nc.dram_tensor(name, list(shape), dt, kind='ExternalOutput' if debug_scratch else 'Internal').ap()

    din('xT', [8, 128, NT])
    din('state_gla', [DEPTH, NSS, 4, 128, 256])
    din('spoolT', [DEPTH, 8, 128, NSS, 15])
    din('sconvT', [DEPTH, NFF, 128, NSS, 2])
    for gi, (win, dil) in enumerate(GROUPS):
        din('ck%d' % gi, [DEPTH, NSS, win, 256])
        din('cv%d' % gi, [DEPTH, NSS, win, 256])
    din('w_in', [DEPTH, DM, INW])
    din('gla_wa2', [DEPTH, 16, 512])
    din('pool_w', [DEPTH, 4, 256, 256])
    din('w_oa', [DEPTH, 1024, DM])
    din('w_ob', [DEPTH, 1024, DM])
    din('w_oc', [DEPTH, 768, DM])
    din('w_out', [DEPTH, DM, DM])
    din('ffn_w_up', [DEPTH, DM, 2 * DFF])
    din('ffn_w_down', [DEPTH, DFF, DM])
    din('vecs', [128, NV])
    din('bm', [128, 12, 256])
    for gi, (win, dil) in enumerate(GROUPS):
        din('bms%d' % gi, [128, 4, win // 128 + 1, 8])
    din('maskU', [128, 128])
    din('invc', [128, 8, 16])
    dout('yT', [8, 128, NT])
    dout('gla_p', [DEPTH, NPS, 4, 128, 256])
    dout('gla_s', [DEPTH, NSS, 4, 128, 256])
    dout('poolT', [DEPTH, 8, 128, NPS + NSS, 15])
    dout('convT', [DEPTH, NFF, 128, NPS + NSS, 2])
    for gi, (win, dil) in enumerate(GROUPS):
        dout('kT%d' % gi, [DEPTH, NPS, 2, 128, win])
        dout('v%dp' % gi, [DEPTH, NPS, win, 256])
        dout('k%ds' % gi, [DEPTH, NSS, win, 256])
        dout('v%ds' % gi, [DEPTH, NSS, win, 256])
    dscr('hT', [8, 128, NT], F32)
    dscr('PF', [NFB, 128, NT], BF16)
    dscr('UP', [8, 128, NT], F32)
    dscr('VG', [NT, 1024], BF16)
    dscr('VA', [NT, 768], BF16)
    dscr('OG', [8, 128, NT], BF16)
    dscr('PM', [8, 128, NT], BF16)
    dscr('OC', [6, 128, NT], BF16)
    dscr('HF', [NFF, 128, NT], BF16)
    dscr('QK32', [8, 128, NT], F32)
    if debug_scratch:
        dscr('XN', [8, 128, NT], BF16)

    P = Prog(nc)
    es = ExitStack()
    arena_t = es.enter_context(nc.sbuf_tensor('arena', [128, ARENA_WORDS], F32))
    A = Arena(arena_t, ARENA_WORDS)
    pq = [es.enter_context(nc.psum_tensor('pq%d' % i, [128, 1024], F32)) for i in range(3)]
    psf = []
    for i in range(3):
        psf.append(T(pq[i][:, 0:512], 'ps%d' % (2 * i)))
        psf.append(T(pq[i][:, 512:1024], 'ps%d' % (2 * i + 1)))
    psf.append(T(es.enter_context(nc.psum_tensor('ps6', [128, 512], F32))[:], 'ps6'))
    psb = T(es.enter_context(nc.psum_tensor('psb', [128, 1024], BF16))[:], 'psb')
    PS = Ring(psf)

    vecs = A.alloc([NV], F32, 'vecs')
    ident16 = A.alloc([128], BF16, 'ident16')
    ones16 = A.alloc([128], BF16, 'ones16')
    onesZ = [A.alloc([128], BF16, 'onesZ0'), A.alloc([128], BF16, 'onesZ1')]
    nba = A.alloc([8], F32, 'nba')
    dma(P, 'sp', vecs.ap, D['vecs'], writes=[vecs])
    ms(P, 'pool', ident16.ap, 0.0, [ident16])
    P.op('pool', lambda e: e.affine_select(out=ident16.ap, in_=ident16.ap, pattern=[[-1, 128]],
                                           compare_op=ALU.not_equal, fill=1.0, base=0, channel_multiplier=1),
         reads=[ident16.b], writes=[ident16.b])
    ms(P, 'pool', ones16.ap, 1.0, [ones16])
    for j in range(2):
        ms(P, 'pool', onesZ[j].ap, 0.0, [onesZ[j]])
        ms(P, 'pool', onesZ[j].ap[:, j * 64:(j + 1) * 64], 1.0, [onesZ[j]])
    ts(P, 'dve', nba.ap, vecs.ap[:, V_BA:V_BA + 8], -1.0, None, ALU.mult, ALU.bypass, [vecs], [nba])
    MARK0 = A.top

    def vcol(c):
        return vecs.ap[:, c:c + 1]

    def phase_reset(mark):
        P.barrier()
        A.top = mark

    evac_rr = [0]

    def evac_eng():
        evac_rr[0] ^= 1
        return 'act' if evac_rr[0] else 'dve'

    def phase_norm(src, gcol, mode, need_lo=False):
        phase_reset(MARK0)
        xn = None
        xnb = None
        xlo = None
        if mode == 'xn':
            xn = A.alloc([8, NT], BF16, 'xn')
            xnb = [Buf('xn%d' % i) for i in range(len(SUPER))]
        mark_xn = A.top
        if need_lo:
            xlo = A.alloc([8, NT], BF16, 'xlo')
            xfr = Ring([A.alloc([512], F32, 'xf%d' % i) for i in range(4)])
        mark_lo = A.top
        hsr = Ring([A.alloc([8, 512], F32, 'hs%d' % i) for i in range(2)])
        sqr = Ring([A.alloc([8, 512], BF16, 'sq%d' % i) for i in range(2)])
        lnr = Ring([A.alloc([512], F32, 'ln%d' % i) for i in range(2)])
        rsr = Ring([A.alloc([512], F32, 'rs%d' % i) for i in range(2)])
        ysr = Ring([A.alloc([8, 512], F32, 'ys%d' % i) for i in range(2)]) if mode == 'final' else None
        srcv = src.rearrange('c p t -> p c t')
        for si, (t0, n) in enumerate(SUPER):
            hs = hsr.next()
            dma(P, 'sp', hs.ap[:, :, 0:n], srcv[:, :, t0:t0 + n], writes=[hs])
            sq = sqr.next()
            act(P, sq.ap[:, :, 0:n], hs.ap[:, :, 0:n], AF.Square, [hs], [sq])
            ps = PS.next()
            for c in range(8):
                mm(P, ps.ap[:, 0:n], ones16.ap, sq.ap[:, c, 0:n], c == 0, c == 7, [ones16, sq], [ps])
            ln = lnr.next()
            act(P, ln.ap[:, 0:n], ps.ap[:, 0:n], AF.Ln, [ps], [ln], scale=1.0 / DM, bias=1e-6)
            rs = rsr.next()
            act(P, rs.ap[:, 0:n], ln.ap[:, 0:n], AF.Exp, [ln], [rs], scale=-0.5)
            if mode == 'xn' and need_lo:
                for c in range(8):
                    xf = xfr.next()
                    stt(P, xf.ap[:, 0:n], hs.ap[:, c, 0:n], vcol(gcol + c), rs.ap[:, 0:n],
                        ALU.mult, ALU.mult, [hs, rs, vecs], [xf])
                    cp(P, 'act', xn.ap[:, c, t0:t0 + n], xf.ap[:, 0:n], [xf], [xnb[si]])
                    tt(P, 'pool' if c % 2 == 0 else 'dve', xlo.ap[:, c, t0:t0 + n], xf.ap[:, 0:n],
                       xn.ap[:, c, t0:t0 + n], ALU.subtract, [xf, xnb[si]], [xnb[si]])
            elif mode == 'xn':
                for c in range(8):
                    stt(P, xn.ap[:, c, t0:t0 + n], hs.ap[:, c, 0:n], vcol(gcol + c), rs.ap[:, 0:n],
                        ALU.mult, ALU.mult, [hs, rs, vecs], [xnb[si]])
            else:
                ys = ysr.next()
                for c in range(8):
                    stt(P, ys.ap[:, c, 0:n], hs.ap[:, c, 0:n], vcol(gcol + c), rs.ap[:, 0:n],
                        ALU.mult, ALU.mult, [hs, rs, vecs], [ys])
                dma(P, 'sp', D['yT'].rearrange('c p t -> p c t')[:, :, t0:t0 + n], ys.ap[:, :, 0:n], reads=[ys])
        return xn, xnb, mark_xn, xlo, mark_lo

    def load_w(wt, w2d, col0, ncols, kch):
        src = w2d[:, col0:col0 + ncols].rearrange('(kc p) n -> p kc n', p=128)
        dma(P, 'pool', wt.ap[:, 0:kch, 0:ncols], src, writes=[wt])

    def phase_inproj_qk(l, xn, xnb, xlo, mark_lo):
        P.barrier()
        A.top = mark_lo
        w2d = D['w_in'][l]
        whr = Ring([A.alloc([8, 512], BF16, 'whi%d' % i) for i in range(2)])
        w32 = A.alloc([8, 512], F32, 'w32')
        wlo = A.alloc([8, 512], BF16, 'wlo')
        st32 = Ring([A.alloc([512], F32, 'st32_%d' % i) for i in range(4)])
        for blk in range(2):
            whi = whr.next()
            load_w(whi, w2d, C_QG + blk * 512, 512, 8)
            dma(P, 'sp', w32.ap, w2d[:, C_QG + blk * 512:C_QG + (blk + 1) * 512].rearrange('(kc p) n -> p kc n', p=128),
                writes=[w32])
            tt(P, 'pool', wlo.ap, w32.ap, whi.ap, ALU.subtract, [w32, whi], [wlo])
            for x in range(4):
                cidx = blk * 4 + x
                for si, (t0, n) in enumerate(SUPER):
                    ps = PS.next()
                    k_ = 0
                    for wt_, xt_ in ((whi, xn), (whi, xlo), (wlo, xn)):
                        for kc in range(8):
                            mm(P, ps.ap[:, 0:n], wt_.ap[:, kc, x * 128:(x + 1) * 128], xt_.ap[:, kc, t0:t0 + n],
                               k_ == 0, k_ == 23, [wt_, xnb[si]], [ps])
                            k_ += 1
                    sf = st32.next()
                    cp(P, evac_eng(), sf.ap[:, 0:n], ps.ap[:, 0:n], [ps], [sf])
                    dma(P, 'sp', D['QK32'][cidx, :, t0:t0 + n], sf.ap[:, 0:n], reads=[sf])

    def phase_inproj(l, xn, xnb, mark):
        P.barrier()
        A.top = mark
        w2d = D['w_in'][l]
        wr = Ring([A.alloc([8, 512], BF16, 'wb%d' % i) for i in range(3)])
        stg = Ring([A.alloc([NT], BF16, 'stg%d' % i) for i in range(3)])
        stf = Ring([A.alloc([NT], F32, 'stf%d' % i) for i in range(2)])
        tst = Ring([A.alloc([512], BF16, 'tst%d' % i) for i in range(3)])
        tsf = Ring([A.alloc([512], F32, 'tsf%d' % i) for i in range(3)])

        def fm_group(col0, nch, kind, fbase, Mlast=128):
            ci = 0
            while ci < nch:
                nb = min(4, nch - ci)
                ncols = sum(128 if (ci + x) < nch - 1 else Mlast for x in range(nb))
                wt = wr.next()
                load_w(wt, w2d, col0 + ci * 128, ncols, 8)
                for x in range(nb):
                    cidx = ci + x
                    M = 128 if cidx < nch - 1 else Mlast
                    sb = stg.next() if kind in ('pf', 'ka') else None
                    sf = stf.next() if kind in ('up', 'ka') else None
                    for si, (t0, n) in enumerate(SUPER):
                        ps = PS.next()
                        for kc in range(8):
                            mm(P, ps.ap[0:M, 0:n], wt.ap[:, kc, x * 128:x * 128 + M], xn.ap[:, kc, t0:t0 + n],
                               kc == 0, kc == 7, [wt, xnb[si]], [ps])
                        if sf is not None:
                            cp(P, evac_eng(), sf.ap[0:M, t0:t0 + n], ps.ap[0:M, 0:n], [ps], [sf])
                            if sb is not None:
                                cp(P, 'pool', sb.ap[0:M, t0:t0 + n], sf.ap[0:M, t0:t0 + n], [sf], [sb])
                        elif sb is not None:
                            cp(P, evac_eng(), sb.ap[0:M, t0:t0 + n], ps.ap[0:M, 0:n], [ps], [sb])
                    if kind in ('pf', 'ka'):
                        dma(P, 'sp', D['PF'][fbase + cidx, 0:M, :], sb.ap[0:M, :], reads=[sb])
                    if kind == 'up':
                        dma(P, 'sp', D['UP'][cidx], sf.ap, reads=[sf])
                    if kind == 'ka':
                        g, hp = cidx // 2, cidx % 2
                        keep = GROUPS[g][0]
                        for s in range(NPS):
                            dma(P, 'sp', D['kT%d' % g][l, s, hp], sf.ap[:, (s + 1) * SEQ - keep:(s + 1) * SEQ],
                                reads=[sf])
                ci += nb

        def on(k):
            return sub is None or k in sub
        if on('rg'):
            fm_group(C_RG, 8, 'pf', F_RG)
        if on('alr'):
            fm_group(C_ALR, 1, 'pf', F_ALR)
        if on('up'):
            fm_group(C_UP, 8, 'up', 0)
        if on('qa'):
            fm_group(C_QA, 6, 'pf', F_QA)
        if on('ka'):
            fm_group(C_KA, 6, 'ka', F_KA)
        if on('ga'):
            fm_group(C_GA, 24, 'pf', F_GA)

        tiles = [(i * 128, 128) for i in range(NTP // 128)] + [(NTP, NSS * DSEQ)]

        def tm_group(col0, ncols_tot, kind):
            c = 0
            while c < ncols_tot:
                ncols = min(512, ncols_tot - c)
                wt = wr.next()
                load_w(wt, w2d, col0 + c, ncols, 8)
                for ti, (t0, n) in enumerate(tiles):
                    if kind == 'ks' and ti < len(tiles) - 1:
                        continue
                    si = min(t0 // 512, len(SUPER) - 1)
                    ps = PS.next()
                    for kc in range(8):
                        mm(P, ps.ap[0:n, 0:ncols], xn.ap[:, kc, t0:t0 + n], wt.ap[:, kc, 0:ncols],
                           kc == 0, kc == 7, [wt, xnb[si]], [ps])
                    if kind in ('vg', 'va'):
                        tb = tst.next()
                        e1 = evac_eng()
                        cp(P, e1, tb.ap[0:n, 0:ncols], ps.ap[0:n, 0:ncols], [ps], [tb])
                        dst = D['VG'] if kind == 'vg' else D['VA']
                        dma(P, 'sp', dst[t0:t0 + n, c:c + ncols], tb.ap[0:n, 0:ncols], reads=[tb])
                    if kind in ('va', 'ks'):
                        sample = ti == len(tiles) - 1
                        need = []
                        for g in range(3):
                            gc0 = g * 256
                            if not (c <= gc0 < c + ncols):
                                continue
                            keep = GROUPS[g][0]
                            if sample:
                                need.append((g, gc0 - c))
                            else:
                                lt = (t0 % SEQ) // 128
                                if lt * 128 >= SEQ - keep:
                                    need.append((g, gc0 - c))
                        if need:
                            tf = tsf.next()
                            cp(P, e1 if kind == 'va' else evac_eng(), tf.ap[0:n, 0:ncols], ps.ap[0:n, 0:ncols], [ps], [tf])
                            for g, off in need:
                                keep = GROUPS[g][0]
                                if sample:
                                    nm = ('v%ds' if kind == 'va' else 'k%ds') % g
                                    for b in range(NSS):
                                        dma(P, 'sp', D[nm][l, b, keep - DSEQ:keep, :],
                                            tf.ap[b * DSEQ:(b + 1) * DSEQ, off:off + 256], reads=[tf])
                                else:
                                    s = t0 // SEQ
                                    r0 = (t0 % SEQ) - (SEQ - keep)
                                    dma(P, 'sp', D['v%dp' % g][l, s, r0:r0 + 128, :], tf.ap[0:128, off:off + 256],
                                        reads=[tf])
                c += ncols

        if on('vg'):
            tm_group(C_VG, 1024, 'vg')
        if on('va'):
            tm_group(C_VA, 768, 'va')
        if on('ks'):
            tm_group(C_KA, 768, 'ks')

    def gla_shared(l, PT, C):
        wa2 = A.alloc([512], BF16, 'wa2')
        ms(P, 'pool', wa2.ap, 0.0, [wa2])
        dma(P, 'pool', wa2.ap[0:16, :], D['gla_wa2'][l], writes=[wa2])
        maskU4 = A.alloc([4, 128], F32, 'maskU4')
        for h in range(4):
            dma(P, 'sp', maskU4.ap[:, h, :], D['maskU'], writes=[maskU4])
        rm = A.alloc([4 * PT], F32, 'resetm')
        ms(P, 'pool', rm.ap, 1.0, [rm])
        ms(P, 'pool', rm.ap.rearrange('p (a c) -> p a c', c=C)[:, :, 0:1], 0.0, [rm])
        Gt = A.alloc([8, C], F32, 'Gt')
        ms(P, 'pool', Gt.ap, 1.0, [Gt])
        for hv in range(8):
            ts(P, 'pool', Gt.ap[:, hv, :], Gt.ap[:, hv, :], vcol(V_GN + l * 2 + hv % 2), None, ALU.mult, ALU.bypass,
               [Gt, vecs], [Gt])
        return wa2, maskU4, rm, Gt

    def gen_gla(l, tok0, T, C, s0_src, s_out, shared, PTMAX=256, cache=None):
        def AL(free, dt, name, parts=128):
            if cache is None:
                return A.alloc(free, dt, name, parts)
            k_ = (name, tuple(free), str(dt))
            if k_ not in cache:
                cache[k_] = A.alloc(free, dt, name, parts)
            return cache[k_]
        PT = min(T, PTMAX)
        npiece = T // PT
        nch = PT // C
        DKS = 128.0 ** -0.5
        wa2, maskU4, rm, Gt = shared
        S = AL([4, 256], F32, 'S')
        Sb = AL([4, 256], BF16, 'Sb')
        if s0_src is None:
            ms(P, 'pool', S.ap, 0.0, [S])
        else:
            dma(P, 'sp', S.ap, s0_src.rearrange('h k v -> k h v'), writes=[S])
        cp(P, 'pool', Sb.ap, S.ap, [S], [Sb])
        qr = Ring([AL([4, PT], F32, 'q%d' % i) for i in range(1)])
        kr = Ring([AL([4, PT], F32, 'k%d' % i) for i in range(1)])
        rr = Ring([AL([8, PT], BF16, 'r%d' % i) for i in range(1)])
        vr = Ring([AL([nch, 1024], BF16, 'v%d' % i) for i in range(2)])
        yield
        ar = Ring([AL([PT], BF16, 'alr%d' % i) for i in range(2)])
        lbuf = AL([4, PT], F32, 'lbuf')
        cbuf = AL([4, PT], F32, 'cbuf')
        ein = AL([4, PT], F32, 'ein')
        eq = AL([4, PT], F32, 'eq')
        ek = AL([4, PT], F32, 'ek')
        est = AL([4, PT], BF16, 'est')
        if C == 128:
            qs, ks = eq, ek
        else:
            qs = AL([4, PT], BF16, 'qs')
            ks = AL([4, PT], BF16, 'ks')
        qin = AL([4, PT], BF16, 'qin')
        kstT = AL([4, PT], BF16, 'kstT')
        attr = Ring([AL([4, C], BF16, 'attm%d' % i) for i in range(2)])
        kstr = Ring([AL([4, 128], BF16, 'kst%d' % i) for i in range(2)])
        sqg = AL([8, C], BF16, 'sqg')
        lnb = AL([4, C], F32, 'lnb')
        rstd = AL([4, C], F32, 'rstd')
        sr = AL([8, C], F32, 'sr')
        srg = AL([8, C], F32, 'srg')
        t1 = AL([8, C], F32, 't1')
        ogr = Ring([AL([8, PT], BF16, 'ogst%d' % i) for i in range(2)])
        r_ref = (C - 1) // 2
        PFv = D['PF']
        for pi in range(npiece):
            p0 = tok0 + pi * PT
            qT, kT, rT, vt, alr = qr.next(), kr.next(), rr.next(), vr.next(), ar.next()
            dma(P, 'sp', qT.ap, D['QK32'][0:4].rearrange('h p t -> p h t')[:, :, p0:p0 + PT], writes=[qT])
            dma(P, 'sp', kT.ap, D['QK32'][4:8].rearrange('h p t -> p h t')[:, :, p0:p0 + PT], writes=[kT])
            dma(P, 'sp', rT.ap, PFv[F_RG:F_RG + 8].rearrange('h p t -> p h t')[:, :, p0:p0 + PT], writes=[rT])
            dma(P, 'sp', alr.ap, PFv[F_ALR, :, p0:p0 + PT], writes=[alr])
            if C == 128:
                dma(P, 'sp', vt.ap, D['VG'][p0:p0 + PT, :].rearrange('(a p) c -> p a c', p=128), writes=[vt])
            else:
                dma(P, 'sp', vt.ap[0:C, 0, :], D['VG'][p0:p0 + PT, :], writes=[vt])
            for h in range(4):
                ps = PS.next()
                mm(P, ps.ap[:, 0:PT], wa2.ap[:, h * 128:(h + 1) * 128], alr.ap[:, 0:PT], True, True,
                   [wa2, alr], [ps])
                act(P, lbuf.ap[:, h, :], ps.ap[:, 0:PT], AF.Exp, [ps, nba], [lbuf], scale=-1.0,
                    bias=nba.ap[:, l * 4 + h:l * 4 + h + 1])
            act(P, lbuf.ap, lbuf.ap, AF.Ln, [lbuf], [lbuf], scale=1.0, bias=1.0)
            lf = lbuf.ap.rearrange('p h t -> p (h t)')
            cf = cbuf.ap.rearrange('p h t -> p (h t)')
            P.op('dve', lambda e, cf=cf, lf=lf: e.tensor_tensor_scan(out=cf, data0=rm.ap, data1=lf, initial=0.0,
                                                                   op0=ALU.mult, op1=ALU.add),
                 reads=[rm.b, lbuf.b], writes=[cbuf.b])
            cv = cf.rearrange('p (a c) -> p a c', c=C)
            lv = lf.rearrange('p (a c) -> p a c', c=C)
            na = 4 * nch
            tt(P, 'dve', lv, cv, cv[:, :, r_ref:r_ref + 1].to_broadcast([128, na, C]), ALU.subtract, [cbuf], [lbuf])
            act(P, eq.ap, lbuf.ap, AF.Exp, [lbuf], [eq], scale=-1.0 / 16)
            act(P, ek.ap, lbuf.ap, AF.Exp, [lbuf], [ek], scale=1.0 / 16)
            act(P, ein.ap, cbuf.ap, AF.Exp, [cbuf], [ein], scale=-1.0 / 16)
            tt(P, 'dve', lv, cv, cv[:, :, C - 1:C].to_broadcast([128, na, C]), ALU.subtract, [cbuf, eq, ek], [lbuf])
            act(P, est.ap, lbuf.ap, AF.Exp, [lbuf], [est], scale=1.0 / 16)
            stt(P, qs.ap, qT.ap, DKS, eq.ap, ALU.mult, ALU.mult, [qT, eq], [qs, eq])
            tt(P, 'pool', ks.ap, kT.ap, ek.ap, ALU.mult, [kT, ek], [ks, ek])
            stt(P, qin.ap, qT.ap, DKS, ein.ap, ALU.mult, ALU.mult, [qT, ein], [qin])
            tt(P, 'pool', kstT.ap, kT.ap, est.ap, ALU.mult, [kT, est], [kstT])
            ogst = ogr.next()
            yield
            for ci in range(nch):
                cs = ci * C
                psa = PS.next()
                for h in range(4):
                    mm(P, psa.ap[0:C, h * 128:h * 128 + C], ks.ap[:, h, cs:cs + C], qs.ap[:, h, cs:cs + C], True, True,
                       [ks, qs], [psa])
                attm = attr.next()
                tt(P, 'dve', attm.ap[0:C, :, :], psa.ap[0:C, :].rearrange('p (h t) -> p h t', t=128)[:, :, 0:C],
                   maskU4.ap[0:C, :, 0:C], ALU.mult, [psa, maskU4], [attm])
                for h in range(4):
                    tr(P, psb.ap[0:C, h * 128:(h + 1) * 128], kstT.ap[:, h, cs:cs + C], ident16.ap, [kstT, ident16], [psb])
                kst = kstr.next()
                cp(P, 'act', kst.ap[0:C, :, :], psb.ap[0:C, 0:512].rearrange('p (h k) -> p h k', k=128), [psb], [kst])
                if C == 128:
                    yield
                pso = [PS.next(), PS.next()]
                for h in range(4):
                    for vc in range(2):
                        off = ((h % 2) * 2 + vc) * 128
                        o_ap = pso[h // 2].ap[:, off:off + C]
                        mm(P, o_ap, vt.ap[0:C, ci, h * 256 + vc * 128:h * 256 + (vc + 1) * 128], attm.ap[0:C, h, :],
                           True, False, [vt, attm], [pso[h // 2]])
                        mm(P, o_ap, Sb.ap[:, h, vc * 128:(vc + 1) * 128], qin.ap[:, h, cs:cs + C],
                           False, True, [Sb, qin], [pso[h // 2]])
                if C == 128:
                    yield
                for b in range(2):
                    act(P, sqg.ap[:, b * 4:(b + 1) * 4, :],
                        pso[b].ap.rearrange('p (x t) -> p x t', t=128)[:, :, 0:C], AF.Square, [pso[b]], [sqg])
                pss = PS.next()
                for h in range(4):
                    for vc in range(2):
                        mm(P, pss.ap[:, h * 128:h * 128 + C], ones16.ap, sqg.ap[:, h * 2 + vc, :], vc == 0, vc == 1,
                           [ones16, sqg], [pss])
                if C == 128:
                    yield
                act(P, lnb.ap, pss.ap.rearrange('p (h t) -> p h t', t=128)[:, :, 0:C], AF.Ln, [pss], [lnb],
                    scale=1.0 / 256, bias=1e-6)
                act(P, rstd.ap, lnb.ap, AF.Exp, [lnb], [rstd], scale=-0.5)
                act(P, sr.ap, rT.ap[:, :, cs:cs + C], AF.Silu, [rT], [sr])
                tt(P, 'pool', srg.ap, sr.ap, Gt.ap, ALU.mult, [sr, Gt], [srg])
                for h in range(4):
                    off = (h % 2) * 256
                    tt(P, 'dve', t1.ap[:, 2 * h:2 * h + 2, :],
                       pso[h // 2].ap[:, off:off + 256].rearrange('p (x t) -> p x t', t=128)[:, :, 0:C],
                       rstd.ap[:, h:h + 1, :].to_broadcast([128, 2, C]), ALU.mult, [pso[h // 2], rstd], [t1])
                tt(P, 'pool', ogst.ap[:, :, cs:cs + C], t1.ap, srg.ap, ALU.mult, [t1, srg], [ogst])
                if C == 128:
                    yield
                psk = [PS.next(), PS.next()]
                for h in range(4):
                    mm(P, psk[h // 2].ap[:, (h % 2) * 256:(h % 2 + 1) * 256], kst.ap[0:C, h, :],
                       vt.ap[0:C, ci, h * 256:(h + 1) * 256], True, True, [kst, vt], [psk[h // 2]])
                for h in range(4):
                    stt(P, S.ap[:, h, :], S.ap[:, h, :], ein.ap[:, h, cs + C - 1:cs + C],
                        psk[h // 2].ap[:, (h % 2) * 256:(h % 2 + 1) * 256], ALU.mult, ALU.add,
                        [S, ein, psk[h // 2]], [S])
                cp(P, 'pool', Sb.ap, S.ap, [S], [Sb])
                if C == 128:
                    yield
            dma(P, 'sp', D['OG'].rearrange('c p t -> p c t')[:, :, p0:p0 + PT], ogst.ap, reads=[ogst])
        dma(P, 'sp', s_out.rearrange('h k v -> k h v'), S.ap, reads=[S])

    def pool_shared(l):
        pw = A.alloc([4, 2, 256], BF16, 'pw')
        dma(P, 'pool', pw.ap, D['pool_w'][l].rearrange('g (kc p) d -> p g kc d', p=128), writes=[pw])
        invc = A.alloc([8, 16], F32, 'invc')
        dma(P, 'sp', invc.ap, D['invc'], writes=[invc])
        return pw, invc

    def gen_pool(l, seqidx, tok0, T, sample_b, shared):
        PT = min(T, 512)
        pw, invc = shared
        pooled = A.alloc([8, T], BF16, 'pooled')
        pmr = Ring([A.alloc([PT], BF16, 'pmst%d' % i) for i in range(3)])
        ur = Ring([A.alloc([15 + T], F32, 'u%d' % i) for i in range(2)])
        sr_ = Ring([A.alloc([15 + T], F32, 'ws%d' % i) for i in range(2)])
        tmpf = Ring([A.alloc([16], F32, 'tmpf%d' % i) for i in range(2)])
        for cc in range(8):
            w = (2, 4, 8, 16)[cc // 2]
            u = ur.next()
            if sample_b is None:
                ms(P, 'pool', u.ap[:, 0:15], 0.0, [u])
            else:
                dma(P, 'sp', u.ap[:, 0:15], D['spoolT'][l, cc, :, sample_b, :], writes=[u])
            dma(P, 'sp', u.ap[:, 15:15 + T], D['UP'][cc, :, tok0:tok0 + T], writes=[u])
            cur = u
            k = 1
            lo = 0
            while k < w:
                nx = sr_.next()
                lo2 = lo + k
                tt(P, 'pool' if cc in (1, 7) else 'dve', nx.ap[:, lo2:15 + T], cur.ap[:, lo2:15 + T],
                   cur.ap[:, lo2 - k:15 + T - k], ALU.add, [cur], [nx])
                cur = nx
                lo = lo2
                k *= 2
            stt(P, pooled.ap[:, cc, :], cur.ap[:, 15:15 + T], 1.0 / w, u.ap[:, 15:15 + T], ALU.mult, ALU.subtract,
                [cur, u], [pooled])
            if sample_b is None:
                tf = tmpf.next()
                tt(P, 'dve', tf.ap, cur.ap[:, 15:31], invc.ap[:, cc, :], ALU.mult, [cur, invc], [tf])
                tt(P, 'dve', pooled.ap[:, cc, 0:16], tf.ap, u.ap[:, 15:31], ALU.subtract, [tf, u], [pooled])
            dma(P, 'sp', D['poolT'][l, cc, :, seqidx, :], u.ap[:, T:T + 15], reads=[u])
            yield
        for dc in range(8):
            g, half = dc // 2, dc % 2
            for t in range(0, T, PT):
                ps = PS.next()
                for kc in range(2):
                    mm(P, ps.ap[:, 0:PT], pw.ap[:, g, kc, half * 128:(half + 1) * 128],
                       pooled.ap[:, g * 2 + kc, t:t + PT], kc == 0, kc == 1, [pw, pooled], [ps])
                pm = pmr.next()
                act(P, pm.ap[:, 0:PT], ps.ap[:, 0:PT], AF.Copy, [ps, vecs], [pm],
                    scale=vcol(V_PS + l * 8 + dc))
                dma(P, 'sp', D['PM'][dc, :, tok0 + t:tok0 + t + PT], pm.ap[:, 0:PT], reads=[pm])
            yield

    def gen_attn_prompt(l, tok0):
        T = SEQ
        bm = A.alloc([12, 256], F32, 'bm')
        dma(P, 'sp', bm.ap, D['bm'], writes=[bm])
        qN = A.alloc([2, T], BF16, 'qN')
        kN = A.alloc([2, T], BF16, 'kN')
        qPr = Ring([A.alloc([2, T], BF16, 'qP%d' % i) for i in range(2)])
        kPr = Ring([A.alloc([2, T], BF16, 'kP%d' % i) for i in range(2)])
        vP = A.alloc([16, 256], BF16, 'vP')
        vZ = [A.alloc([16, 2, 128], BF16, 'vZ%d' % j) for j in range(2)]
        Oacc = A.alloc([6, T], BF16, 'Oacc')
        Dsum = A.alloc([2, T], F32, 'Dsum')
        rden = Dsum
        tmpr = Ring([A.alloc([2, 256], F32, 'sc%d' % i) for i in range(3)])
        ptr_ = Ring([A.alloc([2, 256], BF16, 'pT%d' % i) for i in range(5)])
        for j in range(2):
            ms(P, 'pool', vZ[j].ap, 0.0, [vZ[j]])
        pair_rr = [0]
        psb32 = bm.__class__(psb.ap.bitcast(F32), 'psb32')
        psb32.b = psb.b
        PSO = Ring(psf[4:6])
        PSD = Ring([psf[6], psb32])
        for g, (win, dil) in enumerate(GROUPS):
            nb = T // dil // 128
            dma(P, 'sp', qN.ap, D['PF'][F_QA + 2 * g:F_QA + 2 * g + 2].rearrange('h p t -> p h t')[:, :, tok0:tok0 + T],
                writes=[qN])
            dma(P, 'sp', kN.ap, D['PF'][F_KA + 2 * g:F_KA + 2 * g + 2].rearrange('h p t -> p h t')[:, :, tok0:tok0 + T],
                writes=[kN])
            vsrc = D['VA'][tok0:tok0 + T, g * 256:(g + 1) * 256].rearrange('(b p r) c -> p r b c', p=128, r=dil)
            for r in range(dil):
                dma(P, 'sp', vP.ap[:, r * nb:(r + 1) * nb, :], vsrc[:, r, :, :], writes=[vP])
            qU, kU = qPr.next(), kPr.next()
            if dil == 1:
                cp(P, 'pool', qU.ap, qN.ap, [qN], [qU])
                cp(P, 'pool', kU.ap, kN.ap, [kN], [kU])
            else:
                cp(P, 'pool', qU.ap.rearrange('p h (r i) -> p h r i', r=dil),
                   qN.ap.rearrange('p h (i r) -> p h r i', r=dil), [qN], [qU])
                cp(P, 'pool', kU.ap.rearrange('p h (r i) -> p h r i', r=dil),
                   kN.ap.rearrange('p h (i r) -> p h r i', r=dil), [kN], [kU])
            vPv = vP.ap.rearrange('p t (hp x) -> p t hp x', x=128)
            cp(P, 'dve', vZ[0].ap[:, :, :, 0:64], vPv[:, :, :, 0:64], [vP], [vZ[0]])
            cp(P, 'dve', vZ[1].ap[:, :, :, 64:128], vPv[:, :, :, 64:128], [vP], [vZ[1]])
            for hp in range(2):
                Ov = Oacc.ap[:, g * 2 + hp, :].rearrange('p (b q r) -> p b q r', q=128, r=dil)
                Dv = Dsum.ap[:, hp, :].rearrange('p (b q r) -> p b q r', q=128, r=dil)
                def scores(r, b):
                    kt = r * nb + b
                    nq = 256 if b < nb - 1 else 128
                    h0 = g * 4 + hp * 2
                    pk = pair_rr[0]
                    pair_rr[0] = (pk + 1) % 2
                    pbanks = [psf[2 * pk], psf[2 * pk + 1]]
                    for j in range(2):
                        mm(P, pbanks[j].ap[:, 0:nq], kU.ap[j * 64:(j + 1) * 64, hp, kt * 128:(kt + 1) * 128],
                           qU.ap[j * 64:(j + 1) * 64, hp, kt * 128:kt * 128 + nq], True, True, [kU, qU], [pbanks[j]])
                    sc = tmpr.next()
                    stt(P, sc.ap[:, :, 0:nq], pq[pk][:, :].rearrange('p (j q) -> p j q', q=512)[:, :, 0:nq], 0.125,
                        bm.ap[:, h0:h0 + 2, 0:nq], ALU.mult, ALU.add, pbanks + [bm], [sc])
                    pT = ptr_.next()
                    act(P, pT.ap[:, :, 0:nq], sc.ap[:, :, 0:nq], AF.Exp, [sc], [pT])
                    return pT

                units = [(r, b) for r in range(dil) for b in range(nb)]
                nxt = scores(*units[0])
                prev = None
                for ui, (r, b) in enumerate(units):
                    kt = r * nb + b
                    cur = nxt
                    nxt = scores(*units[ui + 1]) if ui + 1 < len(units) else None
                    if b == 0:
                        prev = None
                    pso, psd = PSO.next(), PSD.next()
                    srcs = []
                    for j in range(2):
                        if prev is not None:
                            srcs.append((j, kt - 1, prev, 128))
                        srcs.append((j, kt, cur, 0))
                    for which in range(2):
                        pt_ = pso if which == 0 else psd
                        for n_, (j, ktile, pt, coff) in enumerate(srcs):
                            lhs = vZ[j].ap[:, ktile, hp, :] if which == 0 else onesZ[j].ap
                            mm(P, pt_.ap[:, 0:128], lhs, pt.ap[:, j, coff:coff + 128], n_ == 0, n_ == len(srcs) - 1,
                               [vZ[j], onesZ[j], pt], [pt_])
                    cp(P, 'act', Ov[:, b, :, r], pso.ap[:, 0:128], [pso], [Oacc])
                    if g == 0:
                        cp(P, 'dve', Dv[:, b, :, r], psd.ap[:, 0:128], [psd], [Dsum])
                    else:
                        tt(P, 'dve', Dv[:, b, :, r], Dv[:, b, :, r], psd.ap[:, 0:128], ALU.add, [psd, Dsum], [Dsum])
                    prev = cur
                    yield
        act(P, Dsum.ap, Dsum.ap, AF.Ln, [Dsum], [Dsum])
        act(P, Dsum.ap, Dsum.ap, AF.Exp, [Dsum], [Dsum], scale=-1.0)
        for g in range(3):
            tt(P, 'dve', Oacc.ap[:, 2 * g:2 * g + 2, :], Oacc.ap[:, 2 * g:2 * g + 2, :], Dsum.ap,
               ALU.mult, [Oacc, Dsum], [Oacc])
        dma(P, 'sp', D['OC'].rearrange('c p t -> p c t')[:, :, tok0:tok0 + T], Oacc.ap, reads=[Oacc])

    def gen_attn_sample(l):
        tokS0 = NTP
        MAXT = 17
        bms = []
        for g, (win, dil) in enumerate(GROUPS):
            t_ = A.alloc([4, (win // 128 + 1) * 8], F32, 'bms%d' % g)
            dma(P, 'sp', t_.ap, D['bms%d' % g].rearrange('p h t q -> p h (t q)'), writes=[t_])
            bms.append(t_)
        kf = Ring([A.alloc([16, 256], BF16, 'kf%d' % i) for i in range(1)])
        vf = Ring([A.alloc([16, 256], BF16, 'vf%d' % i) for i in range(1)])
        vnew = A.alloc([256], BF16, 'vnew')
        kTs = A.alloc([2, MAXT * 128], BF16, 'kTs')
        qTs = A.alloc([2, 8], BF16, 'qTs')
        vZ = [A.alloc([MAXT, 2, 128], BF16, 'vZs%d' % j) for j in range(2)]
        Os = A.alloc([6, 8], F32, 'Os')
        Ds = A.alloc([2, 8], F32, 'Ds')
        rd = A.alloc([2, 8], F32, 'rd')
        ocs = A.alloc([6, 8], BF16, 'ocs')
        scr_ = Ring([A.alloc([MAXT * 8], F32, 'scs%d' % i) for i in range(2)])
        pts = Ring([A.alloc([MAXT * 8], BF16, 'pTs%d' % i) for i in range(4)])
        for j in range(2):
            ms(P, 'pool', vZ[j].ap, 0.0, [vZ[j]])
        ms(P, 'pool', vnew.ap, 0.0, [vnew])
        for b in range(NSS):
            tk = tokS0 + b * DSEQ
            for g, (win, dil) in enumerate(GROUPS):
                ntc = win // 128
                nt_ = ntc + 1
                dma(P, 'sp', D['k%ds' % g][l, b, 0:win - DSEQ, :], D['ck%d' % g][l, b, DSEQ:win, :])
                dma(P, 'sp', D['v%ds' % g][l, b, 0:win - DSEQ, :], D['cv%d' % g][l, b, DSEQ:win, :])
                kc_, vc_ = kf.next(), vf.next()
                dma(P, 'pool', kc_.ap[:, 0:ntc, :], D['ck%d' % g][l, b].rearrange('(t p) c -> p t c', p=128), writes=[kc_])
                dma(P, 'pool', vc_.ap[:, 0:ntc, :], D['cv%d' % g][l, b].rearrange('(t p) c -> p t c', p=128), writes=[vc_])
                ms(P, 'pool', kTs.ap, 0.0, [kTs])
                dma(P, 'sp', kTs.ap[:, :, win:win + DSEQ],
                    D['PF'][F_KA + 2 * g:F_KA + 2 * g + 2].rearrange('h p t -> p h t')[:, :, tk:tk + DSEQ], writes=[kTs])
                dma(P, 'sp', qTs.ap, D['PF'][F_QA + 2 * g:F_QA + 2 * g + 2].rearrange('h p t -> p h t')[:, :, tk:tk + DSEQ],
                    writes=[qTs])
                dma(P, 'sp', vnew.ap[0:DSEQ, :], D['VA'][tk:tk + DSEQ, g * 256:(g + 1) * 256], writes=[vnew])
                for hp in range(2):
                    for t0_ in range(0, ntc, 8):
                        nn = min(8, ntc - t0_)
                        for x in range(nn):
                            tr(P, psb.ap[:, x * 128:(x + 1) * 128], kc_.ap[:, t0_ + x, hp * 128:(hp + 1) * 128],
                               ident16.ap, [kc_, ident16], [psb])
                        cp(P, 'act', kTs.ap[:, hp, t0_ * 128:(t0_ + nn) * 128], psb.ap[:, 0:nn * 128], [psb], [kTs])
                vcv = vc_.ap.rearrange('p t (hp x) -> p t hp x', x=128)
                cp(P, 'pool', vZ[0].ap[:, 0:ntc, :, 0:64], vcv[:, 0:ntc, :, 0:64], [vc_], [vZ[0]])
                cp(P, 'pool', vZ[1].ap[:, 0:ntc, :, 64:128], vcv[:, 0:ntc, :, 64:128], [vc_], [vZ[1]])
                vnv = vnew.ap.rearrange('p (hp x) -> p hp x', x=128)
                cp(P, 'pool', vZ[0].ap[:, ntc, :, 0:64], vnv[:, :, 0:64], [vnew], [vZ[0]])
                cp(P, 'pool', vZ[1].ap[:, ntc, :, 64:128], vnv[:, :, 64:128], [vnew], [vZ[1]])
                for hp in range(2):
                    cur = []
                    for j in range(2):
                        hh = hp * 2 + j
                        ps = PS.next()
                        for t in range(nt_):
                            mm(P, ps.ap[:, t * 8:(t + 1) * 8], kTs.ap[j * 64:(j + 1) * 64, hp, t * 128:(t + 1) * 128],
                               qTs.ap[j * 64:(j + 1) * 64, hp, :], True, True, [kTs, qTs], [ps])
                        sc = scr_.next()
                        stt(P, sc.ap[:, 0:nt_ * 8], ps.ap[:, 0:nt_ * 8], 0.125, bms[g].ap[:, hh, :], ALU.mult, ALU.add,
                            [ps, bms[g]], [sc])
                        pT = pts.next()
                        act(P, pT.ap[:, 0:nt_ * 8], sc.ap[:, 0:nt_ * 8], AF.Exp, [sc], [pT])
                        cur.append(pT)
                    pso = PS.next()
                    for which in range(2):
                        oap = pso.ap[:, which * 8:(which + 1) * 8]
                        cnt = 0
                        for j in range(2):
                            for t in range(nt_):
                                lhs = vZ[j].ap[:, t, hp, :] if which == 0 else onesZ[j].ap
                                mm(P, oap, lhs, cur[j].ap[:, t * 8:(t + 1) * 8], cnt == 0, cnt == 2 * nt_ - 1,
                                   [vZ[j], onesZ[j], cur[j]], [pso])
                                cnt += 1
                    cp(P, 'act', Os.ap[:, g * 2 + hp, :], pso.ap[:, 0:8], [pso], [Os])
                    if g == 0:
                        cp(P, 'dve', Ds.ap[:, hp, :], pso.ap[:, 8:16], [pso, Os], [Ds])
                    else:
                        tt(P, 'dve', Ds.ap[:, hp, :], Ds.ap[:, hp, :], pso.ap[:, 8:16], ALU.add, [pso, Ds, Os], [Ds])
                    yield
            P.op('dve', lambda e: e.reciprocal(out=rd.ap, in_=Ds.ap), reads=[Ds.b], writes=[rd.b])
            for g in range(3):
                tt(P, 'dve', ocs.ap[:, 2 * g:2 * g + 2, :], Os.ap[:, 2 * g:2 * g + 2, :], rd.ap, ALU.mult, [Os, rd], [ocs])
            dma(P, 'sp', D['OC'].rearrange('c p t -> p c t')[:, :, tk:tk + DSEQ], ocs.ap, reads=[ocs])

    def load_outproj_weights(l, lazy=False):
        wo = {}
        todo = []
        for nm, kch in (('w_oa', 8), ('w_ob', 8), ('w_oc', 6), ('w_out', 8)):
            wo[nm] = A.alloc([kch, 1024], BF16, nm)
            for kc in range(kch):
                todo.append((nm, kc))

        def issue(k=1):
            for _ in range(k):
                if todo:
                    nm, kc = todo.pop(0)
                    dma(P, 'pool', wo[nm].ap[:, kc, :], D[nm][l, kc * 128:(kc + 1) * 128, :], writes=[wo[nm]])
        if not lazy:
            issue(len(todo))
        return wo, A.top, issue, todo

    def phase_outproj(l, src_h, wo, mark_wo):
        P.barrier()
        A.top = mark_wo
        obr = Ring([A.alloc([8, 512], BF16, 'ob%d' % i) for i in range(2)])
        gbr = Ring([A.alloc([8, 512], BF16, 'gb%d' % i) for i in range(2)])
        sgr = Ring([A.alloc([512], F32, 'sg%d' % i) for i in range(2)])
        tmr = Ring([A.alloc([512], F32, 'tm%d' % i) for i in range(2)])
        merged = A.alloc([8, 512], F32, 'merged')
        m16 = A.alloc([8, 512], BF16, 'm16')
        hr = Ring([A.alloc([8, 512], F32, 'h%d' % i) for i in range(2)])
        hn = Ring([A.alloc([8, 512], F32, 'hn%d' % i) for i in range(2)])
        srcv = src_h.rearrange('c p t -> p c t')
        branches = (('w_oa', 'OG', 8, F_GA), ('w_ob', 'PM', 8, F_GB), ('w_oc', 'OC', 6, F_GC))
        for si, (t0, n) in enumerate(SUPER):
            ht = hr.next()
            dma(P, 'sp', ht.ap[:, :, 0:n], srcv[:, :, t0:t0 + n], writes=[ht])
            for bi, (wn, sn, kch, fg) in enumerate(branches):
                ob, gb = obr.next(), gbr.next()
                dma(P, 'sp', ob.ap[:, 0:kch, 0:n], D[sn].rearrange('c p t -> p c t')[:, :, t0:t0 + n], writes=[ob])
                dma(P, 'sp', gb.ap[:, :, 0:n], D['PF'][fg:fg + 8].rearrange('c p t -> p c t')[:, :, t0:t0 + n], writes=[gb])
                for dc in range(8):
                    ps = PS.next()
                    for kc in range(kch):
                        mm(P, ps.ap[:, 0:n], wo[wn].ap[:, kc, dc * 128:(dc + 1) * 128], ob.ap[:, kc, 0:n],
                           kc == 0, kc == kch - 1, [wo[wn], ob], [ps])
                    sg = sgr.next()
                    act(P, sg.ap[:, 0:n], gb.ap[:, dc, 0:n], AF.Sigmoid, [gb], [sg])
                    if bi == 0:
                        tt(P, 'dve', merged.ap[:, dc, 0:n], ps.ap[:, 0:n], sg.ap[:, 0:n], ALU.mult, [ps, sg], [merged])
                    else:
                        tm = tmr.next()
                        tt(P, 'dve', tm.ap[:, 0:n], ps.ap[:, 0:n], sg.ap[:, 0:n], ALU.mult, [ps, sg], [tm])
                        if bi == 1:
                            tt(P, 'pool', merged.ap[:, dc, 0:n], merged.ap[:, dc, 0:n], tm.ap[:, 0:n], ALU.add,
                               [merged, tm], [merged])
                        else:
                            tt(P, 'pool', m16.ap[:, dc, 0:n], merged.ap[:, dc, 0:n], tm.ap[:, 0:n], ALU.add,
                               [merged, tm], [m16])
            hnew = hn.next()
            for dc in range(8):
                ps = PS.next()
                for kc in range(8):
                    mm(P, ps.ap[:, 0:n], wo['w_out'].ap[:, kc, dc * 128:(dc + 1) * 128], m16.ap[:, kc, 0:n],
                       kc == 0, kc == 7, [wo['w_out'], m16], [ps])
                tt(P, 'dve', hnew.ap[:, dc, 0:n], ps.ap[:, 0:n], ht.ap[:, dc, 0:n], ALU.add, [ps, ht], [hnew])
            dma(P, 'sp', D['hT'].rearrange('c p t -> p c t')[:, :, t0:t0 + n], hnew.ap[:, :, 0:n], reads=[hnew])

    def phase_ffn_up(l, xn, xnb, mark):
        P.barrier()
        A.top = mark
        w2d = D['ffn_w_up'][l]
        ABW = NPS * (SEQ + 2) + NSS * (DSEQ + 2)
        SOFF = NPS * (SEQ + 2)
        wr = Ring([A.alloc([8, 2, 128], BF16, 'wu%d' % i) for i in range(4)])
        abr = Ring([A.alloc([ABW], F32, 'ab%d' % i) for i in range(2)])
        bbr = Ring([A.alloc([NT], BF16, 'bb%d' % i) for i in range(2)])
        ybr = Ring([A.alloc([NT], F32, 'ybuf%d' % i) for i in range(2)])
        hsr = Ring([A.alloc([NT], BF16, 'hst%d' % i) for i in range(2)])
        halos = {id(t_): Buf('halo') for t_ in abr.tiles}
        for t_ in abr.tiles:
            ms(P, 'pool', t_.ap, 0.0, [t_, halos[id(t_)]])

        def conv_stages(j, ab, bb, ybuf):
            halo = halos[id(ab)]
            sv = ab.ap[:, SOFF:ABW].rearrange('p (b x) -> p b x', x=DSEQ + 2)
            pv = ab.ap[:, 0:SOFF].rearrange('p (s x) -> p s x', x=SEQ + 2)
            w0, w1, w2 = (vcol(V_CW + l * 66 + k * NFF + j) for k in range(3))
            cb = vcol(V_CB + l * NFF + j)
            segs = [(lambda sh: pv[:, :, sh:sh + SEQ], ybuf.ap[:, 0:NTP].rearrange('p (s t) -> p s t', t=SEQ)),
                    (lambda sh: sv[:, :, sh:sh + DSEQ], ybuf.ap[:, NTP:NT].rearrange('p (b i) -> p b i', i=DSEQ))]

            def st_a():
                for av, yv in segs:
                    act(P, yv, av(0), AF.Identity, [ab, halo, vecs], [ybuf], scale=w0, bias=cb)

            def st_b():
                for av, yv in segs:
                    stt(P, yv, av(1), w1, yv, ALU.mult, ALU.add, [ab, halo, ybuf, vecs], [ybuf])

            def st_c():
                for av, yv in segs:
                    stt(P, yv, av(2), w2, yv, ALU.mult, ALU.add, [ab, ybuf, vecs], [ybuf])
                for s_ in range(NPS):
                    base = s_ * (SEQ + 2)
                    dma(P, 'sp', D['convT'][l, j, :, s_, :], ab.ap[:, base + SEQ:base + SEQ + 2], reads=[ab])
                dma(P, 'sp', D['convT'][l, j, :, NPS:NPS + NSS, :], sv[:, :, DSEQ:DSEQ + 2], reads=[ab])

            def st_d():
                act(P, ybuf.ap, ybuf.ap, AF.Silu, [ybuf], [ybuf])

            def st_e():
                hs = hsr.next()
                tt(P, 'pool', hs.ap, ybuf.ap, bb.ap, ALU.mult, [ybuf, bb], [hs])
                dma(P, 'sp', D['HF'][j], hs.ap, reads=[hs])

            return {0: st_a, 2: st_b, 4: st_c, 6: st_d, 8: st_e}

        pending = {}
        wts = {}

        def issue_w(jj):
            if jj < NFF:
                wts[jj] = wr.next()
                for half in range(2):
                    src = w2d[:, half * DFF + jj * 128:half * DFF + (jj + 1) * 128].rearrange('(kc p) n -> p kc n', p=128)
                    dma(P, 'pool', wts[jj].ap[:, :, half, :], src, writes=[wts[jj]])

        issue_w(0)
        issue_w(1)
        for j in range(NFF + 1):
            if j < NFF:
                issue_w(j + 2)
                wt = wts[j]
                ab, bb = abr.next(), bbr.next()
                ybuf = ybr.next()
                sv = ab.ap[:, SOFF:ABW].rearrange('p (b x) -> p b x', x=DSEQ + 2)
                halo = halos[id(ab)]
                dma(P, 'sp', sv[:, :, 0:2], D['sconvT'][l, j], writes=[halo])
            for si, (t0, n) in enumerate(SUPER):
                if j < NFF:
                    psa, psb_ = PS.next(), PS.next()
                    for kc in range(8):
                        mm(P, psa.ap[:, 0:n], wt.ap[:, kc, 0, :], xn.ap[:, kc, t0:t0 + n], kc == 0, kc == 7,
                           [wt, xnb[si]], [psa])
                    for kc in range(8):
                        mm(P, psb_.ap[:, 0:n], wt.ap[:, kc, 1, :], xn.ap[:, kc, t0:t0 + n], kc == 0, kc == 7,
                           [wt, xnb[si]], [psb_])
                    if si < 8:
                        s_ = si // 4
                        c0 = s_ * (SEQ + 2) + 2 + (si % 4) * 512
                        cp(P, 'act', ab.ap[:, c0:c0 + n], psa.ap[:, 0:n], [psa], [ab])
                    else:
                        cp(P, 'act', sv[:, :, 2:2 + DSEQ], psa.ap[:, 0:n].rearrange('p (b i) -> p b i', i=DSEQ), [psa], [ab])
                    cp(P, 'dve', bb.ap[:, t0:t0 + n], psb_.ap[:, 0:n], [psb_], [bb])
                if si in pending:
                    pending[si]()
            pending = conv_stages(j, ab, bb, ybuf) if j < NFF else {}

    def phase_ffn_down(l):
        phase_reset(MARK0)
        wd = A.alloc([NFF, 1024], BF16, 'wd')
        wdb = [Buf('wd%d' % kc) for kc in range(NFF)]
        for kc in range(NFF):
            dma(P, 'pool', wd.ap[:, kc, :], D['ffn_w_down'][l, kc * 128:(kc + 1) * 128, :], writes=[wdb[kc]])
        hfr = Ring([A.alloc([NFF, 512], BF16, 'hf%d' % i) for i in range(2)])
        hr = Ring([A.alloc([8, 512], F32, 'h%d' % i) for i in range(2)])
        hn = Ring([A.alloc([8, 512], F32, 'hn%d' % i) for i in range(2)])
        hv = D['hT'].rearrange('c p t -> p c t')
        for si, (t0, n) in enumerate(SUPER):
            hf, ht, hnew = hfr.next(), hr.next(), hn.next()
            dma(P, 'sp', hf.ap[:, :, 0:n], D['HF'].rearrange('c p t -> p c t')[:, :, t0:t0 + n], writes=[hf])
            dma(P, 'sp', ht.ap[:, :, 0:n], hv[:, :, t0:t0 + n], writes=[ht])
            for dc in range(8):
                ps = PS.next()
                for kc in range(NFF):
                    mm(P, ps.ap[:, 0:n], wd.ap[:, kc, dc * 128:(dc + 1) * 128], hf.ap[:, kc, 0:n],
                       kc == 0, kc == NFF - 1, [wdb[kc], hf], [ps])
                tt(P, 'dve', hnew.ap[:, dc, 0:n], ps.ap[:, 0:n], ht.ap[:, dc, 0:n], ALU.add, [ps, ht], [hnew])
            dma(P, 'sp', hv[:, :, t0:t0 + n], hnew.ap[:, :, 0:n], reads=[hnew])

    def run_all():
        for l in range(DEPTH):
            src = D['xT'] if l == 0 else D['hT']
            xn, xnb, mark, xlo, mark_lo = phase_norm(src, V_N1 + l * 8, 'xn', need_lo=True)
            if stop_after == ('norm', l):
                if debug_scratch:
                    dma(P, 'sp', D['XN'].rearrange('c p t -> p c t'), xn.ap, reads=xnb)
                return
            phase_inproj_qk(l, xn, xnb, xlo, mark_lo)
            phase_inproj(l, xn, xnb, mark)
            if stop_after == ('inproj', l):
                return
            def run_group(mk):
                phase_reset(MARK0)
                gens = mk()
                while gens:
                    for g_ in list(gens):
                        try:
                            next(g_)
                        except StopIteration:
                            gens.remove(g_)

            def grp1():
                sh = gla_shared(l, 256, 128)
                return [gen_gla(l, s_ * SEQ, SEQ, 128, None, D['gla_p'][l, s_], sh) for s_ in range(NPS)]

            def grp2():
                sh = gla_shared(l, DSEQ, DSEQ)
                psh = pool_shared(l)
                gc_ = {}

                def sample_chain():
                    for b in range(NSS):
                        yield from gen_gla(l, NTP + b * DSEQ, DSEQ, DSEQ, D['state_gla'][l, b], D['gla_s'][l, b], sh,
                                           cache=gc_)
                return ([sample_chain()]
                        + [gen_pool(l, s_, s_ * SEQ, SEQ, None, psh) for s_ in range(NPS)]
                        + [gen_pool(l, NPS + b, NTP + b * DSEQ, DSEQ, b, psh) for b in range(NSS)])

            run_group(grp1)
            run_group(grp2)
            run_group(lambda: [gen_attn_prompt(l, 0), gen_attn_sample(l)])
            wo_pre = {}
            phase_reset(MARK0)
            wo_pre['wo'], wo_pre['mark'], issue_wo, todo_wo = load_outproj_weights(l, lazy=True)
            g5 = gen_attn_prompt(l, SEQ)
            step = 0
            for _ in g5:
                step += 1
                if step >= 4 and step % 3 == 0:
                    issue_wo(1)
            issue_wo(len(todo_wo))
            if stop_after == ('attn', l):
                return
            phase_outproj(l, src, wo_pre['wo'], wo_pre['mark'])
            if stop_after == ('outproj', l):
                return
            xn, xnb, mark, _, _ = phase_norm(D['hT'], V_N2 + l * 8, 'xn')
            phase_ffn_up(l, xn, xnb, mark)
            phase_ffn_down(l)
            if stop_after == ('ffn', l):
                return
        phase_norm(D['hT'], V_FN, 'final')

    run_all()
    P.barrier()
    P.emit(es)
    es.close()
    return nc


_NC_CACHE = {}


def _get_nc():
    if 'nc' not in _NC_CACHE:
        _NC_CACHE['nc'] = build_program()
    return _NC_CACHE['nc']


def make_in_maps(inp, n_cores=8):
    f = lambda a: np.ascontiguousarray(np.asarray(a, dtype=np.float32))
    consts = host_constants()
    vec = np.zeros((128, NV), np.float32)

    def put(col0, arr2d):
        a = f(arr2d).reshape(-1, 128)
        vec[:, col0:col0 + a.shape[0]] = a.T

    put(V_N1, inp['norm1_g'])
    put(V_N2, inp['norm2_g'])
    put(V_FN, inp['final_norm_g'])
    put(V_BA, inp['gla_ba'])
    put(V_GN, inp['gla_norm_g'])
    put(V_PS, inp['pool_scale'])
    put(V_CW, inp['ffn_conv_w'])
    put(V_CB, inp['ffn_conv_b'])
    shared = {k: f(inp[k]) for k in ('w_in', 'gla_wa2', 'pool_w', 'w_oa', 'w_ob', 'w_oc', 'w_out', 'ffn_w_up',
                                     'ffn_w_down')}
    shared['vecs'] = vec
    shared.update(consts)
    maps = []
    xp, xs = f(inp['x_prompt']), f(inp['x_sample'])
    for c in range(n_cores):
        m = dict(shared)
        xa = np.concatenate([xp[NPS * c:NPS * (c + 1)].reshape(NTP, DM), xs[NSS * c:NSS * (c + 1)].reshape(NSS * DSEQ, DM)], 0)
        m['xT'] = np.ascontiguousarray(xa.T).reshape(8, 128, NT)
        sl = slice(NSS * c, NSS * (c + 1))
        m['state_gla'] = f(inp['state_gla'][:, sl])
        m['spoolT'] = np.ascontiguousarray(f(inp['state_pool'][:, sl]).reshape(DEPTH, NSS, 15, 8, 128).transpose(0, 3, 4, 1, 2))
        m['sconvT'] = np.ascontiguousarray(f(inp['state_ffn_conv'][:, sl]).reshape(DEPTH, NSS, 2, NFF, 128).transpose(0, 3, 4, 1, 2))
        for gi, (win, dil) in enumerate(GROUPS):
            m['ck%d' % gi] = f(inp['cache_k_w%d' % win][:, sl]).reshape(DEPTH, NSS, win, 256)
            m['cv%d' % gi] = f(inp['cache_v_w%d' % win][:, sl]).reshape(DEPTH, NSS, win, 256)
        maps.append(m)
    return maps


def assemble(results):
    cat = lambda xs, ax: np.ascontiguousarray(np.concatenate(xs, axis=ax))
    yp, ys, poolp, pools, convp, convs = [], [], [], [], [], []
    for r in results:
        y = r['yT'].reshape(DM, NT).T
        yp.append(y[:NTP].reshape(NPS, SEQ, DM))
        ys.append(y[NTP:].reshape(NSS, DSEQ, DM))
        pl = r['poolT'].transpose(0, 3, 4, 1, 2).reshape(DEPTH, NPS + NSS, 15, DM)
        poolp.append(pl[:, :NPS])
        pools.append(pl[:, NPS:])
        cv = r['convT'].transpose(0, 3, 4, 1, 2).reshape(DEPTH, NPS + NSS, 2, DFF)
        convp.append(cv[:, :NPS])
        convs.append(cv[:, NPS:])
    out = [cat(yp, 0), cat(ys, 0),
           cat([r['gla_p'] for r in results], 1), cat([r['gla_s'] for r in results], 1),
           cat(poolp, 1), cat(pools, 1)]
    for gi, (win, dil) in enumerate(GROUPS):
        kp = cat([r['kT%d' % gi].transpose(0, 1, 4, 2, 3).reshape(DEPTH, NPS, win, 4, 64) for r in results], 1)
        ks = cat([r['k%ds' % gi].reshape(DEPTH, NSS, win, 4, 64) for r in results], 1)
        vp = cat([r['v%dp' % gi].reshape(DEPTH, NPS, win, 4, 64) for r in results], 1)
        vs = cat([r['v%ds' % gi].reshape(DEPTH, NSS, win, 4, 64) for r in results], 1)
        out += [kp, ks, vp, vs]
    out += [cat(convp, 1), cat(convs, 1)]
    return tuple(np.asarray(o, dtype=np.float32) for o in out)


def kernel(**inputs):
    nc = _get_nc()
    in_maps = make_in_maps(inputs)
    res = run_bass_kernel_spmd(nc, in_maps, core_ids=list(range(8)))
    return assemble(res.results)
```

```python
import numpy as np
from contextlib import ExitStack
import concourse.bass as bass
import concourse.mybir as mybir
from concourse.bass_utils import run_bass_kernel_spmd

F32 = mybir.dt.float32
BF16 = mybir.dt.bfloat16
AF = mybir.ActivationFunctionType
ALU = mybir.AluOpType

SAME_ENGINE_SYNC = True
PREFETCH_WO = True
SEM_CHUNK = 30000
N_DMA_SLOTS = {'sp': 8, 'pool': 6}


class Buf:
    __slots__ = ('w', 'r', 'name')

    def __init__(self, name=''):
        self.w = None
        self.r = {}
        self.name = name


class Prog:
    ENGS = ('pe', 'act', 'dve', 'pool', 'sp')

    def __init__(self, nc):
        self.nc = nc
        self.ops = {e: [] for e in self.ENGS}
        self.waited_c = {e: {} for e in self.ENGS}
        self.waited_d = {e: {} for e in self.ENGS}
        self.dma_n = {}
        self.dma_rr = {q: 0 for q in N_DMA_SLOTS}
        self.pending = {e: [] for e in self.ENGS}

    def barrier(self):
        evs = []
        for f in self.ENGS:
            if f in ('pe', 'act', 'dve', 'pool') or True:
                for i in range(len(self.ops[f]) - 1, -1, -1):
                    if self.ops[f][i]['dma'] is None:
                        evs.append(('c', f, i))
                        break
        for (q, k), n in self.dma_n.items():
            evs.append(('d', q, k, n))
        for e in self.ENGS:
            self.pending[e].extend(self._add_waits(e, evs, force_same=True))

    def _collect(self, reads, writes):
        deps = []
        for b in reads:
            if b.w is not None:
                deps.append(b.w)
        for b in writes:
            if b.w is not None:
                deps.append(b.w)
            deps.extend(b.r.values())
        return deps

    def _add_waits(self, eng, deps, is_pe_mm=False, force_same=False):
        waits = []
        for ev in deps:
            if ev[0] == 'c':
                _, src, idx = ev
                if src == eng and (force_same or not SAME_ENGINE_SYNC or eng == 'pe'):
                    continue
                if self.waited_c[eng].get(src, -1) >= idx:
                    continue
                self.waited_c[eng][src] = idx
                self.ops[src][idx]['sig'] = True
                waits.append(ev)
            else:
                _, q, k, n = ev
                if self.waited_d[eng].get((q, k), 0) >= n:
                    continue
                self.waited_d[eng][(q, k)] = n
                waits.append(ev)
        return waits

    def _publish(self, ev, key, reads, writes):
        for b in reads:
            b.r[key] = ev
        for b in writes:
            b.w = ev
            b.r = {}

    def op(self, eng, fn, reads=(), writes=(), pe_acc=False):
        deps = self._collect(reads, writes)
        waits = self.pending[eng] + self._add_waits(eng, deps, is_pe_mm=pe_acc)
        self.pending[eng] = []
        idx = len(self.ops[eng])
        self.ops[eng].append({'fn': fn, 'waits': waits, 'sig': False, 'dma': None})
        self._publish(('c', eng, idx), eng, reads, writes)

    def dma(self, q, out, in_, reads=(), writes=(), **kw):
        nslots = N_DMA_SLOTS[q]
        k = self.dma_rr[q]
        self.dma_rr[q] = (k + 1) % nslots
        n = self.dma_n.get((q, k), 0)
        deps = self._collect(reads, writes)
        if n > 0:
            deps.append(('d', q, k, n))
        waits = self.pending[q] + self._add_waits(q, deps)
        self.pending[q] = []
        self.dma_n[(q, k)] = n + 1
        self.ops[q].append({'fn': (lambda e, out=out, in_=in_, kw=kw: e.dma_start(out=out, in_=in_, **kw)),
                            'waits': waits, 'sig': False, 'dma': (q, k)})
        self._publish(('d', q, k, n + 1), ('d', q, k), reads, writes)

    def emit(self, es):
        nc = self.nc
        sig_rank = {}
        eng_sems = {}
        for e in self.ENGS:
            rank = 0
            for i, o in enumerate(self.ops[e]):
                if o['sig']:
                    sig_rank[(e, i)] = rank
                    rank += 1
            nsem = (rank + SEM_CHUNK - 1) // SEM_CHUNK
            eng_sems[e] = [es.enter_context(nc.semaphore('s_%s_%d' % (e, j))) for j in range(nsem)]
        dma_sems = {}
        for (q, k) in self.dma_n:
            dma_sems[(q, k)] = es.enter_context(nc.semaphore('d_%s_%d' % (q, k)))

        def resolve(ev):
            if ev[0] == 'c':
                r = sig_rank[(ev[1], ev[2])]
                return eng_sems[ev[1]][r // SEM_CHUNK], (r % SEM_CHUNK) + 1
            return dma_sems[(ev[1], ev[2])], 16 * ev[3]

        final_waits = [('d', q, k, n) for (q, k), n in self.dma_n.items()]

        def run(ename, e):
            for i, o in enumerate(self.ops[ename]):
                for ev in o['waits']:
                    s, v = resolve(ev)
                    e.wait_ge(s, v)
                ins = o['fn'](e)
                if o['dma'] is not None:
                    ins.then_inc(dma_sems[o['dma']], 16)
                elif o['sig']:
                    r = sig_rank[(ename, i)]
                    ins.then_inc(eng_sems[ename][r // SEM_CHUNK], 1)
            if ename == 'sp':
                for ev in final_waits:
                    s, v = resolve(ev)
                    e.wait_ge(s, v)

        with nc.Block() as block:
            @block.tensor
            def _(e):
                run('pe', e)

            @block.scalar
            def _(e):
                run('act', e)

            @block.vector
            def _(e):
                run('dve', e)

            @block.gpsimd
            def _(e):
                run('pool', e)

            @block.sync
            def _(e):
                run('sp', e)


class T:
    __slots__ = ('ap', 'b')

    def __init__(self, ap, name=''):
        self.ap = ap
        self.b = Buf(name)


class Arena:
    def __init__(self, tensor, words):
        self.t = tensor
        self.words = words
        self.top = 0

    def alloc(self, free, dt, name='', parts=128):
        n = 1
        for f in free:
            n *= f
        w = n if dt == F32 else (n + 1) // 2
        w = (w + 3) // 4 * 4
        off = self.top
        self.top += w
        assert self.top <= self.words, 'arena overflow %s %d > %d' % (name, self.top, self.words)
        nw = n if dt == F32 else (n + 1) // 2
        ap = self.t[0:parts, off:off + nw]
        if dt == BF16:
            ap = ap.bitcast(BF16)
            if n % 2:
                ap = ap[:, 0:n]
        if len(free) == 2:
            ap = ap.rearrange('p (a b) -> p a b', b=free[1])
        elif len(free) == 3:
            ap = ap.rearrange('p (a b c) -> p a b c', b=free[1], c=free[2])
        return T(ap, name)


def _bs(ts):
    return [t.b if isinstance(t, T) else t for t in ts]


def mm(P, out, lhsT, rhs, start, stop, reads, writes):
    P.op('pe', lambda e: e.matmul(out, lhsT=lhsT, rhs=rhs, start=start, stop=stop),
         reads=_bs(reads), writes=_bs(writes))


def tr(P, out, in_, ident, reads, writes):
    P.op('pe', lambda e: e.transpose(out, in_, ident), reads=_bs(reads), writes=_bs(writes))


def act(P, out, in_, func, reads, writes, scale=1.0, bias=None):
    if bias is None:
        P.op('act', lambda e: e.activation(out=out, in_=in_, func=func, scale=scale),
             reads=_bs(reads), writes=_bs(writes))
    else:
        P.op('act', lambda e: e.activation(out=out, in_=in_, func=func, scale=scale, bias=bias),
             reads=_bs(reads), writes=_bs(writes))


def tt(P, eng, out, in0, in1, op, reads, writes):
    P.op(eng, lambda e: e.tensor_tensor(out=out, in0=in0, in1=in1, op=op), reads=_bs(reads), writes=_bs(writes))


def ts(P, eng, out, in0, s1, s2, op0, op1, reads, writes):
    P.op(eng, lambda e: e.tensor_scalar(out=out, in0=in0, scalar1=s1, scalar2=s2, op0=op0, op1=op1),
         reads=_bs(reads), writes=_bs(writes))


def stt(P, out, in0, scalar, in1, op0, op1, reads, writes):
    P.op('dve', lambda e: e.scalar_tensor_tensor(out=out, in0=in0, scalar=scalar, in1=in1, op0=op0, op1=op1),
         reads=_bs(reads), writes=_bs(writes))


def cp(P, eng, out, in_, reads, writes):
    if eng == 'act':
        P.op('act', lambda e: e.copy(out=out, in_=in_), reads=_bs(reads), writes=_bs(writes))
    else:
        P.op(eng, lambda e: e.tensor_copy(out=out, in_=in_), reads=_bs(reads), writes=_bs(writes))


def ms(P, eng, ap, val, writes):
    P.op(eng, lambda e: e.memset(ap, val), writes=_bs(writes))


def dma(P, q, out, in_, reads=(), writes=()):
    P.dma(q, out, in_, reads=_bs(reads), writes=_bs(writes))


DEPTH = 2
DM = 1024
NPS = 2
SEQ = 2048
NSS = 4
DSEQ = 8
NTP = NPS * SEQ
NT = NTP + NSS * DSEQ
DFF = 2816
NFF = 22
INW = 9488
C_QG, C_KG, C_VG, C_RG, C_ALR, C_UP, C_QA, C_KA, C_VA, C_GA, C_GB, C_GC = (
    0, 512, 1024, 2048, 3072, 3088, 4112, 4880, 5648, 6416, 7440, 8464)
F_QG, F_KG, F_RG, F_ALR, F_QA, F_KA, F_GA, F_GB, F_GC, NFB = 0, 4, 8, 16, 17, 23, 29, 37, 45, 53
GROUPS = ((128, 1), (512, 4), (2048, 16))
SUPER = [(i * 512, 512) for i in range(8)] + [(NTP, NSS * DSEQ)]
V_N1, V_N2, V_FN, V_BA, V_GN, V_PS, V_CW, V_CB, NV = 0, 16, 32, 40, 48, 52, 68, 200, 244
ARENA_WORDS = 52736


def alibi_slopes():
    return (2.0 ** (-8.0 * np.arange(1, 13) / 12)).astype(np.float32)


def host_constants():
    sl = alibi_slopes()
    c = {}
    kl = np.arange(128)[:, None]
    col = np.arange(256)[None, :]
    j = col - kl
    bm = np.zeros((128, 12, 256), np.float32)
    for h in range(12):
        dil = GROUPS[h // 4][1]
        bm[:, h, :] = np.where((j >= 0) & (j <= 128), -sl[h] * (j * dil).astype(np.float32), -1e30)
    c['bm'] = bm
    for gi, (win, dil) in enumerate(GROUPS):
        nt_ = win // 128 + 1
        t = np.zeros((128, 4, nt_, 8), np.float32)
        p = np.arange(128)[:, None, None]
        ti = np.arange(nt_)[None, :, None]
        qi = np.arange(8)[None, None, :]
        cidx = np.where(ti < nt_ - 1, ti * 128 + p, win + p)
        okrow = np.where(ti < nt_ - 1, True, p < 8)
        dist = win + qi - cidx
        valid = okrow & (dist >= 0) & (dist % dil == 0) & (dist // dil <= win // dil)
        for hh in range(4):
            t[:, hh] = np.where(valid, -sl[gi * 4 + hh] * dist.astype(np.float32), -1e30)
        c['bms%d' % gi] = t
    mu = (np.arange(128)[:, None] <= np.arange(128)[None, :]).astype(np.float32)
    c['maskU'] = mu
    ic = np.zeros((128, 8, 16), np.float32)
    for cc in range(8):
        w = (2, 4, 8, 16)[cc // 2]
        ic[:, cc, :] = 1.0 / np.minimum(np.arange(16) + 1, w)
    c['invc'] = ic
    return c


class Ring:
    def __init__(self, tiles):
        self.tiles = tiles
        self.i = 0

    def next(self):
        t = self.tiles[self.i]
        self.i = (self.i + 1) % len(self.tiles)
        return t


def build_program(stop_after=None, debug_scratch=False, sub=None):
    nc = bass.Bass('TRN2', target_bir_lowering=False)
    D = {}

    def din(name, shape, dt=F32):
        D[name] = nc.dram_tensor(name, list(shape), dt, kind='ExternalInput').ap()

    def dout(name, shape, dt=F32):
        D[name] = nc.dram_tensor(name, list(shape), dt, kind='ExternalOutput').ap()

    def dscr(name, shape, dt):
        D[name] = nc.dram_tensor(name, list(shape), dt, kind='ExternalOutput' if debug_scratch else 'Internal').ap()

    din('xT', [8, 128, NT])
    din('state_gla', [DEPTH, NSS, 4, 128, 256])
    din('spoolT', [DEPTH, 8, 128, NSS, 15])
    din('sconvT', [DEPTH, NFF, 128, NSS, 2])
    for gi, (win, dil) in enumerate(GROUPS):
        din('ck%d' % gi, [DEPTH, NSS, win, 256])
        din('cv%d' % gi, [DEPTH, NSS, win, 256])
    din('w_in', [DEPTH, DM, INW])
    din('gla_wa2', [DEPTH, 16, 512])
    din('pool_w', [DEPTH, 4, 256, 256])
    din('w_oa', [DEPTH, 1024, DM])
    din('w_ob', [DEPTH, 1024, DM])
    din('w_oc', [DEPTH, 768, DM])
    din('w_out', [DEPTH, DM, DM])
    din('ffn_w_up', [DEPTH, DM, 2 * DFF])
    din('ffn_w_down', [DEPTH, DFF, DM])
    din('vecs', [128, NV])
    din('bm', [128, 12, 256])
    for gi, (win, dil) in enumerate(GROUPS):
        din('bms%d' % gi, [128, 4, win // 128 + 1, 8])
    din('maskU', [128, 128])
    din('invc', [128, 8, 16])
    dout('yT', [8, 128, NT])
    dout('gla_p', [DEPTH, NPS, 4, 128, 256])
    dout('gla_s', [DEPTH, NSS, 4, 128, 256])
    dout('poolT', [DEPTH, 8, 128, NPS + NSS, 15])
    dout('convT', [DEPTH, NFF, 128, NPS + NSS, 2])
    for gi, (win, dil) in enumerate(GROUPS):
        dout('kT%d' % gi, [DEPTH, NPS, 2, 128, win])
        dout('v%dp' % gi, [DEPTH, NPS, win, 256])
        dout('k%ds' % gi, [DEPTH, NSS, win, 256])
        dout('v%ds' % gi, [DEPTH, NSS, win, 256])
    dscr('hT', [8, 128, NT], F32)
    dscr('PF', [NFB, 128, NT], BF16)
    dscr('UP', [8, 128, NT], F32)
    dscr('VG', [NT, 1024], BF16)
    dscr('VA', [NT, 768], BF16)
    dscr('OG', [8, 128, NT], BF16)
    dscr('PM', [8, 128, NT], BF16)
    dscr('OC', [6, 128, NT], BF16)
    dscr('HF', [NFF, 128, NT], BF16)
    dscr('QK32', [8, 128, NT], F32)
    if debug_scratch:
        dscr('XN', [8, 128, NT], BF16)

    P = Prog(nc)
    es = ExitStack()
    arena_t = es.enter_context(nc.sbuf_tensor('arena', [128, ARENA_WORDS], F32))
    A = Arena(arena_t, ARENA_WORDS)
    pq = [es.enter_context(nc.psum_tensor('pq%d' % i, [128, 1024], F32)) for i in range(3)]
    psf = []
    for i in range(3):
        psf.append(T(pq[i][:, 0:512], 'ps%d' % (2 * i)))
        psf.append(T(pq[i][:, 512:1024], 'ps%d' % (2 * i + 1)))
    psf.append(T(es.enter_context(nc.psum_tensor('ps6', [128, 512], F32))[:], 'ps6'))
    psb = T(es.enter_context(nc.psum_tensor('psb', [128, 1024], BF16))[:], 'psb')
    PS = Ring(psf)

    vecs = A.alloc([NV], F32, 'vecs')
    ident16 = A.alloc([128], BF16, 'ident16')
    ones16 = A.alloc([128], BF16, 'ones16')
    onesZ = [A.alloc([128], BF16, 'onesZ0'), A.alloc([128], BF16, 'onesZ1')]
    nba = A.alloc([8], F32, 'nba')
    dma(P, 'sp', vecs.ap, D['vecs'], writes=[vecs])
    ms(P, 'pool', ident16.ap, 0.0, [ident16])
    P.op('pool', lambda e: e.affine_select(out=ident16.ap, in_=ident16.ap, pattern=[[-1, 128]],
                                           compare_op=ALU.not_equal, fill=1.0, base=0, channel_multiplier=1),
         reads=[ident16.b], writes=[ident16.b])
    ms(P, 'pool', ones16.ap, 1.0, [ones16])
    for j in range(2):
        ms(P, 'pool', onesZ[j].ap, 0.0, [onesZ[j]])
        ms(P, 'pool', onesZ[j].ap[:, j * 64:(j + 1) * 64], 1.0, [onesZ[j]])
    ts(P, 'dve', nba.ap, vecs.ap[:, V_BA:V_BA + 8], -1.0, None, ALU.mult, ALU.bypass, [vecs], [nba])
    MARK0 = A.top

    def vcol(c):
        return vecs.ap[:, c:c + 1]

    def phase_reset(mark):
        P.barrier()
        A.top = mark

    evac_rr = [0]

    def evac_eng():
        evac_rr[0] ^= 1
        return 'act' if evac_rr[0] else 'dve'

    def phase_norm(src, gcol, mode, need_lo=False):
        phase_reset(MARK0)
        xn = None
        xnb = None
        xlo = None
        if mode == 'xn':
            xn = A.alloc([8, NT], BF16, 'xn')
            xnb = [Buf('xn%d' % i) for i in range(len(SUPER))]
        mark_xn = A.top
        if need_lo:
            xlo = A.alloc([8, NT], BF16, 'xlo')
            xfr = Ring([A.alloc([512], F32, 'xf%d' % i) for i in range(4)])
        mark_lo = A.top
        hsr = Ring([A.alloc([8, 512], F32, 'hs%d' % i) for i in range(2)])
        sqr = Ring([A.alloc([8, 512], BF16, 'sq%d' % i) for i in range(2)])
        lnr = Ring([A.alloc([512], F32, 'ln%d' % i) for i in range(2)])
        rsr = Ring([A.alloc([512], F32, 'rs%d' % i) for i in range(2)])
        ysr = Ring([A.alloc([8, 512], F32, 'ys%d' % i) for i in range(2)]) if mode == 'final' else None
        srcv = src.rearrange('c p t -> p c t')
        for si, (t0, n) in enumerate(SUPER):
            hs = hsr.next()
            dma(P, 'sp', hs.ap[:, :, 0:n], srcv[:, :, t0:t0 + n], writes=[hs])
            sq = sqr.next()
            act(P, sq.ap[:, :, 0:n], hs.ap[:, :, 0:n], AF.Square, [hs], [sq])
            ps = PS.next()
            for c in range(8):
                mm(P, ps.ap[:, 0:n], ones16.ap, sq.ap[:, c, 0:n], c == 0, c == 7, [ones16, sq], [ps])
            ln = lnr.next()
            act(P, ln.ap[:, 0:n], ps.ap[:, 0:n], AF.Ln, [ps], [ln], scale=1.0 / DM, bias=1e-6)
            rs = rsr.next()
            act(P, rs.ap[:, 0:n], ln.ap[:, 0:n], AF.Exp, [ln], [rs], scale=-0.5)
            if mode == 'xn' and need_lo:
                for c in range(8):
                    xf = xfr.next()
                    stt(P, xf.ap[:, 0:n], hs.ap[:, c, 0:n], vcol(gcol + c), rs.ap[:, 0:n],
                        ALU.mult, ALU.mult, [hs, rs, vecs], [xf])
                    cp(P, 'act', xn.ap[:, c, t0:t0 + n], xf.ap[:, 0:n], [xf], [xnb[si]])
                    tt(P, 'pool' if c % 2 == 0 else 'dve', xlo.ap[:, c, t0:t0 + n], xf.ap[:, 0:n],
                       xn.ap[:, c, t0:t0 + n], ALU.subtract, [xf, xnb[si]], [xnb[si]])
            elif mode == 'xn':
                for c in range(8):
                    stt(P, xn.ap[:, c, t0:t0 + n], hs.ap[:, c, 0:n], vcol(gcol + c), rs.ap[:, 0:n],
                        ALU.mult, ALU.mult, [hs, rs, vecs], [xnb[si]])
            else:
                ys = ysr.next()
                for c in range(8):
                    stt(P, ys.ap[:, c, 0:n], hs.ap[:, c, 0:n], vcol(gcol + c), rs.ap[:, 0:n],
                        ALU.mult, ALU.mult, [hs, rs, vecs], [ys])
                dma(P, 'sp', D['yT'].rearrange('c p t -> p c t')[:, :, t0:t0 + n], ys.ap[:, :, 0:n], reads=[ys])
        return xn, xnb, mark_xn, xlo, mark_lo

    def load_w(wt, w2d, col0, ncols, kch):
        src = w2d[:, col0:col0 + ncols].rearrange('(kc p) n -> p kc n', p=128)
        dma(P, 'pool', wt.ap[:, 0:kch, 0:ncols], src, writes=[wt])

    def phase_inproj_qk(l, xn, xnb, xlo, mark_lo):
        P.barrier()
        A.top = mark_lo
        w2d = D['w_in'][l]
        whr = Ring([A.alloc([8, 512], BF16, 'whi%d' % i) for i in range(2)])
        w32 = A.alloc([8, 512], F32, 'w32')
        wlo = A.alloc([8, 512], BF16, 'wlo')
        st32 = Ring([A.alloc([512], F32, 'st32_%d' % i) for i in range(4)])
        for blk in range(2):
            whi = whr.next()
            load_w(whi, w2d, C_QG + blk * 512, 512, 8)
            dma(P, 'sp', w32.ap, w2d[:, C_QG + blk * 512:C_QG + (blk + 1) * 512].rearrange('(kc p) n -> p kc n', p=128),
                writes=[w32])
            tt(P, 'pool', wlo.ap, w32.ap, whi.ap, ALU.subtract, [w32, whi], [wlo])
            for x in range(4):
                cidx = blk * 4 + x
                for si, (t0, n) in enumerate(SUPER):
                    ps = PS.next()
                    k_ = 0
                    for wt_, xt_ in ((whi, xn), (whi, xlo), (wlo, xn)):
                        for kc in range(8):
                            mm(P, ps.ap[:, 0:n], wt_.ap[:, kc, x * 128:(x + 1) * 128], xt_.ap[:, kc, t0:t0 + n],
                               k_ == 0, k_ == 23, [wt_, xnb[si]], [ps])
                            k_ += 1
                    sf = st32.next()
                    cp(P, evac_eng(), sf.ap[:, 0:n], ps.ap[:, 0:n], [ps], [sf])
                    dma(P, 'sp', D['QK32'][cidx, :, t0:t0 + n], sf.ap[:, 0:n], reads=[sf])

    def phase_inproj(l, xn, xnb, mark):
        P.barrier()
        A.top = mark
        w2d = D['w_in'][l]
        wr = Ring([A.alloc([8, 512], BF16, 'wb%d' % i) for i in range(3)])
        stg = Ring([A.alloc([NT], BF16, 'stg%d' % i) for i in range(3)])
        stf = Ring([A.alloc([NT], F32, 'stf%d' % i) for i in range(2)])
        tst = Ring([A.alloc([512], BF16, 'tst%d' % i) for i in range(3)])
        tsf = Ring([A.alloc([512], F32, 'tsf%d' % i) for i in range(3)])

        def fm_group(col0, nch, kind, fbase, Mlast=128):
            ci = 0
            while ci < nch:
                nb = min(4, nch - ci)
                ncols = sum(128 if (ci + x) < nch - 1 else Mlast for x in range(nb))
                wt = wr.next()
                load_w(wt, w2d, col0 + ci * 128, ncols, 8)
                for x in range(nb):
                    cidx = ci + x
                    M = 128 if cidx < nch - 1 else Mlast
                    sb = stg.next() if kind in ('pf', 'ka') else None
                    sf = stf.next() if kind in ('up', 'ka') else None
                    for si, (t0, n) in enumerate(SUPER):
                        ps = PS.next()
                        for kc in range(8):
                            mm(P, ps.ap[0:M, 0:n], wt.ap[:, kc, x * 128:x * 128 + M], xn.ap[:, kc, t0:t0 + n],
                               kc == 0, kc == 7, [wt, xnb[si]], [ps])
                        if sf is not None:
                            cp(P, evac_eng(), sf.ap[0:M, t0:t0 + n], ps.ap[0:M, 0:n], [ps], [sf])
                            if sb is not None:
                                cp(P, 'pool', sb.ap[0:M, t0:t0 + n], sf.ap[0:M, t0:t0 + n], [sf], [sb])
                        elif sb is not None:
                            cp(P, evac_eng(), sb.ap[0:M, t0:t0 + n], ps.ap[0:M, 0:n], [ps], [sb])
                    if kind in ('pf', 'ka'):
                        dma(P, 'sp', D['PF'][fbase + cidx, 0:M, :], sb.ap[0:M, :], reads=[sb])
                    if kind == 'up':
                        dma(P, 'sp', D['UP'][cidx], sf.ap, reads=[sf])
                    if kind == 'ka':
                        g, hp = cidx // 2, cidx % 2
                        keep = GROUPS[g][0]
                        for s in range(NPS):
                            dma(P, 'sp', D['kT%d' % g][l, s, hp], sf.ap[:, (s + 1) * SEQ - keep:(s + 1) * SEQ],
                                reads=[sf])
                ci += nb

        def on(k):
            return sub is None or k in sub
        if on('rg'):
            fm_group(C_RG, 8, 'pf', F_RG)
        if on('alr'):
            fm_group(C_ALR, 1, 'pf', F_ALR)
        if on('up'):
            fm_group(C_UP, 8, 'up', 0)
        if on('qa'):
            fm_group(C_QA, 6, 'pf', F_QA)
        if on('ka'):
            fm_group(C_KA, 6, 'ka', F_KA)
        if on('ga'):
            fm_group(C_GA, 24, 'pf', F_GA)

        tiles = [(i * 128, 128) for i in range(NTP // 128)] + [(NTP, NSS * DSEQ)]

        def tm_group(col0, ncols_tot, kind):
            c = 0
            while c < ncols_tot:
                ncols = min(512, ncols_tot - c)
                wt = wr.next()
                load_w(wt, w2d, col0 + c, ncols, 8)
                for ti, (t0, n) in enumerate(tiles):
                    if kind == 'ks' and ti < len(tiles) - 1:
                        continue
                    si = min(t0 // 512, len(SUPER) - 1)
                    ps = PS.next()
                    for kc in range(8):
                        mm(P, ps.ap[0:n, 0:ncols], xn.ap[:, kc, t0:t0 + n], wt.ap[:, kc, 0:ncols],
                           kc == 0, kc == 7, [wt, xnb[si]], [ps])
                    if kind in ('vg', 'va'):
                        tb = tst.next()
                        e1 = evac_eng()
                        cp(P, e1, tb.ap[0:n, 0:ncols], ps.ap[0:n, 0:ncols], [ps], [tb])
                        dst = D['VG'] if kind == 'vg' else D['VA']
                        dma(P, 'sp', dst[t0:t0 + n, c:c + ncols], tb.ap[0:n, 0:ncols], reads=[tb])
                    if kind in ('va', 'ks'):
                        sample = ti == len(tiles) - 1
                        need = []
                        for g in range(3):
                            gc0 = g * 256
                            if not (c <= gc0 < c + ncols):
                                continue
                            keep = GROUPS[g][0]
                            if sample:
                                need.append((g, gc0 - c))
                            else:
                                lt = (t0 % SEQ) // 128
                                if lt * 128 >= SEQ - keep:
                                    need.append((g, gc0 - c))
                        if need:
                            tf = tsf.next()
                            cp(P, e1 if kind == 'va' else evac_eng(), tf.ap[0:n, 0:ncols], ps.ap[0:n, 0:ncols], [ps], [tf])
                            for g, off in need:
                                keep = GROUPS[g][0]
                                if sample:
                                    nm = ('v%ds' if kind == 'va' else 'k%ds') % g
                                    for b in range(NSS):
                                        dma(P, 'sp', D[nm][l, b, keep - DSEQ:keep, :],
                                            tf.ap[b * DSEQ:(b + 1) * DSEQ, off:off + 256], reads=[tf])
                                else:
                                    s = t0 // SEQ
                                    r0 = (t0 % SEQ) - (SEQ - keep)
                                    dma(P, 'sp', D['v%dp' % g][l, s, r0:r0 + 128, :], tf.ap[0:128, off:off + 256],
                                        reads=[tf])
                c += ncols

        if on('vg'):
            tm_group(C_VG, 1024, 'vg')
        if on('va'):
            tm_group(C_VA, 768, 'va')
        if on('ks'):
            tm_group(C_KA, 768, 'ks')

    def gla_shared(l, PT, C):
        wa2 = A.alloc([512], BF16, 'wa2')
        ms(P, 'pool', wa2.ap, 0.0, [wa2])
        dma(P, 'pool', wa2.ap[0:16, :], D['gla_wa2'][l], writes=[wa2])
        maskU4 = A.alloc([4, 128], F32, 'maskU4')
        for h in range(4):
            dma(P, 'sp', maskU4.ap[:, h, :], D['maskU'], writes=[maskU4])
        rm = A.alloc([4 * PT], F32, 'resetm')
        ms(P, 'pool', rm.ap, 1.0, [rm])
        ms(P, 'pool', rm.ap.rearrange('p (a c) -> p a c', c=C)[:, :, 0:1], 0.0, [rm])
        Gt = A.alloc([8, C], F32, 'Gt')
        ms(P, 'pool', Gt.ap, 1.0, [Gt])
        for hv in range(8):
            ts(P, 'pool', Gt.ap[:, hv, :], Gt.ap[:, hv, :], vcol(V_GN + l * 2 + hv % 2), None, ALU.mult, ALU.bypass,
               [Gt, vecs], [Gt])
        return wa2, maskU4, rm, Gt

    def gen_gla(l, tok0, T, C, s0_src, s_out, shared, PTMAX=256, cache=None):
        def AL(free, dt, name, parts=128):
            if cache is None:
                return A.alloc(free, dt, name, parts)
            k_ = (name, tuple(free), str(dt))
            if k_ not in cache:
                cache[k_] = A.alloc(free, dt, name, parts)
            return cache[k_]
        PT = min(T, PTMAX)
        npiece = T // PT
        nch = PT // C
        DKS = 128.0 ** -0.5
        wa2, maskU4, rm, Gt = shared
        S = AL([4, 256], F32, 'S')
        Sb = AL([4, 256], BF16, 'Sb')
        if s0_src is None:
            ms(P, 'pool', S.ap, 0.0, [S])
        else:
            dma(P, 'sp', S.ap, s0_src.rearrange('h k v -> k h v'), writes=[S])
        cp(P, 'pool', Sb.ap, S.ap, [S], [Sb])
        qr = Ring([AL([4, PT], F32, 'q%d' % i) for i in range(1)])
        kr = Ring([AL([4, PT], F32, 'k%d' % i) for i in range(1)])
        rr = Ring([AL([8, PT], BF16, 'r%d' % i) for i in range(1)])
        vr = Ring([AL([nch, 1024], BF16, 'v%d' % i) for i in range(2)])
        yield
        ar = Ring([AL([PT], BF16, 'alr%d' % i) for i in range(2)])
        lbuf = AL([4, PT], F32, 'lbuf')
        cbuf = AL([4, PT], F32, 'cbuf')
        ein = AL([4, PT], F32, 'ein')
        eq = AL([4, PT], F32, 'eq')
        ek = AL([4, PT], F32, 'ek')
        est = AL([4, PT], BF16, 'est')
        if C == 128:
            qs, ks = eq, ek
        else:
            qs = AL([4, PT], BF16, 'qs')
            ks = AL([4, PT], BF16, 'ks')
        qin = AL([4, PT], BF16, 'qin')
        kstT = AL([4, PT], BF16, 'kstT')
        attr = Ring([AL([4, C], BF16, 'attm%d' % i) for i in range(2)])
        kstr = Ring([AL([4, 128], BF16, 'kst%d' % i) for i in range(2)])
        sqg = AL([8, C], BF16, 'sqg')
        lnb = AL([4, C], F32, 'lnb')
        rstd = AL([4, C], F32, 'rstd')
        sr = AL([8, C], F32, 'sr')
        srg = AL([8, C], F32, 'srg')
        t1 = AL([8, C], F32, 't1')
        ogr = Ring([AL([8, PT], BF16, 'ogst%d' % i) for i in range(2)])
        r_ref = (C - 1) // 2
        PFv = D['PF']
        for pi in range(npiece):
            p0 = tok0 + pi * PT
            qT, kT, rT, vt, alr = qr.next(), kr.next(), rr.next(), vr.next(), ar.next()
            dma(P, 'sp', qT.ap, D['QK32'][0:4].rearrange('h p t -> p h t')[:, :, p0:p0 + PT], writes=[qT])
            dma(P, 'sp', kT.ap, D['QK32'][4:8].rearrange('h p t -> p h t')[:, :, p0:p0 + PT], writes=[kT])
            dma(P, 'sp', rT.ap, PFv[F_RG:F_RG + 8].rearrange('h p t -> p h t')[:, :, p0:p0 + PT], writes=[rT])
            dma(P, 'sp', alr.ap, PFv[F_ALR, :, p0:p0 + PT], writes=[alr])
            if C == 128:
                dma(P, 'sp', vt.ap, D['VG'][p0:p0 + PT, :].rearrange('(a p) c -> p a c', p=128), writes=[vt])
            else:
                dma(P, 'sp', vt.ap[0:C, 0, :], D['VG'][p0:p0 + PT, :], writes=[vt])
            for h in range(4):
                ps = PS.next()
                mm(P, ps.ap[:, 0:PT], wa2.ap[:, h * 128:(h + 1) * 128], alr.ap[:, 0:PT], True, True,
                   [wa2, alr], [ps])
                act(P, lbuf.ap[:, h, :], ps.ap[:, 0:PT], AF.Exp, [ps, nba], [lbuf], scale=-1.0,
                    bias=nba.ap[:, l * 4 + h:l * 4 + h + 1])
            act(P, lbuf.ap, lbuf.ap, AF.Ln, [lbuf], [lbuf], scale=1.0, bias=1.0)
            lf = lbuf.ap.rearrange('p h t -> p (h t)')
            cf = cbuf.ap.rearrange('p h t -> p (h t)')
            P.op('dve', lambda e, cf=cf, lf=lf: e.tensor_tensor_scan(out=cf, data0=rm.ap, data1=lf, initial=0.0,
                                                                   op0=ALU.mult, op1=ALU.add),
                 reads=[rm.b, lbuf.b], writes=[cbuf.b])
            cv = cf.rearrange('p (a c) -> p a c', c=C)
            lv = lf.rearrange('p (a c) -> p a c', c=C)
            na = 4 * nch
            tt(P, 'dve', lv, cv, cv[:, :, r_ref:r_ref + 1].to_broadcast([128, na, C]), ALU.subtract, [cbuf], [lbuf])
            act(P, eq.ap, lbuf.ap, AF.Exp, [lbuf], [eq], scale=-1.0 / 16)
            act(P, ek.ap, lbuf.ap, AF.Exp, [lbuf], [ek], scale=1.0 / 16)
            act(P, ein.ap, cbuf.ap, AF.Exp, [cbuf], [ein], scale=-1.0 / 16)
            tt(P, 'dve', lv, cv, cv[:, :, C - 1:C].to_broadcast([128, na, C]), ALU.subtract, [cbuf, eq, ek], [lbuf])
            act(P, est.ap, lbuf.ap, AF.Exp, [lbuf], [est], scale=1.0 / 16)
            stt(P, qs.ap, qT.ap, DKS, eq.ap, ALU.mult, ALU.mult, [qT, eq], [qs, eq])
            tt(P, 'pool', ks.ap, kT.ap, ek.ap, ALU.mult, [kT, ek], [ks, ek])
            stt(P, qin.ap, qT.ap, DKS, ein.ap, ALU.mult, ALU.mult, [qT, ein], [qin])
            tt(P, 'pool', kstT.ap, kT.ap, est.ap, ALU.mult, [kT, est], [kstT])
            ogst = ogr.next()
            yield
            for ci in range(nch):
                cs = ci * C
                psa = PS.next()
                for h in range(4):
                    mm(P, psa.ap[0:C, h * 128:h * 128 + C], ks.ap[:, h, cs:cs + C], qs.ap[:, h, cs:cs + C], True, True,
                       [ks, qs], [psa])
                attm = attr.next()
                tt(P, 'dve', attm.ap[0:C, :, :], psa.ap[0:C, :].rearrange('p (h t) -> p h t', t=128)[:, :, 0:C],
                   maskU4.ap[0:C, :, 0:C], ALU.mult, [psa, maskU4], [attm])
                for h in range(4):
                    tr(P, psb.ap[0:C, h * 128:(h + 1) * 128], kstT.ap[:, h, cs:cs + C], ident16.ap, [kstT, ident16], [psb])
                kst = kstr.next()
                cp(P, 'act', kst.ap[0:C, :, :], psb.ap[0:C, 0:512].rearrange('p (h k) -> p h k', k=128), [psb], [kst])
                if C == 128:
                    yield
                pso = [PS.next(), PS.next()]
                for h in range(4):
                    for vc in range(2):
                        off = ((h % 2) * 2 + vc) * 128
                        o_ap = pso[h // 2].ap[:, off:off + C]
                        mm(P, o_ap, vt.ap[0:C, ci, h * 256 + vc * 128:h * 256 + (vc + 1) * 128], attm.ap[0:C, h, :],
                           True, False, [vt, attm], [pso[h // 2]])
                        mm(P, o_ap, Sb.ap[:, h, vc * 128:(vc + 1) * 128], qin.ap[:, h, cs:cs + C],
                           False, True, [Sb, qin], [pso[h // 2]])
                if C == 128:
                    yield
                for b in range(2):
                    act(P, sqg.ap[:, b * 4:(b + 1) * 4, :],
                        pso[b].ap.rearrange('p (x t) -> p x t', t=128)[:, :, 0:C], AF.Square, [pso[b]], [sqg])
                pss = PS.next()
                for h in range(4):
                    for vc in range(2):
                        mm(P, pss.ap[:, h * 128:h * 128 + C], ones16.ap, sqg.ap[:, h * 2 + vc, :], vc == 0, vc == 1,
                           [ones16, sqg], [pss])
                if C == 128:
                    yield
                act(P, lnb.ap, pss.ap.rearrange('p (h t) -> p h t', t=128)[:, :, 0:C], AF.Ln, [pss], [lnb],
                    scale=1.0 / 256, bias=1e-6)
                act(P, rstd.ap, lnb.ap, AF.Exp, [lnb], [rstd], scale=-0.5)
                act(P, sr.ap, rT.ap[:, :, cs:cs + C], AF.Silu, [rT], [sr])
                tt(P, 'pool', srg.ap, sr.ap, Gt.ap, ALU.mult, [sr, Gt], [srg])
                for h in range(4):
                    off = (h % 2) * 256
                    tt(P, 'dve', t1.ap[:, 2 * h:2 * h + 2, :],
                       pso[h // 2].ap[:, off:off + 256].rearrange('p (x t) -> p x t', t=128)[:, :, 0:C],
                       rstd.ap[:, h:h + 1, :].to_broadcast([128, 2, C]), ALU.mult, [pso[h // 2], rstd], [t1])
                tt(P, 'pool', ogst.ap[:, :, cs:cs + C], t1.ap, srg.ap, ALU.mult, [t1, srg], [ogst])
                if C == 128:
                    yield
                psk = [PS.next(), PS.next()]
                for h in range(4):
                    mm(P, psk[h // 2].ap[:, (h % 2) * 256:(h % 2 + 1) * 256], kst.ap[0:C, h, :],
                       vt.ap[0:C, ci, h * 256:(h + 1) * 256], True, True, [kst, vt], [psk[h // 2]])
                for h in range(4):
                    stt(P, S.ap[:, h, :], S.ap[:, h, :], ein.ap[:, h, cs + C - 1:cs + C],
                        psk[h // 2].ap[:, (h % 2) * 256:(h % 2 + 1) * 256], ALU.mult, ALU.add,
                        [S, ein, psk[h // 2]], [S])
                cp(P, 'pool', Sb.ap, S.ap, [S], [Sb])
                if C == 128:
                    yield
            dma(P, 'sp', D['OG'].rearrange('c p t -> p c t')[:, :, p0:p0 + PT], ogst.ap, reads=[ogst])
        dma(P, 'sp', s_out.rearrange('h k v -> k h v'), S.ap, reads=[S])

    def pool_shared(l):
        pw = A.alloc([4, 2, 256], BF16, 'pw')
        dma(P, 'pool', pw.ap, D['pool_w'][l].rearrange('g (kc p) d -> p g kc d', p=128), writes=[pw])
        invc = A.alloc([8, 16], F32, 'invc')
        dma(P, 'sp', invc.ap, D['invc'], writes=[invc])
        return pw, invc

    def gen_pool(l, seqidx, tok0, T, sample_b, shared):
        PT = min(T, 512)
        pw, invc = shared
        pooled = A.alloc([8, T], BF16, 'pooled')
        pmr = Ring([A.alloc([PT], BF16, 'pmst%d' % i) for i in range(3)])
        ur = Ring([A.alloc([15 + T], F32, 'u%d' % i) for i in range(2)])
        sr_ = Ring([A.alloc([15 + T], F32, 'ws%d' % i) for i in range(2)])
        tmpf = Ring([A.alloc([16], F32, 'tmpf%d' % i) for i in range(2)])
        for cc in range(8):
            w = (2, 4, 8, 16)[cc // 2]
            u = ur.next()
            if sample_b is None:
                ms(P, 'pool', u.ap[:, 0:15], 0.0, [u])
            else:
                dma(P, 'sp', u.ap[:, 0:15], D['spoolT'][l, cc, :, sample_b, :], writes=[u])
            dma(P, 'sp', u.ap[:, 15:15 + T], D['UP'][cc, :, tok0:tok0 + T], writes=[u])
            cur = u
            k = 1
            lo = 0
            while k < w:
                nx = sr_.next()
                lo2 = lo + k
                tt(P, 'pool' if cc in (1, 7) else 'dve', nx.ap[:, lo2:15 + T], cur.ap[:, lo2:15 + T],
                   cur.ap[:, lo2 - k:15 + T - k], ALU.add, [cur], [nx])
                cur = nx
                lo = lo2
                k *= 2
            stt(P, pooled.ap[:, cc, :], cur.ap[:, 15:15 + T], 1.0 / w, u.ap[:, 15:15 + T], ALU.mult, ALU.subtract,
                [cur, u], [pooled])
            if sample_b is None:
                tf = tmpf.next()
                tt(P, 'dve', tf.ap, cur.ap[:, 15:31], invc.ap[:, cc, :], ALU.mult, [cur, invc], [tf])
                tt(P, 'dve', pooled.ap[:, cc, 0:16], tf.ap, u.ap[:, 15:31], ALU.subtract, [tf, u], [pooled])
            dma(P, 'sp', D['poolT'][l, cc, :, seqidx, :], u.ap[:, T:T + 15], reads=[u])
            yield
        for dc in range(8):
            g, half = dc // 2, dc % 2
            for t in range(0, T, PT):
                ps = PS.next()
                for kc in range(2):
                    mm(P, ps.ap[:, 0:PT], pw.ap[:, g, kc, half * 128:(half + 1) * 128],
                       pooled.ap[:, g * 2 + kc, t:t + PT], kc == 0, kc == 1, [pw, pooled], [ps])
                pm = pmr.next()
                act(P, pm.ap[:, 0:PT], ps.ap[:, 0:PT], AF.Copy, [ps, vecs], [pm],
                    scale=vcol(V_PS + l * 8 + dc))
                dma(P, 'sp', D['PM'][dc, :, tok0 + t:tok0 + t + PT], pm.ap[:, 0:PT], reads=[pm])
            yield

    def gen_attn_prompt(l, tok0):
        T = SEQ
        bm = A.alloc([12, 256], F32, 'bm')
        dma(P, 'sp', bm.ap, D['bm'], writes=[bm])
        qN = A.alloc([2, T], BF16, 'qN')
        kN = A.alloc([2, T], BF16, 'kN')
        qPr = Ring([A.alloc([2, T], BF16, 'qP%d' % i) for i in range(2)])
        kPr = Ring([A.alloc([2, T], BF16, 'kP%d' % i) for i in range(2)])
        vP = A.alloc([16, 256], BF16, 'vP')
        vZ = [A.alloc([16, 2, 128], BF16, 'vZ%d' % j) for j in range(2)]
        Oacc = A.alloc([6, T], BF16, 'Oacc')
        Dsum = A.alloc([2, T], F32, 'Dsum')
        rden = Dsum
        tmpr = Ring([A.alloc([2, 256], F32, 'sc%d' % i) for i in range(3)])
        ptr_ = Ring([A.alloc([2, 256], BF16, 'pT%d' % i) for i in range(5)])
        for j in range(2):
            ms(P, 'pool', vZ[j].ap, 0.0, [vZ[j]])
        pair_rr = [0]
        psb32 = bm.__class__(psb.ap.bitcast(F32), 'psb32')
        psb32.b = psb.b
        PSO = Ring(psf[4:6])
        PSD = Ring([psf[6], psb32])
        for g, (win, dil) in enumerate(GROUPS):
            nb = T // dil // 128
            dma(P, 'sp', qN.ap, D['PF'][F_QA + 2 * g:F_QA + 2 * g + 2].rearrange('h p t -> p h t')[:, :, tok0:tok0 + T],
                writes=[qN])
            dma(P, 'sp', kN.ap, D['PF'][F_KA + 2 * g:F_KA + 2 * g + 2].rearrange('h p t -> p h t')[:, :, tok0:tok0 + T],
                writes=[kN])
            vsrc = D['VA'][tok0:tok0 + T, g * 256:(g + 1) * 256].rearrange('(b p r) c -> p r b c', p=128, r=dil)
            for r in range(dil):
                dma(P, 'sp', vP.ap[:, r * nb:(r + 1) * nb, :], vsrc[:, r, :, :], writes=[vP])
            qU, kU = qPr.next(), kPr.next()
            if dil == 1:
                cp(P, 'pool', qU.ap, qN.ap, [qN], [qU])
                cp(P, 'pool', kU.ap, kN.ap, [kN], [kU])
            else:
                cp(P, 'pool', qU.ap.rearrange('p h (r i) -> p h r i', r=dil),
                   qN.ap.rearrange('p h (i r) -> p h r i', r=dil), [qN], [qU])
                cp(P, 'pool', kU.ap.rearrange('p h (r i) -> p h r i', r=dil),
                   kN.ap.rearrange('p h (i r) -> p h r i', r=dil), [kN], [kU])
            vPv = vP.ap.rearrange('p t (hp x) -> p t hp x', x=128)
            cp(P, 'dve', vZ[0].ap[:, :, :, 0:64], vPv[:, :, :, 0:64], [vP], [vZ[0]])
            cp(P, 'dve', vZ[1].ap[:, :, :, 64:128], vPv[:, :, :, 64:128], [vP], [vZ[1]])
            for hp in range(2):
                Ov = Oacc.ap[:, g * 2 + hp, :].rearrange('p (b q r) -> p b q r', q=128, r=dil)
                Dv = Dsum.ap[:, hp, :].rearrange('p (b q r) -> p b q r', q=128, r=dil)
                def scores(r, b):
                    kt = r * nb + b
                    nq = 256 if b < nb - 1 else 128
                    h0 = g * 4 + hp * 2
                    pk = pair_rr[0]
                    pair_rr[0] = (pk + 1) % 2
                    pbanks = [psf[2 * pk], psf[2 * pk + 1]]
                    for j in range(2):
                        mm(P, pbanks[j].ap[:, 0:nq], kU.ap[j * 64:(j + 1) * 64, hp, kt * 128:(kt + 1) * 128],
                           qU.ap[j * 64:(j + 1) * 64, hp, kt * 128:kt * 128 + nq], True, True, [kU, qU], [pbanks[j]])
                    sc = tmpr.next()
                    stt(P, sc.ap[:, :, 0:nq], pq[pk][:, :].rearrange('p (j q) -> p j q', q=512)[:, :, 0:nq], 0.125,
                        bm.ap[:, h0:h0 + 2, 0:nq], ALU.mult, ALU.add, pbanks + [bm], [sc])
                    pT = ptr_.next()
                    act(P, pT.ap[:, :, 0:nq], sc.ap[:, :, 0:nq], AF.Exp, [sc], [pT])
                    return pT

                units = [(r, b) for r in range(dil) for b in range(nb)]
                nxt = scores(*units[0])
                prev = None
                for ui, (r, b) in enumerate(units):
                    kt = r * nb + b
                    cur = nxt
                    nxt = scores(*units[ui + 1]) if ui + 1 < len(units) else None
                    if b == 0:
                        prev = None
                    pso, psd = PSO.next(), PSD.next()
                    srcs = []
                    for j in range(2):
                        if prev is not None:
                            srcs.append((j, kt - 1, prev, 128))
                        srcs.append((j, kt, cur, 0))
                    for which in range(2):
                        pt_ = pso if which == 0 else psd
                        for n_, (j, ktile, pt, coff) in enumerate(srcs):
                            lhs = vZ[j].ap[:, ktile, hp, :] if which == 0 else onesZ[j].ap
                            mm(P, pt_.ap[:, 0:128], lhs, pt.ap[:, j, coff:coff + 128], n_ == 0, n_ == len(srcs) - 1,
                               [vZ[j], onesZ[j], pt], [pt_])
                    cp(P, 'act', Ov[:, b, :, r], pso.ap[:, 0:128], [pso], [Oacc])
                    if g == 0:
                        cp(P, 'dve', Dv[:, b, :, r], psd.ap[:, 0:128], [psd], [Dsum])
                    else:
                        tt(P, 'dve', Dv[:, b, :, r], Dv[:, b, :, r], psd.ap[:, 0:128], ALU.add, [psd, Dsum], [Dsum])
                    prev = cur
                    yield
        act(P, Dsum.ap, Dsum.ap, AF.Ln, [Dsum], [Dsum])
        act(P, Dsum.ap, Dsum.ap, AF.Exp, [Dsum], [Dsum], scale=-1.0)
        for g in range(3):
            tt(P, 'dve', Oacc.ap[:, 2 * g:2 * g + 2, :], Oacc.ap[:, 2 * g:2 * g + 2, :], Dsum.ap,
               ALU.mult, [Oacc, Dsum], [Oacc])
        dma(P, 'sp', D['OC'].rearrange('c p t -> p c t')[:, :, tok0:tok0 + T], Oacc.ap, reads=[Oacc])

    def gen_attn_sample(l):
        tokS0 = NTP
        MAXT = 17
        bms = []
        for g, (win, dil) in enumerate(GROUPS):
            t_ = A.alloc([4, (win // 128 + 1) * 8], F32, 'bms%d' % g)
            dma(P, 'sp', t_.ap, D['bms%d' % g].rearrange('p h t q -> p h (t q)'), writes=[t_])
            bms.append(t_)
        kf = Ring([A.alloc([16, 256], BF16, 'kf%d' % i) for i in range(1)])
        vf = Ring([A.alloc([16, 256], BF16, 'vf%d' % i) for i in range(1)])
        vnew = A.alloc([256], BF16, 'vnew')
        kTs = A.alloc([2, MAXT * 128], BF16, 'kTs')
        qTs = A.alloc([2, 8], BF16, 'qTs')
        vZ = [A.alloc([MAXT, 2, 128], BF16, 'vZs%d' % j) for j in range(2)]
        Os = A.alloc([6, 8], F32, 'Os')
        Ds = A.alloc([2, 8], F32, 'Ds')
        rd = A.alloc([2, 8], F32, 'rd')
        ocs = A.alloc([6, 8], BF16, 'ocs')
        scr_ = Ring([A.alloc([MAXT * 8], F32, 'scs%d' % i) for i in range(2)])
        pts = Ring([A.alloc([MAXT * 8], BF16, 'pTs%d' % i) for i in range(4)])
        for j in range(2):
            ms(P, 'pool', vZ[j].ap, 0.0, [vZ[j]])
        ms(P, 'pool', vnew.ap, 0.0, [vnew])
        for b in range(NSS):
            tk = tokS0 + b * DSEQ
            for g, (win, dil) in enumerate(GROUPS):
                ntc = win // 128
                nt_ = ntc + 1
                dma(P, 'sp', D['k%ds' % g][l, b, 0:win - DSEQ, :], D['ck%d' % g][l, b, DSEQ:win, :])
                dma(P, 'sp', D['v%ds' % g][l, b, 0:win - DSEQ, :], D['cv%d' % g][l, b, DSEQ:win, :])
                kc_, vc_ = kf.next(), vf.next()
                dma(P, 'pool', kc_.ap[:, 0:ntc, :], D['ck%d' % g][l, b].rearrange('(t p) c -> p t c', p=128), writes=[kc_])
                dma(P, 'pool', vc_.ap[:, 0:ntc, :], D['cv%d' % g][l, b].rearrange('(t p) c -> p t c', p=128), writes=[vc_])
                ms(P, 'dve', kTs.ap[:, :, win + DSEQ:(ntc + 1) * 128], 0.0, [kTs])
                dma(P, 'sp', kTs.ap[:, :, win:win + DSEQ],
                    D['PF'][F_KA + 2 * g:F_KA + 2 * g + 2].rearrange('h p t -> p h t')[:, :, tk:tk + DSEQ], writes=[kTs])
                dma(P, 'sp', qTs.ap, D['PF'][F_QA + 2 * g:F_QA + 2 * g + 2].rearrange('h p t -> p h t')[:, :, tk:tk + DSEQ],
                    writes=[qTs])
                dma(P, 'sp', vnew.ap[0:DSEQ, :], D['VA'][tk:tk + DSEQ, g * 256:(g + 1) * 256], writes=[vnew])
                for hp in range(2):
                    for t0_ in range(0, ntc, 8):
                        nn = min(8, ntc - t0_)
                        for x in range(nn):
                            tr(P, psb.ap[:, x * 128:(x + 1) * 128], kc_.ap[:, t0_ + x, hp * 128:(hp + 1) * 128],
                               ident16.ap, [kc_, ident16], [psb])
                        cp(P, 'act', kTs.ap[:, hp, t0_ * 128:(t0_ + nn) * 128], psb.ap[:, 0:nn * 128], [psb], [kTs])
                vcv = vc_.ap.rearrange('p t (hp x) -> p t hp x', x=128)
                cp(P, 'dve', vZ[0].ap[:, 0:ntc, :, 0:64], vcv[:, 0:ntc, :, 0:64], [vc_], [vZ[0]])
                cp(P, 'dve', vZ[1].ap[:, 0:ntc, :, 64:128], vcv[:, 0:ntc, :, 64:128], [vc_], [vZ[1]])
                vnv = vnew.ap.rearrange('p (hp x) -> p hp x', x=128)
                cp(P, 'dve', vZ[0].ap[:, ntc, :, 0:64], vnv[:, :, 0:64], [vnew], [vZ[0]])
                cp(P, 'dve', vZ[1].ap[:, ntc, :, 64:128], vnv[:, :, 64:128], [vnew], [vZ[1]])
                for hp in range(2):
                    cur = []
                    for j in range(2):
                        hh = hp * 2 + j
                        ps = PS.next()
                        for t in range(nt_):
                            mm(P, ps.ap[:, t * 8:(t + 1) * 8], kTs.ap[j * 64:(j + 1) * 64, hp, t * 128:(t + 1) * 128],
                               qTs.ap[j * 64:(j + 1) * 64, hp, :], True, True, [kTs, qTs], [ps])
                        sc = scr_.next()
                        stt(P, sc.ap[:, 0:nt_ * 8], ps.ap[:, 0:nt_ * 8], 0.125, bms[g].ap[:, hh, :], ALU.mult, ALU.add,
                            [ps, bms[g]], [sc])
                        pT = pts.next()
                        act(P, pT.ap[:, 0:nt_ * 8], sc.ap[:, 0:nt_ * 8], AF.Exp, [sc], [pT])
                        cur.append(pT)
                    pso = PS.next()
                    for which in range(2):
                        oap = pso.ap[:, which * 8:(which + 1) * 8]
                        cnt = 0
                        for j in range(2):
                            for t in range(nt_):
                                lhs = vZ[j].ap[:, t, hp, :] if which == 0 else onesZ[j].ap
                                mm(P, oap, lhs, cur[j].ap[:, t * 8:(t + 1) * 8], cnt == 0, cnt == 2 * nt_ - 1,
                                   [vZ[j], onesZ[j], cur[j]], [pso])
                                cnt += 1
                    cp(P, 'act', Os.ap[:, g * 2 + hp, :], pso.ap[:, 0:8], [pso], [Os])
                    if g == 0:
                        cp(P, 'dve', Ds.ap[:, hp, :], pso.ap[:, 8:16], [pso, Os], [Ds])
                    else:
                        tt(P, 'dve', Ds.ap[:, hp, :], Ds.ap[:, hp, :], pso.ap[:, 8:16], ALU.add, [pso, Ds, Os], [Ds])
                    yield
            P.op('dve', lambda e: e.reciprocal(out=rd.ap, in_=Ds.ap), reads=[Ds.b], writes=[rd.b])
            for g in range(3):
                tt(P, 'dve', ocs.ap[:, 2 * g:2 * g + 2, :], Os.ap[:, 2 * g:2 * g + 2, :], rd.ap, ALU.mult, [Os, rd], [ocs])
            dma(P, 'sp', D['OC'].rearrange('c p t -> p c t')[:, :, tk:tk + DSEQ], ocs.ap, reads=[ocs])

    def load_outproj_weights(l, lazy=False):
        wo = {}
        todo = []
        for nm, kch in (('w_oa', 8), ('w_ob', 8), ('w_oc', 6), ('w_out', 8)):
            wo[nm] = A.alloc([kch, 1024], BF16, nm)
            for kc in range(kch):
                todo.append((nm, kc))

        def issue(k=1):
            for _ in range(k):
                if todo:
                    nm, kc = todo.pop(0)
                    dma(P, 'pool', wo[nm].ap[:, kc, :], D[nm][l, kc * 128:(kc + 1) * 128, :], writes=[wo[nm]])
        if not lazy:
            issue(len(todo))
        return wo, A.top, issue, todo

    def phase_outproj(l, src_h, wo, mark_wo):
        P.barrier()
        A.top = mark_wo
        obr = Ring([A.alloc([8, 512], BF16, 'ob%d' % i) for i in range(2)])
        gbr = Ring([A.alloc([8, 512], BF16, 'gb%d' % i) for i in range(2)])
        sgr = Ring([A.alloc([512], F32, 'sg%d' % i) for i in range(2)])
        tmr = Ring([A.alloc([512], F32, 'tm%d' % i) for i in range(2)])
        merged = A.alloc([8, 512], F32, 'merged')
        m16 = A.alloc([8, 512], BF16, 'm16')
        hr = Ring([A.alloc([8, 512], F32, 'h%d' % i) for i in range(2)])
        hn = Ring([A.alloc([8, 512], F32, 'hn%d' % i) for i in range(2)])
        srcv = src_h.rearrange('c p t -> p c t')
        branches = (('w_oa', 'OG', 8, F_GA), ('w_ob', 'PM', 8, F_GB), ('w_oc', 'OC', 6, F_GC))
        for si, (t0, n) in enumerate(SUPER):
            ht = hr.next()
            dma(P, 'sp', ht.ap[:, :, 0:n], srcv[:, :, t0:t0 + n], writes=[ht])
            for bi, (wn, sn, kch, fg) in enumerate(branches):
                ob, gb = obr.next(), gbr.next()
                dma(P, 'sp', ob.ap[:, 0:kch, 0:n], D[sn].rearrange('c p t -> p c t')[:, :, t0:t0 + n], writes=[ob])
                dma(P, 'sp', gb.ap[:, :, 0:n], D['PF'][fg:fg + 8].rearrange('c p t -> p c t')[:, :, t0:t0 + n], writes=[gb])
                for dc in range(8):
                    ps = PS.next()
                    for kc in range(kch):
                        mm(P, ps.ap[:, 0:n], wo[wn].ap[:, kc, dc * 128:(dc + 1) * 128], ob.ap[:, kc, 0:n],
                           kc == 0, kc == kch - 1, [wo[wn], ob], [ps])
                    sg = sgr.next()
                    act(P, sg.ap[:, 0:n], gb.ap[:, dc, 0:n], AF.Sigmoid, [gb], [sg])
                    if bi == 0:
                        tt(P, 'dve', merged.ap[:, dc, 0:n], ps.ap[:, 0:n], sg.ap[:, 0:n], ALU.mult, [ps, sg], [merged])
                    else:
                        tm = tmr.next()
                        tt(P, 'dve', tm.ap[:, 0:n], ps.ap[:, 0:n], sg.ap[:, 0:n], ALU.mult, [ps, sg], [tm])
                        if bi == 1:
                            tt(P, 'pool', merged.ap[:, dc, 0:n], merged.ap[:, dc, 0:n], tm.ap[:, 0:n], ALU.add,
                               [merged, tm], [merged])
                        else:
                            tt(P, 'pool', m16.ap[:, dc, 0:n], merged.ap[:, dc, 0:n], tm.ap[:, 0:n], ALU.add,
                               [merged, tm], [m16])
            hnew = hn.next()
            for dc in range(8):
                ps = PS.next()
                for kc in range(8):
                    mm(P, ps.ap[:, 0:n], wo['w_out'].ap[:, kc, dc * 128:(dc + 1) * 128], m16.ap[:, kc, 0:n],
                       kc == 0, kc == 7, [wo['w_out'], m16], [ps])
                tt(P, 'dve', hnew.ap[:, dc, 0:n], ps.ap[:, 0:n], ht.ap[:, dc, 0:n], ALU.add, [ps, ht], [hnew])
            dma(P, 'sp', D['hT'].rearrange('c p t -> p c t')[:, :, t0:t0 + n], hnew.ap[:, :, 0:n], reads=[hnew])

    def phase_ffn_up(l, xn, xnb, mark):
        P.barrier()
        A.top = mark
        w2d = D['ffn_w_up'][l]
        ABW = NPS * (SEQ + 2) + NSS * (DSEQ + 2)
        SOFF = NPS * (SEQ + 2)
        wr = Ring([A.alloc([8, 2, 128], BF16, 'wu%d' % i) for i in range(4)])
        abr = Ring([A.alloc([ABW], F32, 'ab%d' % i) for i in range(2)])
        bbr = Ring([A.alloc([NT], BF16, 'bb%d' % i) for i in range(2)])
        ybr = Ring([A.alloc([NT], F32, 'ybuf%d' % i) for i in range(2)])
        hsr = Ring([A.alloc([NT], BF16, 'hst%d' % i) for i in range(2)])
        halos = {id(t_): Buf('halo') for t_ in abr.tiles}
        for t_ in abr.tiles:
            ms(P, 'pool', t_.ap, 0.0, [t_, halos[id(t_)]])

        def conv_stages(j, ab, bb, ybuf):
            halo = halos[id(ab)]
            sv = ab.ap[:, SOFF:ABW].rearrange('p (b x) -> p b x', x=DSEQ + 2)
            pv = ab.ap[:, 0:SOFF].rearrange('p (s x) -> p s x', x=SEQ + 2)
            w0, w1, w2 = (vcol(V_CW + l * 66 + k * NFF + j) for k in range(3))
            cb = vcol(V_CB + l * NFF + j)
            segs = [(lambda sh: pv[:, :, sh:sh + SEQ], ybuf.ap[:, 0:NTP].rearrange('p (s t) -> p s t', t=SEQ)),
                    (lambda sh: sv[:, :, sh:sh + DSEQ], ybuf.ap[:, NTP:NT].rearrange('p (b i) -> p b i', i=DSEQ))]

            def st_a():
                for av, yv in segs:
                    act(P, yv, av(0), AF.Identity, [ab, halo, vecs], [ybuf], scale=w0, bias=cb)

            def st_b():
                for av, yv in segs:
                    stt(P, yv, av(1), w1, yv, ALU.mult, ALU.add, [ab, halo, ybuf, vecs], [ybuf])

            def st_c():
                for av, yv in segs:
                    stt(P, yv, av(2), w2, yv, ALU.mult, ALU.add, [ab, ybuf, vecs], [ybuf])
                for s_ in range(NPS):
                    base = s_ * (SEQ + 2)
                    dma(P, 'sp', D['convT'][l, j, :, s_, :], ab.ap[:, base + SEQ:base + SEQ + 2], reads=[ab])
                dma(P, 'sp', D['convT'][l, j, :, NPS:NPS + NSS, :], sv[:, :, DSEQ:DSEQ + 2], reads=[ab])

            def st_d():
                act(P, ybuf.ap, ybuf.ap, AF.Silu, [ybuf], [ybuf])

            def st_e():
                hs = hsr.next()
                tt(P, 'pool', hs.ap, ybuf.ap, bb.ap, ALU.mult, [ybuf, bb], [hs])
                dma(P, 'sp', D['HF'][j], hs.ap, reads=[hs])

            return {0: st_a, 2: st_b, 4: st_c, 6: st_d, 8: st_e}

        pending = {}
        wts = {}

        def issue_w(jj):
            if jj < NFF:
                wts[jj] = wr.next()
                for half in range(2):
                    src = w2d[:, half * DFF + jj * 128:half * DFF + (jj + 1) * 128].rearrange('(kc p) n -> p kc n', p=128)
                    dma(P, 'pool', wts[jj].ap[:, :, half, :], src, writes=[wts[jj]])

        issue_w(0)
        issue_w(1)
        for j in range(NFF + 1):
            if j < NFF:
                issue_w(j + 2)
                wt = wts[j]
                ab, bb = abr.next(), bbr.next()
                ybuf = ybr.next()
                sv = ab.ap[:, SOFF:ABW].rearrange('p (b x) -> p b x', x=DSEQ + 2)
                halo = halos[id(ab)]
                dma(P, 'sp', sv[:, :, 0:2], D['sconvT'][l, j], writes=[halo])
            for si, (t0, n) in enumerate(SUPER):
                if j < NFF:
                    psa, psb_ = PS.next(), PS.next()
                    for kc in range(8):
                        mm(P, psa.ap[:, 0:n], wt.ap[:, kc, 0, :], xn.ap[:, kc, t0:t0 + n], kc == 0, kc == 7,
                           [wt, xnb[si]], [psa])
                    for kc in range(8):
                        mm(P, psb_.ap[:, 0:n], wt.ap[:, kc, 1, :], xn.ap[:, kc, t0:t0 + n], kc == 0, kc == 7,
                           [wt, xnb[si]], [psb_])
                    if si < 8:
                        s_ = si // 4
                        c0 = s_ * (SEQ + 2) + 2 + (si % 4) * 512
                        cp(P, 'act', ab.ap[:, c0:c0 + n], psa.ap[:, 0:n], [psa], [ab])
                    else:
                        cp(P, 'act', sv[:, :, 2:2 + DSEQ], psa.ap[:, 0:n].rearrange('p (b i) -> p b i', i=DSEQ), [psa], [ab])
                    cp(P, 'dve', bb.ap[:, t0:t0 + n], psb_.ap[:, 0:n], [psb_], [bb])
                if si in pending:
                    pending[si]()
            pending = conv_stages(j, ab, bb, ybuf) if j < NFF else {}

    def phase_ffn_down(l):
        phase_reset(MARK0)
        wd = A.alloc([NFF, 1024], BF16, 'wd')
        wdb = [Buf('wd%d' % kc) for kc in range(NFF)]
        for kc in range(NFF):
            dma(P, 'pool', wd.ap[:, kc, :], D['ffn_w_down'][l, kc * 128:(kc + 1) * 128, :], writes=[wdb[kc]])
        hfr = Ring([A.alloc([NFF, 512], BF16, 'hf%d' % i) for i in range(2)])
        hr = Ring([A.alloc([8, 512], F32, 'h%d' % i) for i in range(2)])
        hn = Ring([A.alloc([8, 512], F32, 'hn%d' % i) for i in range(2)])
        hv = D['hT'].rearrange('c p t -> p c t')
        for si, (t0, n) in enumerate(SUPER):
            hf, ht, hnew = hfr.next(), hr.next(), hn.next()
            dma(P, 'sp', hf.ap[:, :, 0:n], D['HF'].rearrange('c p t -> p c t')[:, :, t0:t0 + n], writes=[hf])
            dma(P, 'sp', ht.ap[:, :, 0:n], hv[:, :, t0:t0 + n], writes=[ht])
            for dc in range(8):
                ps = PS.next()
                for kc in range(NFF):
                    mm(P, ps.ap[:, 0:n], wd.ap[:, kc, dc * 128:(dc + 1) * 128], hf.ap[:, kc, 0:n],
                       kc == 0, kc == NFF - 1, [wdb[kc], hf], [ps])
                tt(P, 'dve', hnew.ap[:, dc, 0:n], ps.ap[:, 0:n], ht.ap[:, dc, 0:n], ALU.add, [ps, ht], [hnew])
            dma(P, 'sp', hv[:, :, t0:t0 + n], hnew.ap[:, :, 0:n], reads=[hnew])

    def run_all():
        for l in range(DEPTH):
            src = D['xT'] if l == 0 else D['hT']
            xn, xnb, mark, xlo, mark_lo = phase_norm(src, V_N1 + l * 8, 'xn', need_lo=True)
            if stop_after == ('norm', l):
                if debug_scratch:
                    dma(P, 'sp', D['XN'].rearrange('c p t -> p c t'), xn.ap, reads=xnb)
                return
            phase_inproj_qk(l, xn, xnb, xlo, mark_lo)
            phase_inproj(l, xn, xnb, mark)
            if stop_after == ('inproj', l):
                return
            def run_group(mk):
                phase_reset(MARK0)
                gens = mk()
                while gens:
                    for g_ in list(gens):
                        try:
                            next(g_)
                        except StopIteration:
                            gens.remove(g_)

            def grp1():
                sh = gla_shared(l, 256, 128)
                return [gen_gla(l, s_ * SEQ, SEQ, 128, None, D['gla_p'][l, s_], sh) for s_ in range(NPS)]

            def grp2():
                sh = gla_shared(l, DSEQ, DSEQ)
                psh = pool_shared(l)
                gc_ = {}

                def sample_chain():
                    for b in range(NSS):
                        yield from gen_gla(l, NTP + b * DSEQ, DSEQ, DSEQ, D['state_gla'][l, b], D['gla_s'][l, b], sh,
                                           cache=gc_)
                return ([sample_chain()]
                        + [gen_pool(l, s_, s_ * SEQ, SEQ, None, psh) for s_ in range(NPS)]
                        + [gen_pool(l, NPS + b, NTP + b * DSEQ, DSEQ, b, psh) for b in range(NSS)])

            run_group(grp1)
            run_group(grp2)
            run_group(lambda: [gen_attn_prompt(l, 0), gen_attn_sample(l)])
            wo_pre = {}
            phase_reset(MARK0)
            wo_pre['wo'], wo_pre['mark'], issue_wo, todo_wo = load_outproj_weights(l, lazy=True)
            g5 = gen_attn_prompt(l, SEQ)
            step = 0
            for _ in g5:
                step += 1
                if step >= 4 and step % 3 == 0:
                    issue_wo(1)
            issue_wo(len(todo_wo))
            if stop_after == ('attn', l):
                return
            phase_outproj(l, src, wo_pre['wo'], wo_pre['mark'])
            if stop_after == ('outproj', l):
                return
            xn, xnb, mark, _, _ = phase_norm(D['hT'], V_N2 + l * 8, 'xn')
            phase_ffn_up(l, xn, xnb, mark)
            phase_ffn_down(l)
            if stop_after == ('ffn', l):
                return
        phase_norm(D['hT'], V_FN, 'final')

    run_all()
    P.barrier()
    P.emit(es)
    es.close()
    return nc


_NC_CACHE = {}


def _get_nc():
    if 'nc' not in _NC_CACHE:
        _NC_CACHE['nc'] = build_program()
    return _NC_CACHE['nc']


def make_in_maps(inp, n_cores=8):
    f = lambda a: np.ascontiguousarray(np.asarray(a, dtype=np.float32))
    consts = host_constants()
    vec = np.zeros((128, NV), np.float32)

    def put(col0, arr2d):
        a = f(arr2d).reshape(-1, 128)
        vec[:, col0:col0 + a.shape[0]] = a.T

    put(V_N1, inp['norm1_g'])
    put(V_N2, inp['norm2_g'])
    put(V_FN, inp['final_norm_g'])
    put(V_BA, inp['gla_ba'])
    put(V_GN, inp['gla_norm_g'])
    put(V_PS, inp['pool_scale'])
    put(V_CW, inp['ffn_conv_w'])
    put(V_CB, inp['ffn_conv_b'])
    shared = {k: f(inp[k]) for k in ('w_in', 'gla_wa2', 'pool_w', 'w_oa', 'w_ob', 'w_oc', 'w_out', 'ffn_w_up',
                                     'ffn_w_down')}
    shared['vecs'] = vec
    shared.update(consts)
    maps = []
    xp, xs = f(inp['x_prompt']), f(inp['x_sample'])
    for c in range(n_cores):
        m = dict(shared)
        xa = np.concatenate([xp[NPS * c:NPS * (c + 1)].reshape(NTP, DM), xs[NSS * c:NSS * (c + 1)].reshape(NSS * DSEQ, DM)], 0)
        m['xT'] = np.ascontiguousarray(xa.T).reshape(8, 128, NT)
        sl = slice(NSS * c, NSS * (c + 1))
        m['state_gla'] = f(inp['state_gla'][:, sl])
        m['spoolT'] = np.ascontiguousarray(f(inp['state_pool'][:, sl]).reshape(DEPTH, NSS, 15, 8, 128).transpose(0, 3, 4, 1, 2))
        m['sconvT'] = np.ascontiguousarray(f(inp['state_ffn_conv'][:, sl]).reshape(DEPTH, NSS, 2, NFF, 128).transpose(0, 3, 4, 1, 2))
        for gi, (win, dil) in enumerate(GROUPS):
            m['ck%d' % gi] = f(inp['cache_k_w%d' % win][:, sl]).reshape(DEPTH, NSS, win, 256)
            m['cv%d' % gi] = f(inp['cache_v_w%d' % win][:, sl]).reshape(DEPTH, NSS, win, 256)
        maps.append(m)
    return maps


def assemble(results):
    cat = lambda xs, ax: np.ascontiguousarray(np.concatenate(xs, axis=ax))
    yp, ys, poolp, pools, convp, convs = [], [], [], [], [], []
    for r in results:
        y = r['yT'].reshape(DM, NT).T
        yp.append(y[:NTP].reshape(NPS, SEQ, DM))
        ys.append(y[NTP:].reshape(NSS, DSEQ, DM))
        pl = r['poolT'].transpose(0, 3, 4, 1, 2).reshape(DEPTH, NPS + NSS, 15, DM)
        poolp.append(pl[:, :NPS])
        pools.append(pl[:, NPS:])
        cv = r['convT'].transpose(0, 3, 4, 1, 2).reshape(DEPTH, NPS + NSS, 2, DFF)
        convp.append(cv[:, :NPS])
        convs.append(cv[:, NPS:])
    out = [cat(yp, 0), cat(ys, 0),
           cat([r['gla_p'] for r in results], 1), cat([r['gla_s'] for r in results], 1),
           cat(poolp, 1), cat(pools, 1)]
    for gi, (win, dil) in enumerate(GROUPS):
        kp = cat([r['kT%d' % gi].transpose(0, 1, 4, 2, 3).reshape(DEPTH, NPS, win, 4, 64) for r in results], 1)
        ks = cat([r['k%ds' % gi].reshape(DEPTH, NSS, win, 4, 64) for r in results], 1)
        vp = cat([r['v%dp' % gi].reshape(DEPTH, NPS, win, 4, 64) for r in results], 1)
        vs = cat([r['v%ds' % gi].reshape(DEPTH, NSS, win, 4, 64) for r in results], 1)
        out += [kp, ks, vp, vs]
    out += [cat(convp, 1), cat(convs, 1)]
    return tuple(np.asarray(o, dtype=np.float32) for o in out)


def kernel(**inputs):
    nc = _get_nc()
    in_maps = make_in_maps(inputs)
    res = run_bass_kernel_spmd(nc, in_maps, core_ids=list(range(8)))
    return assemble(res.results)
```

```python
import numpy as np
from contextlib import ExitStack
import concourse.bass as bass
import concourse.mybir as mybir
from concourse.bass_utils import run_bass_kernel_spmd

F32 = mybir.dt.float32
BF16 = mybir.dt.bfloat16
AF = mybir.ActivationFunctionType
ALU = mybir.AluOpType

SAME_ENGINE_SYNC = True
PREFETCH_WO = True
SEM_CHUNK = 30000
N_DMA_SLOTS = {'sp': 8, 'pool': 6}


class Buf:
    __slots__ = ('w', 'r', 'name')

    def __init__(self, name=''):
        self.w = None
        self.r = {}
        self.name = name


class Prog:
    ENGS = ('pe', 'act', 'dve', 'pool', 'sp')

    def __init__(self, nc):
        self.nc = nc
        self.ops = {e: [] for e in self.ENGS}
        self.waited_c = {e: {} for e in self.ENGS}
        self.waited_d = {e: {} for e in self.ENGS}
        self.dma_n = {}
        self.dma_rr = {q: 0 for q in N_DMA_SLOTS}
        self.pending = {e: [] for e in self.ENGS}

    def barrier(self):
        evs = []
        for f in self.ENGS:
            if f in ('pe', 'act', 'dve', 'pool') or True:
                for i in range(len(self.ops[f]) - 1, -1, -1):
                    if self.ops[f][i]['dma'] is None:
                        evs.append(('c', f, i))
                        break
        for (q, k), n in self.dma_n.items():
            evs.append(('d', q, k, n))
        for e in self.ENGS:
            self.pending[e].extend(self._add_waits(e, evs, force_same=True))

    def _collect(self, reads, writes):
        deps = []
        for b in reads:
            if b.w is not None:
                deps.append(b.w)
        for b in writes:
            if b.w is not None:
                deps.append(b.w)
            deps.extend(b.r.values())
        return deps

    def _add_waits(self, eng, deps, is_pe_mm=False, force_same=False):
        waits = []
        for ev in deps:
            if ev[0] == 'c':
                _, src, idx = ev
                if src == eng and (force_same or not SAME_ENGINE_SYNC or eng == 'pe'):
                    continue
                if self.waited_c[eng].get(src, -1) >= idx:
                    continue
                self.waited_c[eng][src] = idx
                self.ops[src][idx]['sig'] = True
                waits.append(ev)
            else:
                _, q, k, n = ev
                if self.waited_d[eng].get((q, k), 0) >= n:
                    continue
                self.waited_d[eng][(q, k)] = n
                waits.append(ev)
        return waits

    def _publish(self, ev, key, reads, writes):
        for b in reads:
            b.r[key] = ev
        for b in writes:
            b.w = ev
            b.r = {}

    def op(self, eng, fn, reads=(), writes=(), pe_acc=False):
        deps = self._collect(reads, writes)
        waits = self.pending[eng] + self._add_waits(eng, deps, is_pe_mm=pe_acc)
        self.pending[eng] = []
        idx = len(self.ops[eng])
        self.ops[eng].append({'fn': fn, 'waits': waits, 'sig': False, 'dma': None})
        self._publish(('c', eng, idx), eng, reads, writes)

    def dma(self, q, out, in_, reads=(), writes=(), **kw):
        nslots = N_DMA_SLOTS[q]
        k = self.dma_rr[q]
        self.dma_rr[q] = (k + 1) % nslots
        n = self.dma_n.get((q, k), 0)
        deps = self._collect(reads, writes)
        if n > 0:
            deps.append(('d', q, k, n))
        waits = self.pending[q] + self._add_waits(q, deps)
        self.pending[q] = []
        self.dma_n[(q, k)] = n + 1
        self.ops[q].append({'fn': (lambda e, out=out, in_=in_, kw=kw: e.dma_start(out=out, in_=in_, **kw)),
                            'waits': waits, 'sig': False, 'dma': (q, k)})
        self._publish(('d', q, k, n + 1), ('d', q, k), reads, writes)

    def emit(self, es):
        nc = self.nc
        sig_rank = {}
        eng_sems = {}
        for e in self.ENGS:
            rank = 0
            for i, o in enumerate(self.ops[e]):
                if o['sig']:
                    sig_rank[(e, i)] = rank
                    rank += 1
            nsem = (rank + SEM_CHUNK - 1) // SEM_CHUNK
            eng_sems[e] = [es.enter_context(nc.semaphore('s_%s_%d' % (e, j))) for j in range(nsem)]
        dma_sems = {}
        for (q, k) in self.dma_n:
            dma_sems[(q, k)] = es.enter_context(nc.semaphore('d_%s_%d' % (q, k)))

        def resolve(ev):
            if ev[0] == 'c':
                r = sig_rank[(ev[1], ev[2])]
                return eng_sems[ev[1]][r // SEM_CHUNK], (r % SEM_CHUNK) + 1
            return dma_sems[(ev[1], ev[2])], 16 * ev[3]

        final_waits = [('d', q, k, n) for (q, k), n in self.dma_n.items()]

        def run(ename, e):
            for i, o in enumerate(self.ops[ename]):
                for ev in o['waits']:
                    s, v = resolve(ev)
                    e.wait_ge(s, v)
                ins = o['fn'](e)
                if o['dma'] is not None:
                    ins.then_inc(dma_sems[o['dma']], 16)
                elif o['sig']:
                    r = sig_rank[(ename, i)]
                    ins.then_inc(eng_sems[ename][r // SEM_CHUNK], 1)
            if ename == 'sp':
                for ev in final_waits:
                    s, v = resolve(ev)
                    e.wait_ge(s, v)

        with nc.Block() as block:
            @block.tensor
            def _(e):
                run('pe', e)

            @block.scalar
            def _(e):
                run('act', e)

            @block.vector
            def _(e):
                run('dve', e)

            @block.gpsimd
            def _(e):
                run('pool', e)

            @block.sync
            def _(e):
                run('sp', e)


class T:
    __slots__ = ('ap', 'b')

    def __init__(self, ap, name=''):
        self.ap = ap
        self.b = Buf(name)


class Arena:
    def __init__(self, tensor, words):
        self.t = tensor
        self.words = words
        self.top = 0

    def alloc(self, free, dt, name='', parts=128):
        n = 1
        for f in free:
            n *= f
        w = n if dt == F32 else (n + 1) // 2
        w = (w + 3) // 4 * 4
        off = self.top
        self.top += w
        assert self.top <= self.words, 'arena overflow %s %d > %d' % (name, self.top, self.words)
        nw = n if dt == F32 else (n + 1) // 2
        ap = self.t[0:parts, off:off + nw]
        if dt == BF16:
            ap = ap.bitcast(BF16)
            if n % 2:
                ap = ap[:, 0:n]
        if len(free) == 2:
            ap = ap.rearrange('p (a b) -> p a b', b=free[1])
        elif len(free) == 3:
            ap = ap.rearrange('p (a b c) -> p a b c', b=free[1], c=free[2])
        return T(ap, name)


def _bs(ts):
    return [t.b if isinstance(t, T) else t for t in ts]


def mm(P, out, lhsT, rhs, start, stop, reads, writes):
    P.op('pe', lambda e: e.matmul(out, lhsT=lhsT, rhs=rhs, start=start, stop=stop),
         reads=_bs(reads), writes=_bs(writes))


def tr(P, out, in_, ident, reads, writes):
    P.op('pe', lambda e: e.transpose(out, in_, ident), reads=_bs(reads), writes=_bs(writes))


def act(P, out, in_, func, reads, writes, scale=1.0, bias=None):
    if bias is None:
        P.op('act', lambda e: e.activation(out=out, in_=in_, func=func, scale=scale),
             reads=_bs(reads), writes=_bs(writes))
    else:
        P.op('act', lambda e: e.activation(out=out, in_=in_, func=func, scale=scale, bias=bias),
             reads=_bs(reads), writes=_bs(writes))


def tt(P, eng, out, in0, in1, op, reads, writes):
    P.op(eng, lambda e: e.tensor_tensor(out=out, in0=in0, in1=in1, op=op), reads=_bs(reads), writes=_bs(writes))


def ts(P, eng, out, in0, s1, s2, op0, op1, reads, writes):
    P.op(eng, lambda e: e.tensor_scalar(out=out, in0=in0, scalar1=s1, scalar2=s2, op0=op0, op1=op1),
         reads=_bs(reads), writes=_bs(writes))


def stt(P, out, in0, scalar, in1, op0, op1, reads, writes):
    P.op('dve', lambda e: e.scalar_tensor_tensor(out=out, in0=in0, scalar=scalar, in1=in1, op0=op0, op1=op1),
         reads=_bs(reads), writes=_bs(writes))


def cp(P, eng, out, in_, reads, writes):
    if eng == 'act':
        P.op('act', lambda e: e.copy(out=out, in_=in_), reads=_bs(reads), writes=_bs(writes))
    else:
        P.op(eng, lambda e: e.tensor_copy(out=out, in_=in_), reads=_bs(reads), writes=_bs(writes))


def ms(P, eng, ap, val, writes):
    P.op(eng, lambda e: e.memset(ap, val), writes=_bs(writes))


def dma(P, q, out, in_, reads=(), writes=()):
    P.dma(q, out, in_, reads=_bs(reads), writes=_bs(writes))


DEPTH = 2
DM = 1024
NPS = 2
SEQ = 2048
NSS = 4
DSEQ = 8
NTP = NPS * SEQ
NT = NTP + NSS * DSEQ
DFF = 2816
NFF = 22
INW = 9488
C_QG, C_KG, C_VG, C_RG, C_ALR, C_UP, C_QA, C_KA, C_VA, C_GA, C_GB, C_GC = (
    0, 512, 1024, 2048, 3072, 3088, 4112, 4880, 5648, 6416, 7440, 8464)
F_QG, F_KG, F_RG, F_ALR, F_QA, F_KA, F_GA, F_GB, F_GC, NFB = 0, 4, 8, 16, 17, 23, 29, 37, 45, 53
GROUPS = ((128, 1), (512, 4), (2048, 16))
SUPER = [(i * 512, 512) for i in range(8)] + [(NTP, NSS * DSEQ)]
V_N1, V_N2, V_FN, V_BA, V_GN, V_PS, V_CW, V_CB, NV = 0, 16, 32, 40, 48, 52, 68, 200, 244
ARENA_WORDS = 52736


def alibi_slopes():
    return (2.0 ** (-8.0 * np.arange(1, 13) / 12)).astype(np.float32)


def host_constants():
    sl = alibi_slopes()
    c = {}
    kl = np.arange(128)[:, None]
    col = np.arange(256)[None, :]
    j = col - kl
    bm = np.zeros((128, 12, 256), np.float32)
    for h in range(12):
        dil = GROUPS[h // 4][1]
        bm[:, h, :] = np.where((j >= 0) & (j <= 128), -sl[h] * (j * dil).astype(np.float32), -1e30)
    c['bm'] = bm
    for gi, (win, dil) in enumerate(GROUPS):
        nt_ = win // 128 + 1
        t = np.zeros((128, 4, nt_, 8), np.float32)
        p = np.arange(128)[:, None, None]
        ti = np.arange(nt_)[None, :, None]
        qi = np.arange(8)[None, None, :]
        cidx = np.where(ti < nt_ - 1, ti * 128 + p, win + p)
        okrow = np.where(ti < nt_ - 1, True, p < 8)
        dist = win + qi - cidx
        valid = okrow & (dist >= 0) & (dist % dil == 0) & (dist // dil <= win // dil)
        for hh in range(4):
            t[:, hh] = np.where(valid, -sl[gi * 4 + hh] * dist.astype(np.float32), -1e30)
        c['bms%d' % gi] = t
    mu = (np.arange(128)[:, None] <= np.arange(128)[None, :]).astype(np.float32)
    c['maskU'] = mu
    ic = np.zeros((128, 8, 16), np.float32)
    for cc in range(8):
        w = (2, 4, 8, 16)[cc // 2]
        ic[:, cc, :] = 1.0 / np.minimum(np.arange(16) + 1, w)
    c['invc'] = ic
    return c


class Ring:
    def __init__(self, tiles):
        self.tiles = tiles
        self.i = 0

    def next(self):
        t = self.tiles[self.i]
        self.i = (self.i + 1) % len(self.tiles)
        return t


def build_program(stop_after=None, debug_scratch=False, sub=None):
    nc = bass.Bass('TRN2', target_bir_lowering=False)
    D = {}

    def din(name, shape, dt=F32):
        D[name] = nc.dram_tensor(name, list(shape), dt, kind='ExternalInput').ap()

    def dout(name, shape, dt=F32):
        D[name] = nc.dram_tensor(name, list(shape), dt, kind='ExternalOutput').ap()

    def dscr(name, shape, dt):
        D[name] = nc.dram_tensor(name, list(shape), dt, kind='ExternalOutput' if debug_scratch else 'Internal').ap()

    din('xT', [8, 128, NT])
    din('state_gla', [DEPTH, NSS, 4, 128, 256])
    din('spoolT', [DEPTH, 8, 128, NSS, 15])
    din('sconvT', [DEPTH, NFF, 128, NSS, 2])
    for gi, (win, dil) in enumerate(GROUPS):
        din('ck%d' % gi, [DEPTH, NSS, win, 256])
        din('cv%d' % gi, [DEPTH, NSS, win, 256])
    din('w_in', [DEPTH, DM, INW])
    din('gla_wa2', [DEPTH, 16, 512])
    din('pool_w', [DEPTH, 4, 256, 256])
    din('w_oa', [DEPTH, 1024, DM])
    din('w_ob', [DEPTH, 1024, DM])
    din('w_oc', [DEPTH, 768, DM])
    din('w_out', [DEPTH, DM, DM])
    din('ffn_w_up', [DEPTH, DM, 2 * DFF])
    din('ffn_w_down', [DEPTH, DFF, DM])
    din('vecs', [128, NV])
    din('bm', [128, 12, 256])
    for gi, (win, dil) in enumerate(GROUPS):
        din('bms%d' % gi, [128, 4, win // 128 + 1, 8])
    din('maskU', [128, 128])
    din('invc', [128, 8, 16])
    dout('yT', [8, 128, NT])
    dout('gla_p', [DEPTH, NPS, 4, 128, 256])
    dout('gla_s', [DEPTH, NSS, 4, 128, 256])
    dout('poolT', [DEPTH, 8, 128, NPS + NSS, 15])
    dout('convT', [DEPTH, NFF, 128, NPS + NSS, 2])
    for gi, (win, dil) in enumerate(GROUPS):
        dout('kT%d' % gi, [DEPTH, NPS, 2, 128, win])
        dout('v%dp' % gi, [DEPTH, NPS, win, 256])
        dout('k%ds' % gi, [DEPTH, NSS, win, 256])
        dout('v%ds' % gi, [DEPTH, NSS, win, 256])
    dscr('hT', [8, 128, NT], F32)
    dscr('PF', [NFB, 128, NT], BF16)
    dscr('UP', [8, 128, NT], F32)
    dscr('VG', [NT, 1024], BF16)
    dscr('VA', [NT, 768], BF16)
    dscr('OG', [8, 128, NT], BF16)
    dscr('PM', [8, 128, NT], BF16)
    dscr('OC', [6, 128, NT], BF16)
    dscr('HF', [NFF, 128, NT], BF16)
    dscr('QK32', [8, 128, NT], F32)
    if debug_scratch:
        dscr('XN', [8, 128, NT], BF16)

    P = Prog(nc)
    es = ExitStack()
    arena_t = es.enter_context(nc.sbuf_tensor('arena', [128, ARENA_WORDS], F32))
    A = Arena(arena_t, ARENA_WORDS)
    pq = [es.enter_context(nc.psum_tensor('pq%d' % i, [128, 1024], F32)) for i in range(3)]
    psf = []
    for i in range(3):
        psf.append(T(pq[i][:, 0:512], 'ps%d' % (2 * i)))
        psf.append(T(pq[i][:, 512:1024], 'ps%d' % (2 * i + 1)))
    psf.append(T(es.enter_context(nc.psum_tensor('ps6', [128, 512], F32))[:], 'ps6'))
    psb = T(es.enter_context(nc.psum_tensor('psb', [128, 1024], BF16))[:], 'psb')
    PS = Ring(psf)

    vecs = A.alloc([NV], F32, 'vecs')
    ident16 = A.alloc([128], BF16, 'ident16')
    ones16 = A.alloc([128], BF16, 'ones16')
    onesZ = [A.alloc([128], BF16, 'onesZ0'), A.alloc([128], BF16, 'onesZ1')]
    nba = A.alloc([8], F32, 'nba')
    dma(P, 'sp', vecs.ap, D['vecs'], writes=[vecs])
    ms(P, 'pool', ident16.ap, 0.0, [ident16])
    P.op('pool', lambda e: e.affine_select(out=ident16.ap, in_=ident16.ap, pattern=[[-1, 128]],
                                           compare_op=ALU.not_equal, fill=1.0, base=0, channel_multiplier=1),
         reads=[ident16.b], writes=[ident16.b])
    ms(P, 'pool', ones16.ap, 1.0, [ones16])
    for j in range(2):
        ms(P, 'pool', onesZ[j].ap, 0.0, [onesZ[j]])
        ms(P, 'pool', onesZ[j].ap[:, j * 64:(j + 1) * 64], 1.0, [onesZ[j]])
    ts(P, 'dve', nba.ap, vecs.ap[:, V_BA:V_BA + 8], -1.0, None, ALU.mult, ALU.bypass, [vecs], [nba])
    MARK0 = A.top

    def vcol(c):
        return vecs.ap[:, c:c + 1]

    def phase_reset(mark):
        P.barrier()
        A.top = mark

    evac_rr = [0]

    def evac_eng():
        evac_rr[0] ^= 1
        return 'act' if evac_rr[0] else 'dve'

    def phase_norm(src, gcol, mode, need_lo=False):
        phase_reset(MARK0)
        xn = None
        xnb = None
        xlo = None
        if mode == 'xn':
            xn = A.alloc([8, NT], BF16, 'xn')
            xnb = [Buf('xn%d' % i) for i in range(len(SUPER))]
        mark_xn = A.top
        if need_lo:
            xlo = A.alloc([8, NT], BF16, 'xlo')
            xfr = Ring([A.alloc([512], F32, 'xf%d' % i) for i in range(4)])
        mark_lo = A.top
        hsr = Ring([A.alloc([8, 512], F32, 'hs%d' % i) for i in range(2)])
        sqr = Ring([A.alloc([8, 512], BF16, 'sq%d' % i) for i in range(2)])
        lnr = Ring([A.alloc([512], F32, 'ln%d' % i) for i in range(2)])
        rsr = Ring([A.alloc([512], F32, 'rs%d' % i) for i in range(2)])
        ysr = Ring([A.alloc([8, 512], F32, 'ys%d' % i) for i in range(2)]) if mode == 'final' else None
        srcv = src.rearrange('c p t -> p c t')
        for si, (t0, n) in enumerate(SUPER):
            hs = hsr.next()
            dma(P, 'sp', hs.ap[:, :, 0:n], srcv[:, :, t0:t0 + n], writes=[hs])
            sq = sqr.next()
            act(P, sq.ap[:, :, 0:n], hs.ap[:, :, 0:n], AF.Square, [hs], [sq])
            ps = PS.next()
            for c in range(8):
                mm(P, ps.ap[:, 0:n], ones16.ap, sq.ap[:, c, 0:n], c == 0, c == 7, [ones16, sq], [ps])
            ln = lnr.next()
            act(P, ln.ap[:, 0:n], ps.ap[:, 0:n], AF.Ln, [ps], [ln], scale=1.0 / DM, bias=1e-6)
            rs = rsr.next()
            act(P, rs.ap[:, 0:n], ln.ap[:, 0:n], AF.Exp, [ln], [rs], scale=-0.5)
            if mode == 'xn' and need_lo:
                for c in range(8):
                    xf = xfr.next()
                    stt(P, xf.ap[:, 0:n], hs.ap[:, c, 0:n], vcol(gcol + c), rs.ap[:, 0:n],
                        ALU.mult, ALU.mult, [hs, rs, vecs], [xf])
                    cp(P, 'act', xn.ap[:, c, t0:t0 + n], xf.ap[:, 0:n], [xf], [xnb[si]])
                    tt(P, 'pool' if c % 2 == 0 else 'dve', xlo.ap[:, c, t0:t0 + n], xf.ap[:, 0:n],
                       xn.ap[:, c, t0:t0 + n], ALU.subtract, [xf, xnb[si]], [xnb[si]])
            elif mode == 'xn':
                for c in range(8):
                    stt(P, xn.ap[:, c, t0:t0 + n], hs.ap[:, c, 0:n], vcol(gcol + c), rs.ap[:, 0:n],
                        ALU.mult, ALU.mult, [hs, rs, vecs], [xnb[si]])
            else:
                ys = ysr.next()
                for c in range(8):
                    stt(P, ys.ap[:, c, 0:n], hs.ap[:, c, 0:n], vcol(gcol + c), rs.ap[:, 0:n],
                        ALU.mult, ALU.mult, [hs, rs, vecs], [ys])
                dma(P, 'sp', D['yT'].rearrange('c p t -> p c t')[:, :, t0:t0 + n], ys.ap[:, :, 0:n], reads=[ys])
        return xn, xnb, mark_xn, xlo, mark_lo

    def load_w(wt, w2d, col0, ncols, kch):
        src = w2d[:, col0:col0 + ncols].rearrange('(kc p) n -> p kc n', p=128)
        dma(P, 'pool', wt.ap[:, 0:kch, 0:ncols], src, writes=[wt])

    def phase_inproj_qk(l, xn, xnb, xlo, mark_lo):
        P.barrier()
        A.top = mark_lo
        w2d = D['w_in'][l]
        whr = Ring([A.alloc([8, 512], BF16, 'whi%d' % i) for i in range(2)])
        w32 = A.alloc([8, 512], F32, 'w32')
        wlo = A.alloc([8, 512], BF16, 'wlo')
        st32 = Ring([A.alloc([512], F32, 'st32_%d' % i) for i in range(4)])
        for blk in range(2):
            whi = whr.next()
            load_w(whi, w2d, C_QG + blk * 512, 512, 8)
            dma(P, 'sp', w32.ap, w2d[:, C_QG + blk * 512:C_QG + (blk + 1) * 512].rearrange('(kc p) n -> p kc n', p=128),
                writes=[w32])
            tt(P, 'pool', wlo.ap, w32.ap, whi.ap, ALU.subtract, [w32, whi], [wlo])
            for x in range(4):
                cidx = blk * 4 + x
                for si, (t0, n) in enumerate(SUPER):
                    ps = PS.next()
                    k_ = 0
                    for wt_, xt_ in ((whi, xn), (whi, xlo), (wlo, xn)):
                        for kc in range(8):
                            mm(P, ps.ap[:, 0:n], wt_.ap[:, kc, x * 128:(x + 1) * 128], xt_.ap[:, kc, t0:t0 + n],
                               k_ == 0, k_ == 23, [wt_, xnb[si]], [ps])
                            k_ += 1
                    sf = st32.next()
                    cp(P, evac_eng(), sf.ap[:, 0:n], ps.ap[:, 0:n], [ps], [sf])
                    dma(P, 'sp', D['QK32'][cidx, :, t0:t0 + n], sf.ap[:, 0:n], reads=[sf])

    def phase_inproj(l, xn, xnb, mark):
        P.barrier()
        A.top = mark
        w2d = D['w_in'][l]
        wr = Ring([A.alloc([8, 512], BF16, 'wb%d' % i) for i in range(3)])
        stg = Ring([A.alloc([NT], BF16, 'stg%d' % i) for i in range(3)])
        stf = Ring([A.alloc([NT], F32, 'stf%d' % i) for i in range(2)])
        tst = Ring([A.alloc([512], BF16, 'tst%d' % i) for i in range(3)])
        tsf = Ring([A.alloc([512], F32, 'tsf%d' % i) for i in range(3)])

        def fm_group(col0, nch, kind, fbase, Mlast=128):
            ci = 0
            while ci < nch:
                nb = min(4, nch - ci)
                ncols = sum(128 if (ci + x) < nch - 1 else Mlast for x in range(nb))
                wt = wr.next()
                load_w(wt, w2d, col0 + ci * 128, ncols, 8)
                for x in range(nb):
                    cidx = ci + x
                    M = 128 if cidx < nch - 1 else Mlast
                    sb = stg.next() if kind in ('pf', 'ka') else None
                    sf = stf.next() if kind in ('up', 'ka') else None
                    for si, (t0, n) in enumerate(SUPER):
                        ps = PS.next()
                        for kc in range(8):
                            mm(P, ps.ap[0:M, 0:n], wt.ap[:, kc, x * 128:x * 128 + M], xn.ap[:, kc, t0:t0 + n],
                               kc == 0, kc == 7, [wt, xnb[si]], [ps])
                        if sf is not None:
                            cp(P, evac_eng(), sf.ap[0:M, t0:t0 + n], ps.ap[0:M, 0:n], [ps], [sf])
                            if sb is not None:
                                cp(P, 'pool', sb.ap[0:M, t0:t0 + n], sf.ap[0:M, t0:t0 + n], [sf], [sb])
                        elif sb is not None:
                            cp(P, evac_eng(), sb.ap[0:M, t0:t0 + n], ps.ap[0:M, 0:n], [ps], [sb])
                    if kind in ('pf', 'ka'):
                        dma(P, 'sp', D['PF'][fbase + cidx, 0:M, :], sb.ap[0:M, :], reads=[sb])
                    if kind == 'up':
                        dma(P, 'sp', D['UP'][cidx], sf.ap, reads=[sf])
                    if kind == 'ka':
                        g, hp = cidx // 2, cidx % 2
                        keep = GROUPS[g][0]
                        for s in range(NPS):
                            dma(P, 'sp', D['kT%d' % g][l, s, hp], sf.ap[:, (s + 1) * SEQ - keep:(s + 1) * SEQ],
                                reads=[sf])
                ci += nb

        def on(k):
            return sub is None or k in sub
        if on('rg'):
            fm_group(C_RG, 8, 'pf', F_RG)
        if on('alr'):
            fm_group(C_ALR, 1, 'pf', F_ALR)
        if on('up'):
            fm_group(C_UP, 8, 'up', 0)
        if on('qa'):
            fm_group(C_QA, 6, 'pf', F_QA)
        if on('ka'):
            fm_group(C_KA, 6, 'ka', F_KA)
        if on('ga'):
            fm_group(C_GA, 24, 'pf', F_GA)

        tiles = [(i * 128, 128) for i in range(NTP // 128)] + [(NTP, NSS * DSEQ)]

        def tm_group(col0, ncols_tot, kind):
            c = 0
            while c < ncols_tot:
                ncols = min(512, ncols_tot - c)
                wt = wr.next()
                load_w(wt, w2d, col0 + c, ncols, 8)
                for ti, (t0, n) in enumerate(tiles):
                    if kind == 'ks' and ti < len(tiles) - 1:
                        continue
                    si = min(t0 // 512, len(SUPER) - 1)
                    ps = PS.next()
                    for kc in range(8):
                        mm(P, ps.ap[0:n, 0:ncols], xn.ap[:, kc, t0:t0 + n], wt.ap[:, kc, 0:ncols],
                           kc == 0, kc == 7, [wt, xnb[si]], [ps])
                    if kind in ('vg', 'va'):
                        tb = tst.next()
                        e1 = evac_eng()
                        cp(P, e1, tb.ap[0:n, 0:ncols], ps.ap[0:n, 0:ncols], [ps], [tb])
                        dst = D['VG'] if kind == 'vg' else D['VA']
                        dma(P, 'sp', dst[t0:t0 + n, c:c + ncols], tb.ap[0:n, 0:ncols], reads=[tb])
                    if kind in ('va', 'ks'):
                        sample = ti == len(tiles) - 1
                        need = []
                        for g in range(3):
                            gc0 = g * 256
                            if not (c <= gc0 < c + ncols):
                                continue
                            keep = GROUPS[g][0]
                            if sample:
                                need.append((g, gc0 - c))
                            else:
                                lt = (t0 % SEQ) // 128
                                if lt * 128 >= SEQ - keep:
                                    need.append((g, gc0 - c))
                        if need:
                            tf = tsf.next()
                            cp(P, e1 if kind == 'va' else evac_eng(), tf.ap[0:n, 0:ncols], ps.ap[0:n, 0:ncols], [ps], [tf])
                            for g, off in need:
                                keep = GROUPS[g][0]
                                if sample:
                                    nm = ('v%ds' if kind == 'va' else 'k%ds') % g
                                    for b in range(NSS):
                                        dma(P, 'sp', D[nm][l, b, keep - DSEQ:keep, :],
                                            tf.ap[b * DSEQ:(b + 1) * DSEQ, off:off + 256], reads=[tf])
                                else:
                                    s = t0 // SEQ
                                    r0 = (t0 % SEQ) - (SEQ - keep)
                                    dma(P, 'sp', D['v%dp' % g][l, s, r0:r0 + 128, :], tf.ap[0:128, off:off + 256],
                                        reads=[tf])
                c += ncols

        if on('vg'):
            tm_group(C_VG, 1024, 'vg')
        if on('va'):
            tm_group(C_VA, 768, 'va')
        if on('ks'):
            tm_group(C_KA, 768, 'ks')

    def gla_shared(l, PT, C):
        wa2 = A.alloc([512], BF16, 'wa2')
        ms(P, 'pool', wa2.ap, 0.0, [wa2])
        dma(P, 'pool', wa2.ap[0:16, :], D['gla_wa2'][l], writes=[wa2])
        maskU4 = A.alloc([4, 128], F32, 'maskU4')
        for h in range(4):
            dma(P, 'sp', maskU4.ap[:, h, :], D['maskU'], writes=[maskU4])
        rm = A.alloc([4 * PT], F32, 'resetm')
        ms(P, 'pool', rm.ap, 1.0, [rm])
        ms(P, 'pool', rm.ap.rearrange('p (a c) -> p a c', c=C)[:, :, 0:1], 0.0, [rm])
        Gt = A.alloc([8, C], F32, 'Gt')
        ms(P, 'pool', Gt.ap, 1.0, [Gt])
        for hv in range(8):
            ts(P, 'pool', Gt.ap[:, hv, :], Gt.ap[:, hv, :], vcol(V_GN + l * 2 + hv % 2), None, ALU.mult, ALU.bypass,
               [Gt, vecs], [Gt])
        return wa2, maskU4, rm, Gt

    def gen_gla(l, tok0, T, C, s0_src, s_out, shared, PTMAX=256, cache=None):
        def AL(free, dt, name, parts=128):
            if cache is None:
                return A.alloc(free, dt, name, parts)
            k_ = (name, tuple(free), str(dt))
            if k_ not in cache:
                cache[k_] = A.alloc(free, dt, name, parts)
            return cache[k_]
        PT = min(T, PTMAX)
        npiece = T // PT
        nch = PT // C
        DKS = 128.0 ** -0.5
        wa2, maskU4, rm, Gt = shared
        S = AL([4, 256], F32, 'S')
        Sb = AL([4, 256], BF16, 'Sb')
        if s0_src is None:
            ms(P, 'pool', S.ap, 0.0, [S])
        else:
            dma(P, 'sp', S.ap, s0_src.rearrange('h k v -> k h v'), writes=[S])
        cp(P, 'pool', Sb.ap, S.ap, [S], [Sb])
        qr = Ring([AL([4, PT], F32, 'q%d' % i) for i in range(1)])
        kr = Ring([AL([4, PT], F32, 'k%d' % i) for i in range(1)])
        rr = Ring([AL([8, PT], BF16, 'r%d' % i) for i in range(1)])
        vr = Ring([AL([nch, 1024], BF16, 'v%d' % i) for i in range(2)])
        yield
        ar = Ring([AL([PT], BF16, 'alr%d' % i) for i in range(2)])
        lbuf = AL([4, PT], F32, 'lbuf')
        cbuf = AL([4, PT], F32, 'cbuf')
        ein = AL([4, PT], F32, 'ein')
        eq = AL([4, PT], F32, 'eq')
        ek = AL([4, PT], F32, 'ek')
        est = AL([4, PT], BF16, 'est')
        if C == 128:
            qs, ks = eq, ek
        else:
            qs = AL([4, PT], BF16, 'qs')
            ks = AL([4, PT], BF16, 'ks')
        qin = AL([4, PT], BF16, 'qin')
        kstT = AL([4, PT], BF16, 'kstT')
        attr = Ring([AL([4, C], BF16, 'attm%d' % i) for i in range(2)])
        kstr = Ring([AL([4, 128], BF16, 'kst%d' % i) for i in range(2)])
        sqg = AL([8, C], BF16, 'sqg')
        lnb = AL([4, C], F32, 'lnb')
        rstd = AL([4, C], F32, 'rstd')
        sr = AL([8, C], F32, 'sr')
        srg = AL([8, C], F32, 'srg')
        t1 = AL([8, C], F32, 't1')
        ogr = Ring([AL([8, PT], BF16, 'ogst%d' % i) for i in range(2)])
        r_ref = (C - 1) // 2
        PFv = D['PF']
        for pi in range(npiece):
            p0 = tok0 + pi * PT
            qT, kT, rT, vt, alr = qr.next(), kr.next(), rr.next(), vr.next(), ar.next()
            dma(P, 'sp', qT.ap, D['QK32'][0:4].rearrange('h p t -> p h t')[:, :, p0:p0 + PT], writes=[qT])
            dma(P, 'sp', kT.ap, D['QK32'][4:8].rearrange('h p t -> p h t')[:, :, p0:p0 + PT], writes=[kT])
            dma(P, 'sp', rT.ap, PFv[F_RG:F_RG + 8].rearrange('h p t -> p h t')[:, :, p0:p0 + PT], writes=[rT])
            dma(P, 'sp', alr.ap, PFv[F_ALR, :, p0:p0 + PT], writes=[alr])
            if C == 128:
                dma(P, 'sp', vt.ap, D['VG'][p0:p0 + PT, :].rearrange('(a p) c -> p a c', p=128), writes=[vt])
            else:
                dma(P, 'sp', vt.ap[0:C, 0, :], D['VG'][p0:p0 + PT, :], writes=[vt])
            for h in range(4):
                ps = PS.next()
                mm(P, ps.ap[:, 0:PT], wa2.ap[:, h * 128:(h + 1) * 128], alr.ap[:, 0:PT], True, True,
                   [wa2, alr], [ps])
                act(P, lbuf.ap[:, h, :], ps.ap[:, 0:PT], AF.Exp, [ps, nba], [lbuf], scale=-1.0,
                    bias=nba.ap[:, l * 4 + h:l * 4 + h + 1])
            act(P, lbuf.ap, lbuf.ap, AF.Ln, [lbuf], [lbuf], scale=1.0, bias=1.0)
            lf = lbuf.ap.rearrange('p h t -> p (h t)')
            cf = cbuf.ap.rearrange('p h t -> p (h t)')
            P.op('dve', lambda e, cf=cf, lf=lf: e.tensor_tensor_scan(out=cf, data0=rm.ap, data1=lf, initial=0.0,
                                                                   op0=ALU.mult, op1=ALU.add),
                 reads=[rm.b, lbuf.b], writes=[cbuf.b])
            cv = cf.rearrange('p (a c) -> p a c', c=C)
            lv = lf.rearrange('p (a c) -> p a c', c=C)
            na = 4 * nch
            tt(P, 'dve', lv, cv, cv[:, :, r_ref:r_ref + 1].to_broadcast([128, na, C]), ALU.subtract, [cbuf], [lbuf])
            act(P, eq.ap, lbuf.ap, AF.Exp, [lbuf], [eq], scale=-1.0 / 16)
            act(P, ek.ap, lbuf.ap, AF.Exp, [lbuf], [ek], scale=1.0 / 16)
            act(P, ein.ap, cbuf.ap, AF.Exp, [cbuf], [ein], scale=-1.0 / 16)
            tt(P, 'dve', lv, cv, cv[:, :, C - 1:C].to_broadcast([128, na, C]), ALU.subtract, [cbuf, eq, ek], [lbuf])
            act(P, est.ap, lbuf.ap, AF.Exp, [lbuf], [est], scale=1.0 / 16)
            stt(P, qs.ap, qT.ap, DKS, eq.ap, ALU.mult, ALU.mult, [qT, eq], [qs, eq])
            tt(P, 'pool', ks.ap, kT.ap, ek.ap, ALU.mult, [kT, ek], [ks, ek])
            stt(P, qin.ap, qT.ap, DKS, ein.ap, ALU.mult, ALU.mult, [qT, ein], [qin])
            tt(P, 'pool', kstT.ap, kT.ap, est.ap, ALU.mult, [kT, est], [kstT])
            ogst = ogr.next()
            yield
            for ci in range(nch):
                cs = ci * C
                psa = PS.next()
                for h in range(4):
                    mm(P, psa.ap[0:C, h * 128:h * 128 + C], ks.ap[:, h, cs:cs + C], qs.ap[:, h, cs:cs + C], True, True,
                       [ks, qs], [psa])
                attm = attr.next()
                tt(P, 'dve', attm.ap[0:C, :, :], psa.ap[0:C, :].rearrange('p (h t) -> p h t', t=128)[:, :, 0:C],
                   maskU4.ap[0:C, :, 0:C], ALU.mult, [psa, maskU4], [attm])
                for h in range(4):
                    tr(P, psb.ap[0:C, h * 128:(h + 1) * 128], kstT.ap[:, h, cs:cs + C], ident16.ap, [kstT, ident16], [psb])
                kst = kstr.next()
                cp(P, 'act', kst.ap[0:C, :, :], psb.ap[0:C, 0:512].rearrange('p (h k) -> p h k', k=128), [psb], [kst])
                if C == 128:
                    yield
                pso = [PS.next(), PS.next()]
                for h in range(4):
                    for vc in range(2):
                        off = ((h % 2) * 2 + vc) * 128
                        o_ap = pso[h // 2].ap[:, off:off + C]
                        mm(P, o_ap, vt.ap[0:C, ci, h * 256 + vc * 128:h * 256 + (vc + 1) * 128], attm.ap[0:C, h, :],
                           True, False, [vt, attm], [pso[h // 2]])
                        mm(P, o_ap, Sb.ap[:, h, vc * 128:(vc + 1) * 128], qin.ap[:, h, cs:cs + C],
                           False, True, [Sb, qin], [pso[h // 2]])
                if C == 128:
                    yield
                for b in range(2):
                    act(P, sqg.ap[:, b * 4:(b + 1) * 4, :],
                        pso[b].ap.rearrange('p (x t) -> p x t', t=128)[:, :, 0:C], AF.Square, [pso[b]], [sqg])
                pss = PS.next()
                for h in range(4):
                    for vc in range(2):
                        mm(P, pss.ap[:, h * 128:h * 128 + C], ones16.ap, sqg.ap[:, h * 2 + vc, :], vc == 0, vc == 1,
                           [ones16, sqg], [pss])
                if C == 128:
                    yield
                act(P, lnb.ap, pss.ap.rearrange('p (h t) -> p h t', t=128)[:, :, 0:C], AF.Ln, [pss], [lnb],
                    scale=1.0 / 256, bias=1e-6)
                act(P, rstd.ap, lnb.ap, AF.Exp, [lnb], [rstd], scale=-0.5)
                act(P, sr.ap, rT.ap[:, :, cs:cs + C], AF.Silu, [rT], [sr])
                tt(P, 'pool', srg.ap, sr.ap, Gt.ap, ALU.mult, [sr, Gt], [srg])
                for h in range(4):
                    off = (h % 2) * 256
                    tt(P, 'dve', t1.ap[:, 2 * h:2 * h + 2, :],
                       pso[h // 2].ap[:, off:off + 256].rearrange('p (x t) -> p x t', t=128)[:, :, 0:C],
                       rstd.ap[:, h:h + 1, :].to_broadcast([128, 2, C]), ALU.mult, [pso[h // 2], rstd], [t1])
                tt(P, 'pool', ogst.ap[:, :, cs:cs + C], t1.ap, srg.ap, ALU.mult, [t1, srg], [ogst])
                if C == 128:
                    yield
                psk = [PS.next(), PS.next()]
                for h in range(4):
                    mm(P, psk[h // 2].ap[:, (h % 2) * 256:(h % 2 + 1) * 256], kst.ap[0:C, h, :],
                       vt.ap[0:C, ci, h * 256:(h + 1) * 256], True, True, [kst, vt], [psk[h // 2]])
                for h in range(4):
                    stt(P, S.ap[:, h, :], S.ap[:, h, :], ein.ap[:, h, cs + C - 1:cs + C],
                        psk[h // 2].ap[:, (h % 2) * 256:(h % 2 + 1) * 256], ALU.mult, ALU.add,
                        [S, ein, psk[h // 2]], [S])
                cp(P, 'pool', Sb.ap, S.ap, [S], [Sb])
                if C == 128:
                    yield
            dma(P, 'sp', D['OG'].rearrange('c p t -> p c t')[:, :, p0:p0 + PT], ogst.ap, reads=[ogst])
        dma(P, 'sp', s_out.rearrange('h k v -> k h v'), S.ap, reads=[S])

    def pool_shared(l):
        pw = A.alloc([4, 2, 256], BF16, 'pw')
        dma(P, 'pool', pw.ap, D['pool_w'][l].rearrange('g (kc p) d -> p g kc d', p=128), writes=[pw])
        invc = A.alloc([8, 16], F32, 'invc')
        dma(P, 'sp', invc.ap, D['invc'], writes=[invc])
        return pw, invc

    def gen_pool(l, seqidx, tok0, T, sample_b, shared):
        PT = min(T, 512)
        pw, invc = shared
        pooled = A.alloc([8, T], BF16, 'pooled')
        pmr = Ring([A.alloc([PT], BF16, 'pmst%d' % i) for i in range(3)])
        ur = Ring([A.alloc([15 + T], F32, 'u%d' % i) for i in range(2)])
        sr_ = Ring([A.alloc([15 + T], F32, 'ws%d' % i) for i in range(2)])
        tmpf = Ring([A.alloc([16], F32, 'tmpf%d' % i) for i in range(2)])
        for cc in range(8):
            w = (2, 4, 8, 16)[cc // 2]
            u = ur.next()
            if sample_b is None:
                ms(P, 'pool', u.ap[:, 0:15], 0.0, [u])
            else:
                dma(P, 'sp', u.ap[:, 0:15], D['spoolT'][l, cc, :, sample_b, :], writes=[u])
            dma(P, 'sp', u.ap[:, 15:15 + T], D['UP'][cc, :, tok0:tok0 + T], writes=[u])
            cur = u
            k = 1
            lo = 0
            while k < w:
                nx = sr_.next()
                lo2 = lo + k
                tt(P, 'pool' if cc in (1, 7) else 'dve', nx.ap[:, lo2:15 + T], cur.ap[:, lo2:15 + T],
                   cur.ap[:, lo2 - k:15 + T - k], ALU.add, [cur], [nx])
                cur = nx
                lo = lo2
                k *= 2
            stt(P, pooled.ap[:, cc, :], cur.ap[:, 15:15 + T], 1.0 / w, u.ap[:, 15:15 + T], ALU.mult, ALU.subtract,
                [cur, u], [pooled])
            if sample_b is None:
                tf = tmpf.next()
                tt(P, 'dve', tf.ap, cur.ap[:, 15:31], invc.ap[:, cc, :], ALU.mult, [cur, invc], [tf])
                tt(P, 'dve', pooled.ap[:, cc, 0:16], tf.ap, u.ap[:, 15:31], ALU.subtract, [tf, u], [pooled])
            dma(P, 'sp', D['poolT'][l, cc, :, seqidx, :], u.ap[:, T:T + 15], reads=[u])
            yield
        for dc in range(8):
            g, half = dc // 2, dc % 2
            for t in range(0, T, PT):
                ps = PS.next()
                for kc in range(2):
                    mm(P, ps.ap[:, 0:PT], pw.ap[:, g, kc, half * 128:(half + 1) * 128],
                       pooled.ap[:, g * 2 + kc, t:t + PT], kc == 0, kc == 1, [pw, pooled], [ps])
                pm = pmr.next()
                act(P, pm.ap[:, 0:PT], ps.ap[:, 0:PT], AF.Copy, [ps, vecs], [pm],
                    scale=vcol(V_PS + l * 8 + dc))
                dma(P, 'sp', D['PM'][dc, :, tok0 + t:tok0 + t + PT], pm.ap[:, 0:PT], reads=[pm])
            yield

    def gen_attn_prompt(l, tok0):
        T = SEQ
        bm = A.alloc([12, 256], F32, 'bm')
        dma(P, 'sp', bm.ap, D['bm'], writes=[bm])
        qN = A.alloc([2, T], BF16, 'qN')
        kN = A.alloc([2, T], BF16, 'kN')
        qPr = Ring([A.alloc([2, T], BF16, 'qP%d' % i) for i in range(2)])
        kPr = Ring([A.alloc([2, T], BF16, 'kP%d' % i) for i in range(2)])
        vP = A.alloc([16, 256], BF16, 'vP')
        vZ = [A.alloc([16, 2, 128], BF16, 'vZ%d' % j) for j in range(2)]
        Oacc = A.alloc([6, T], BF16, 'Oacc')
        Dsum = A.alloc([2, T], F32, 'Dsum')
        rden = Dsum
        tmpr = Ring([A.alloc([2, 256], F32, 'sc%d' % i) for i in range(3)])
        ptr_ = Ring([A.alloc([2, 256], BF16, 'pT%d' % i) for i in range(5)])
        for j in range(2):
            ms(P, 'pool', vZ[j].ap, 0.0, [vZ[j]])
        pair_rr = [0]
        psb32 = bm.__class__(psb.ap.bitcast(F32), 'psb32')
        psb32.b = psb.b
        PSO = Ring(psf[4:6])
        PSD = Ring([psf[6], psb32])
        for g, (win, dil) in enumerate(GROUPS):
            nb = T // dil // 128
            dma(P, 'sp', qN.ap, D['PF'][F_QA + 2 * g:F_QA + 2 * g + 2].rearrange('h p t -> p h t')[:, :, tok0:tok0 + T],
                writes=[qN])
            dma(P, 'sp', kN.ap, D['PF'][F_KA + 2 * g:F_KA + 2 * g + 2].rearrange('h p t -> p h t')[:, :, tok0:tok0 + T],
                writes=[kN])
            vsrc = D['VA'][tok0:tok0 + T, g * 256:(g + 1) * 256].rearrange('(b p r) c -> p r b c', p=128, r=dil)
            for r in range(dil):
                dma(P, 'sp', vP.ap[:, r * nb:(r + 1) * nb, :], vsrc[:, r, :, :], writes=[vP])
            qU, kU = qPr.next(), kPr.next()
            if dil == 1:
                cp(P, 'pool', qU.ap, qN.ap, [qN], [qU])
                cp(P, 'pool', kU.ap, kN.ap, [kN], [kU])
            else:
                cp(P, 'pool', qU.ap.rearrange('p h (r i) -> p h r i', r=dil),
                   qN.ap.rearrange('p h (i r) -> p h r i', r=dil), [qN], [qU])
                cp(P, 'pool', kU.ap.rearrange('p h (r i) -> p h r i', r=dil),
                   kN.ap.rearrange('p h (i r) -> p h r i', r=dil), [kN], [kU])
            vPv = vP.ap.rearrange('p t (hp x) -> p t hp x', x=128)
            cp(P, 'dve', vZ[0].ap[:, :, :, 0:64], vPv[:, :, :, 0:64], [vP], [vZ[0]])
            cp(P, 'dve', vZ[1].ap[:, :, :, 64:128], vPv[:, :, :, 64:128], [vP], [vZ[1]])
            for hp in range(2):
                Ov = Oacc.ap[:, g * 2 + hp, :].rearrange('p (b q r) -> p b q r', q=128, r=dil)
                Dv = Dsum.ap[:, hp, :].rearrange('p (b q r) -> p b q r', q=128, r=dil)
                def scores(r, b):
                    kt = r * nb + b
                    nq = 256 if b < nb - 1 else 128
                    h0 = g * 4 + hp * 2
                    pk = pair_rr[0]
                    pair_rr[0] = (pk + 1) % 2
                    pbanks = [psf[2 * pk], psf[2 * pk + 1]]
                    for j in range(2):
                        mm(P, pbanks[j].ap[:, 0:nq], kU.ap[j * 64:(j + 1) * 64, hp, kt * 128:(kt + 1) * 128],
                           qU.ap[j * 64:(j + 1) * 64, hp, kt * 128:kt * 128 + nq], True, True, [kU, qU], [pbanks[j]])
                    sc = tmpr.next()
                    stt(P, sc.ap[:, :, 0:nq], pq[pk][:, :].rearrange('p (j q) -> p j q', q=512)[:, :, 0:nq], 0.125,
                        bm.ap[:, h0:h0 + 2, 0:nq], ALU.mult, ALU.add, pbanks + [bm], [sc])
                    pT = ptr_.next()
                    act(P, pT.ap[:, :, 0:nq], sc.ap[:, :, 0:nq], AF.Exp, [sc], [pT])
                    return pT

                units = [(r, b) for r in range(dil) for b in range(nb)]
                nxt = scores(*units[0])
                prev = None
                for ui, (r, b) in enumerate(units):
                    kt = r * nb + b
                    cur = nxt
                    nxt = scores(*units[ui + 1]) if ui + 1 < len(units) else None
                    if b == 0:
                        prev = None
                    pso, psd = PSO.next(), PSD.next()
                    srcs = []
                    for j in range(2):
                        if prev is not None:
                            srcs.append((j, kt - 1, prev, 128))
                        srcs.append((j, kt, cur, 0))
                    for which in range(2):
                        pt_ = pso if which == 0 else psd
                        for n_, (j, ktile, pt, coff) in enumerate(srcs):
                            lhs = vZ[j].ap[:, ktile, hp, :] if which == 0 else onesZ[j].ap
                            mm(P, pt_.ap[:, 0:128], lhs, pt.ap[:, j, coff:coff + 128], n_ == 0, n_ == len(srcs) - 1,
                               [vZ[j], onesZ[j], pt], [pt_])
                    cp(P, 'act', Ov[:, b, :, r], pso.ap[:, 0:128], [pso], [Oacc])
                    if g == 0:
                        cp(P, 'dve', Dv[:, b, :, r], psd.ap[:, 0:128], [psd], [Dsum])
                    else:
                        tt(P, 'dve', Dv[:, b, :, r], Dv[:, b, :, r], psd.ap[:, 0:128], ALU.add, [psd, Dsum], [Dsum])
                    prev = cur
                    yield
        act(P, Dsum.ap, Dsum.ap, AF.Ln, [Dsum], [Dsum])
        act(P, Dsum.ap, Dsum.ap, AF.Exp, [Dsum], [Dsum], scale=-1.0)
        for g in range(3):
            tt(P, 'dve', Oacc.ap[:, 2 * g:2 * g + 2, :], Oacc.ap[:, 2 * g:2 * g + 2, :], Dsum.ap,
               ALU.mult, [Oacc, Dsum], [Oacc])
        dma(P, 'sp', D['OC'].rearrange('c p t -> p c t')[:, :, tok0:tok0 + T], Oacc.ap, reads=[Oacc])

    def gen_attn_sample(l):
        tokS0 = NTP
        MAXT = 17
        bms = []
        for g, (win, dil) in enumerate(GROUPS):
            t_ = A.alloc([4, (win // 128 + 1) * 8], F32, 'bms%d' % g)
            dma(P, 'sp', t_.ap, D['bms%d' % g].rearrange('p h t q -> p h (t q)'), writes=[t_])
            bms.append(t_)
        kf = Ring([A.alloc([16, 256], BF16, 'kf%d' % i) for i in range(1)])
        vf = Ring([A.alloc([16, 256], BF16, 'vf%d' % i) for i in range(1)])
        vnew = A.alloc([256], BF16, 'vnew')
        kTs = A.alloc([2, MAXT * 128], BF16, 'kTs')
        qTs = A.alloc([2, 8], BF16, 'qTs')
        vZ = [A.alloc([MAXT, 2, 128], BF16, 'vZs%d' % j) for j in range(2)]
        Os = A.alloc([6, 8], F32, 'Os')
        Ds = A.alloc([2, 8], F32, 'Ds')
        rd = A.alloc([2, 8], F32, 'rd')
        ocs = A.alloc([6, 8], BF16, 'ocs')
        scr_ = Ring([A.alloc([MAXT * 8], F32, 'scs%d' % i) for i in range(2)])
        pts = Ring([A.alloc([MAXT * 8], BF16, 'pTs%d' % i) for i in range(4)])
        for j in range(2):
            ms(P, 'pool', vZ[j].ap, 0.0, [vZ[j]])
        ms(P, 'pool', vnew.ap, 0.0, [vnew])
        for b in range(NSS):
            tk = tokS0 + b * DSEQ
            for g, (win, dil) in enumerate(GROUPS):
                ntc = win // 128
                nt_ = ntc + 1
                dma(P, 'sp', D['k%ds' % g][l, b, 0:win - DSEQ, :], D['ck%d' % g][l, b, DSEQ:win, :])
                dma(P, 'sp', D['v%ds' % g][l, b, 0:win - DSEQ, :], D['cv%d' % g][l, b, DSEQ:win, :])
                kc_, vc_ = kf.next(), vf.next()
                dma(P, 'pool', kc_.ap[:, 0:ntc, :], D['ck%d' % g][l, b].rearrange('(t p) c -> p t c', p=128), writes=[kc_])
                dma(P, 'pool', vc_.ap[:, 0:ntc, :], D['cv%d' % g][l, b].rearrange('(t p) c -> p t c', p=128), writes=[vc_])
                ms(P, 'pool', kTs.ap[:, :, win + DSEQ:(ntc + 1) * 128], 0.0, [kTs])
                dma(P, 'sp', kTs.ap[:, :, win:win + DSEQ],
                    D['PF'][F_KA + 2 * g:F_KA + 2 * g + 2].rearrange('h p t -> p h t')[:, :, tk:tk + DSEQ], writes=[kTs])
                dma(P, 'sp', qTs.ap, D['PF'][F_QA + 2 * g:F_QA + 2 * g + 2].rearrange('h p t -> p h t')[:, :, tk:tk + DSEQ],
                    writes=[qTs])
                dma(P, 'sp', vnew.ap[0:DSEQ, :], D['VA'][tk:tk + DSEQ, g * 256:(g + 1) * 256], writes=[vnew])
                for hp in range(2):
                    for t0_ in range(0, ntc, 8):
                        nn = min(8, ntc - t0_)
                        for x in range(nn):
                            tr(P, psb.ap[:, x * 128:(x + 1) * 128], kc_.ap[:, t0_ + x, hp * 128:(hp + 1) * 128],
                               ident16.ap, [kc_, ident16], [psb])
                        cp(P, 'act', kTs.ap[:, hp, t0_ * 128:(t0_ + nn) * 128], psb.ap[:, 0:nn * 128], [psb], [kTs])
                vcv = vc_.ap.rearrange('p t (hp x) -> p t hp x', x=128)
                cp(P, 'pool', vZ[0].ap[:, 0:ntc, :, 0:64], vcv[:, 0:ntc, :, 0:64], [vc_], [vZ[0]])
                cp(P, 'pool', vZ[1].ap[:, 0:ntc, :, 64:128], vcv[:, 0:ntc, :, 64:128], [vc_], [vZ[1]])
                vnv = vnew.ap.rearrange('p (hp x) -> p hp x', x=128)
                cp(P, 'pool', vZ[0].ap[:, ntc, :, 0:64], vnv[:, :, 0:64], [vnew], [vZ[0]])
                cp(P, 'pool', vZ[1].ap[:, ntc, :, 64:128], vnv[:, :, 64:128], [vnew], [vZ[1]])
                for hp in range(2):
                    cur = []
                    for j in range(2):
                        hh = hp * 2 + j
                        ps = PS.next()
                        for t in range(nt_):
                            mm(P, ps.ap[:, t * 8:(t + 1) * 8], kTs.ap[j * 64:(j + 1) * 64, hp, t * 128:(t + 1) * 128],
                               qTs.ap[j * 64:(j + 1) * 64, hp, :], True, True, [kTs, qTs], [ps])
                        sc = scr_.next()
                        stt(P, sc.ap[:, 0:nt_ * 8], ps.ap[:, 0:nt_ * 8], 0.125, bms[g].ap[:, hh, :], ALU.mult, ALU.add,
                            [ps, bms[g]], [sc])
                        pT = pts.next()
                        act(P, pT.ap[:, 0:nt_ * 8], sc.ap[:, 0:nt_ * 8], AF.Exp, [sc], [pT])
                        cur.append(pT)
                    pso = PS.next()
                    for which in range(2):
                        oap = pso.ap[:, which * 8:(which + 1) * 8]
                        cnt = 0
                        for j in range(2):
                            for t in range(nt_):
                                lhs = vZ[j].ap[:, t, hp, :] if which == 0 else onesZ[j].ap
                                mm(P, oap, lhs, cur[j].ap[:, t * 8:(t + 1) * 8], cnt == 0, cnt == 2 * nt_ - 1,
                                   [vZ[j], onesZ[j], cur[j]], [pso])
                                cnt += 1
                    cp(P, 'act', Os.ap[:, g * 2 + hp, :], pso.ap[:, 0:8], [pso], [Os])
                    if g == 0:
                        cp(P, 'dve', Ds.ap[:, hp, :], pso.ap[:, 8:16], [pso, Os], [Ds])
                    else:
                        tt(P, 'dve', Ds.ap[:, hp, :], Ds.ap[:, hp, :], pso.ap[:, 8:16], ALU.add, [pso, Ds, Os], [Ds])
                    yield
            P.op('dve', lambda e: e.reciprocal(out=rd.ap, in_=Ds.ap), reads=[Ds.b], writes=[rd.b])
            for g in range(3):
                tt(P, 'dve', ocs.ap[:, 2 * g:2 * g + 2, :], Os.ap[:, 2 * g:2 * g + 2, :], rd.ap, ALU.mult, [Os, rd], [ocs])
            dma(P, 'sp', D['OC'].rearrange('c p t -> p c t')[:, :, tk:tk + DSEQ], ocs.ap, reads=[ocs])

    def load_outproj_weights(l, lazy=False):
        wo = {}
        todo = []
        for nm, kch in (('w_oa', 8), ('w_ob', 8), ('w_oc', 6), ('w_out', 8)):
            wo[nm] = A.alloc([kch, 1024], BF16, nm)
            for kc in range(kch):
                todo.append((nm, kc))

        def issue(k=1):
            for _ in range(k):
                if todo:
                    nm, kc = todo.pop(0)
                    dma(P, 'pool', wo[nm].ap[:, kc, :], D[nm][l, kc * 128:(kc + 1) * 128, :], writes=[wo[nm]])
        if not lazy:
            issue(len(todo))
        return wo, A.top, issue, todo

    def phase_outproj(l, src_h, wo, mark_wo):
        P.barrier()
        A.top = mark_wo
        obr = Ring([A.alloc([8, 512], BF16, 'ob%d' % i) for i in range(2)])
        gbr = Ring([A.alloc([8, 512], BF16, 'gb%d' % i) for i in range(2)])
        sgr = Ring([A.alloc([512], F32, 'sg%d' % i) for i in range(2)])
        tmr = Ring([A.alloc([512], F32, 'tm%d' % i) for i in range(2)])
        merged = A.alloc([8, 512], F32, 'merged')
        m16 = A.alloc([8, 512], BF16, 'm16')
        hr = Ring([A.alloc([8, 512], F32, 'h%d' % i) for i in range(2)])
        hn = Ring([A.alloc([8, 512], F32, 'hn%d' % i) for i in range(2)])
        srcv = src_h.rearrange('c p t -> p c t')
        branches = (('w_oa', 'OG', 8, F_GA), ('w_ob', 'PM', 8, F_GB), ('w_oc', 'OC', 6, F_GC))
        for si, (t0, n) in enumerate(SUPER):
            ht = hr.next()
            dma(P, 'sp', ht.ap[:, :, 0:n], srcv[:, :, t0:t0 + n], writes=[ht])
            for bi, (wn, sn, kch, fg) in enumerate(branches):
                ob, gb = obr.next(), gbr.next()
                dma(P, 'sp', ob.ap[:, 0:kch, 0:n], D[sn].rearrange('c p t -> p c t')[:, :, t0:t0 + n], writes=[ob])
                dma(P, 'sp', gb.ap[:, :, 0:n], D['PF'][fg:fg + 8].rearrange('c p t -> p c t')[:, :, t0:t0 + n], writes=[gb])
                for dc in range(8):
                    ps = PS.next()
                    for kc in range(kch):
                        mm(P, ps.ap[:, 0:n], wo[wn].ap[:, kc, dc * 128:(dc + 1) * 128], ob.ap[:, kc, 0:n],
                           kc == 0, kc == kch - 1, [wo[wn], ob], [ps])
                    sg = sgr.next()
                    act(P, sg.ap[:, 0:n], gb.ap[:, dc, 0:n], AF.Sigmoid, [gb], [sg])
                    if bi == 0:
                        tt(P, 'dve', merged.ap[:, dc, 0:n], ps.ap[:, 0:n], sg.ap[:, 0:n], ALU.mult, [ps, sg], [merged])
                    else:
                        tm = tmr.next()
                        tt(P, 'dve', tm.ap[:, 0:n], ps.ap[:, 0:n], sg.ap[:, 0:n], ALU.mult, [ps, sg], [tm])
                        if bi == 1:
                            tt(P, 'pool', merged.ap[:, dc, 0:n], merged.ap[:, dc, 0:n], tm.ap[:, 0:n], ALU.add,
                               [merged, tm], [merged])
                        else:
                            tt(P, 'pool', m16.ap[:, dc, 0:n], merged.ap[:, dc, 0:n], tm.ap[:, 0:n], ALU.add,
                               [merged, tm], [m16])
            hnew = hn.next()
            for dc in range(8):
                ps = PS.next()
                for kc in range(8):
                    mm(P, ps.ap[:, 0:n], wo['w_out'].ap[:, kc, dc * 128:(dc + 1) * 128], m16.ap[:, kc, 0:n],
                       kc == 0, kc == 7, [wo['w_out'], m16], [ps])
                tt(P, 'dve', hnew.ap[:, dc, 0:n], ps.ap[:, 0:n], ht.ap[:, dc, 0:n], ALU.add, [ps, ht], [hnew])
            dma(P, 'sp', D['hT'].rearrange('c p t -> p c t')[:, :, t0:t0 + n], hnew.ap[:, :, 0:n], reads=[hnew])

    def phase_ffn_up(l, xn, xnb, mark):
        P.barrier()
        A.top = mark
        w2d = D['ffn_w_up'][l]
        ABW = NPS * (SEQ + 2) + NSS * (DSEQ + 2)
        SOFF = NPS * (SEQ + 2)
        wr = Ring([A.alloc([8, 2, 128], BF16, 'wu%d' % i) for i in range(4)])
        abr = Ring([A.alloc([ABW], F32, 'ab%d' % i) for i in range(2)])
        bbr = Ring([A.alloc([NT], BF16, 'bb%d' % i) for i in range(2)])
        ybr = Ring([A.alloc([NT], F32, 'ybuf%d' % i) for i in range(2)])
        hsr = Ring([A.alloc([NT], BF16, 'hst%d' % i) for i in range(2)])
        halos = {id(t_): Buf('halo') for t_ in abr.tiles}
        for t_ in abr.tiles:
            ms(P, 'pool', t_.ap, 0.0, [t_, halos[id(t_)]])

        def conv_stages(j, ab, bb, ybuf):
            halo = halos[id(ab)]
            sv = ab.ap[:, SOFF:ABW].rearrange('p (b x) -> p b x', x=DSEQ + 2)
            pv = ab.ap[:, 0:SOFF].rearrange('p (s x) -> p s x', x=SEQ + 2)
            w0, w1, w2 = (vcol(V_CW + l * 66 + k * NFF + j) for k in range(3))
            cb = vcol(V_CB + l * NFF + j)
            segs = [(lambda sh: pv[:, :, sh:sh + SEQ], ybuf.ap[:, 0:NTP].rearrange('p (s t) -> p s t', t=SEQ)),
                    (lambda sh: sv[:, :, sh:sh + DSEQ], ybuf.ap[:, NTP:NT].rearrange('p (b i) -> p b i', i=DSEQ))]

            def st_a():
                for av, yv in segs:
                    act(P, yv, av(0), AF.Identity, [ab, halo, vecs], [ybuf], scale=w0, bias=cb)

            def st_b():
                for av, yv in segs:
                    stt(P, yv, av(1), w1, yv, ALU.mult, ALU.add, [ab, halo, ybuf, vecs], [ybuf])

            def st_c():
                for av, yv in segs:
                    stt(P, yv, av(2), w2, yv, ALU.mult, ALU.add, [ab, ybuf, vecs], [ybuf])
                for s_ in range(NPS):
                    base = s_ * (SEQ + 2)
                    dma(P, 'sp', D['convT'][l, j, :, s_, :], ab.ap[:, base + SEQ:base + SEQ + 2], reads=[ab])
                dma(P, 'sp', D['convT'][l, j, :, NPS:NPS + NSS, :], sv[:, :, DSEQ:DSEQ + 2], reads=[ab])

            def st_d():
                act(P, ybuf.ap, ybuf.ap, AF.Silu, [ybuf], [ybuf])

            def st_e():
                hs = hsr.next()
                tt(P, 'pool', hs.ap, ybuf.ap, bb.ap, ALU.mult, [ybuf, bb], [hs])
                dma(P, 'sp', D['HF'][j], hs.ap, reads=[hs])

            return {0: st_a, 2: st_b, 4: st_c, 6: st_d, 8: st_e}

        pending = {}
        wts = {}

        def issue_w(jj):
            if jj < NFF:
                wts[jj] = wr.next()
                for half in range(2):
                    src = w2d[:, half * DFF + jj * 128:half * DFF + (jj + 1) * 128].rearrange('(kc p) n -> p kc n', p=128)
                    dma(P, 'pool', wts[jj].ap[:, :, half, :], src, writes=[wts[jj]])

        issue_w(0)
        issue_w(1)
        for j in range(NFF + 1):
            if j < NFF:
                issue_w(j + 2)
                wt = wts[j]
                ab, bb = abr.next(), bbr.next()
                ybuf = ybr.next()
                sv = ab.ap[:, SOFF:ABW].rearrange('p (b x) -> p b x', x=DSEQ + 2)
                halo = halos[id(ab)]
                dma(P, 'sp', sv[:, :, 0:2], D['sconvT'][l, j], writes=[halo])
            for si, (t0, n) in enumerate(SUPER):
                if j < NFF:
                    psa, psb_ = PS.next(), PS.next()
                    for kc in range(8):
                        mm(P, psa.ap[:, 0:n], wt.ap[:, kc, 0, :], xn.ap[:, kc, t0:t0 + n], kc == 0, kc == 7,
                           [wt, xnb[si]], [psa])
                    for kc in range(8):
                        mm(P, psb_.ap[:, 0:n], wt.ap[:, kc, 1, :], xn.ap[:, kc, t0:t0 + n], kc == 0, kc == 7,
                           [wt, xnb[si]], [psb_])
                    if si < 8:
                        s_ = si // 4
                        c0 = s_ * (SEQ + 2) + 2 + (si % 4) * 512
                        cp(P, 'act', ab.ap[:, c0:c0 + n], psa.ap[:, 0:n], [psa], [ab])
                    else:
                        cp(P, 'act', sv[:, :, 2:2 + DSEQ], psa.ap[:, 0:n].rearrange('p (b i) -> p b i', i=DSEQ), [psa], [ab])
                    cp(P, 'dve', bb.ap[:, t0:t0 + n], psb_.ap[:, 0:n], [psb_], [bb])
                if si in pending:
                    pending[si]()
            pending = conv_stages(j, ab, bb, ybuf) if j < NFF else {}

    def phase_ffn_down(l):
        phase_reset(MARK0)
        wd = A.alloc([NFF, 1024], BF16, 'wd')
        wdb = [Buf('wd%d' % kc) for kc in range(NFF)]
        for kc in range(NFF):
            dma(P, 'pool', wd.ap[:, kc, :], D['ffn_w_down'][l, kc * 128:(kc + 1) * 128, :], writes=[wdb[kc]])
        hfr = Ring([A.alloc([NFF, 512], BF16, 'hf%d' % i) for i in range(2)])
        hr = Ring([A.alloc([8, 512], F32, 'h%d' % i) for i in range(2)])
        hn = Ring([A.alloc([8, 512], F32, 'hn%d' % i) for i in range(2)])
        hv = D['hT'].rearrange('c p t -> p c t')
        for si, (t0, n) in enumerate(SUPER):
            hf, ht, hnew = hfr.next(), hr.next(), hn.next()
            dma(P, 'sp', hf.ap[:, :, 0:n], D['HF'].rearrange('c p t -> p c t')[:, :, t0:t0 + n], writes=[hf])
            dma(P, 'sp', ht.ap[:, :, 0:n], hv[:, :, t0:t0 + n], writes=[ht])
            for dc in range(8):
                ps = PS.next()
                for kc in range(NFF):
                    mm(P, ps.ap[:, 0:n], wd.ap[:, kc, dc * 128:(dc + 1) * 128], hf.ap[:, kc, 0:n],
                       kc == 0, kc == NFF - 1, [wdb[kc], hf], [ps])
                tt(P, 'dve', hnew.ap[:, dc, 0:n], ps.ap[:, 0:n], ht.ap[:, dc, 0:n], ALU.add, [ps, ht], [hnew])
            dma(P, 'sp', hv[:, :, t0:t0 + n], hnew.ap[:, :, 0:n], reads=[hnew])

    def run_all():
        for l in range(DEPTH):
            src = D['xT'] if l == 0 else D['hT']
            xn, xnb, mark, xlo, mark_lo = phase_norm(src, V_N1 + l * 8, 'xn', need_lo=True)
            if stop_after == ('norm', l):
                if debug_scratch:
                    dma(P, 'sp', D['XN'].rearrange('c p t -> p c t'), xn.ap, reads=xnb)
                return
            phase_inproj_qk(l, xn, xnb, xlo, mark_lo)
            phase_inproj(l, xn, xnb, mark)
            if stop_after == ('inproj', l):
                return
            def run_group(mk):
                phase_reset(MARK0)
                gens = mk()
                while gens:
                    for g_ in list(gens):
                        try:
                            next(g_)
                        except StopIteration:
                            gens.remove(g_)

            def grp1():
                sh = gla_shared(l, 256, 128)
                return [gen_gla(l, s_ * SEQ, SEQ, 128, None, D['gla_p'][l, s_], sh) for s_ in range(NPS)]

            def grp2():
                sh = gla_shared(l, DSEQ, DSEQ)
                psh = pool_shared(l)
                gc_ = {}

                def sample_chain():
                    for b in range(NSS):
                        yield from gen_gla(l, NTP + b * DSEQ, DSEQ, DSEQ, D['state_gla'][l, b], D['gla_s'][l, b], sh,
                                           cache=gc_)
                return ([sample_chain()]
                        + [gen_pool(l, s_, s_ * SEQ, SEQ, None, psh) for s_ in range(NPS)]
                        + [gen_pool(l, NPS + b, NTP + b * DSEQ, DSEQ, b, psh) for b in range(NSS)])

            run_group(grp1)
            run_group(grp2)
            run_group(lambda: [gen_attn_prompt(l, 0), gen_attn_sample(l)])
            wo_pre = {}
            phase_reset(MARK0)
            wo_pre['wo'], wo_pre['mark'], issue_wo, todo_wo = load_outproj_weights(l, lazy=True)
            g5 = gen_attn_prompt(l, SEQ)
            step = 0
            for _ in g5:
                step += 1
                if step >= 4 and step % 3 == 0:
                    issue_wo(1)
            issue_wo(len(todo_wo))
            if stop_after == ('attn', l):
                return
            phase_outproj(l, src, wo_pre['wo'], wo_pre['mark'])
            if stop_after == ('outproj', l):
                return
            xn, xnb, mark, _, _ = phase_norm(D['hT'], V_N2 + l * 8, 'xn')
            phase_ffn_up(l, xn, xnb, mark)
            phase_ffn_down(l)
            if stop_after == ('ffn', l):
                return
        phase_norm(D['hT'], V_FN, 'final')

    run_all()
    P.barrier()
    P.emit(es)
    es.close()
    return nc


_NC_CACHE = {}


def _get_nc():
    if 'nc' not in _NC_CACHE:
        _NC_CACHE['nc'] = build_program()
    return _NC_CACHE['nc']


def make_in_maps(inp, n_cores=8):
    f = lambda a: np.ascontiguousarray(np.asarray(a, dtype=np.float32))
    consts = host_constants()
    vec = np.zeros((128, NV), np.float32)

    def put(col0, arr2d):
        a = f(arr2d).reshape(-1, 128)
        vec[:, col0:col0 + a.shape[0]] = a.T

    put(V_N1, inp['norm1_g'])
    put(V_N2, inp['norm2_g'])
    put(V_FN, inp['final_norm_g'])
    put(V_BA, inp['gla_ba'])
    put(V_GN, inp['gla_norm_g'])
    put(V_PS, inp['pool_scale'])
    put(V_CW, inp['ffn_conv_w'])
    put(V_CB, inp['ffn_conv_b'])
    shared = {k: f(inp[k]) for k in ('w_in', 'gla_wa2', 'pool_w', 'w_oa', 'w_ob', 'w_oc', 'w_out', 'ffn_w_up',
                                     'ffn_w_down')}
    shared['vecs'] = vec
    shared.update(consts)
    maps = []
    xp, xs = f(inp['x_prompt']), f(inp['x_sample'])
    for c in range(n_cores):
        m = dict(shared)
        xa = np.concatenate([xp[NPS * c:NPS * (c + 1)].reshape(NTP, DM), xs[NSS * c:NSS * (c + 1)].reshape(NSS * DSEQ, DM)], 0)
        m['xT'] = np.ascontiguousarray(xa.T).reshape(8, 128, NT)
        sl = slice(NSS * c, NSS * (c + 1))
        m['state_gla'] = f(inp['state_gla'][:, sl])
        m['spoolT'] = np.ascontiguousarray(f(inp['state_pool'][:, sl]).reshape(DEPTH, NSS, 15, 8, 128).transpose(0, 3, 4, 1, 2))
        m['sconvT'] = np.ascontiguousarray(f(inp['state_ffn_conv'][:, sl]).reshape(DEPTH, NSS, 2, NFF, 128).transpose(0, 3, 4, 1, 2))
        for gi, (win, dil) in enumerate(GROUPS):
            m['ck%d' % gi] = f(inp['cache_k_w%d' % win][:, sl]).reshape(DEPTH, NSS, win, 256)
            m['cv%d' % gi] = f(inp['cache_v_w%d' % win][:, sl]).reshape(DEPTH, NSS, win, 256)
        maps.append(m)
    return maps


def assemble(results):
    cat = lambda xs, ax: np.ascontiguousarray(np.concatenate(xs, axis=ax))
    yp, ys, poolp, pools, convp, convs = [], [], [], [], [], []
    for r in results:
        y = r['yT'].reshape(DM, NT).T
        yp.append(y[:NTP].reshape(NPS, SEQ, DM))
        ys.append(y[NTP:].reshape(NSS, DSEQ, DM))
        pl = r['poolT'].transpose(0, 3, 4, 1, 2).reshape(DEPTH, NPS + NSS, 15, DM)
        poolp.append(pl[:, :NPS])
        pools.append(pl[:, NPS:])
        cv = r['convT'].transpose(0, 3, 4, 1, 2).reshape(DEPTH, NPS + NSS, 2, DFF)
        convp.append(cv[:, :NPS])
        convs.append(cv[:, NPS:])
    out = [cat(yp, 0), cat(ys, 0),
           cat([r['gla_p'] for r in results], 1), cat([r['gla_s'] for r in results], 1),
           cat(poolp, 1), cat(pools, 1)]
    for gi, (win, dil) in enumerate(GROUPS):
        kp = cat([r['kT%d' % gi].transpose(0, 1, 4, 2, 3).reshape(DEPTH, NPS, win, 4, 64) for r in results], 1)
        ks = cat([r['k%ds' % gi].reshape(DEPTH, NSS, win, 4, 64) for r in results], 1)
        vp = cat([r['v%dp' % gi].reshape(DEPTH, NPS, win, 4, 64) for r in results], 1)
        vs = cat([r['v%ds' % gi].reshape(DEPTH, NSS, win, 4, 64) for r in results], 1)
        out += [kp, ks, vp, vs]
    out += [cat(convp, 1), cat(convs, 1)]
    return tuple(np.asarray(o, dtype=np.float32) for o in out)


def kernel(**inputs):
    nc = _get_nc()
    in_maps = make_in_maps(inputs)
    res = run_bass_kernel_spmd(nc, in_maps, core_ids=list(range(8)))
    return assemble(res.results)
```
